# Optimizing a Trainium2 kernel written in Bass

```python
import math
import jax, jax.numpy as jnp
from jax import lax
import numpy as np

D_MODEL = 2048
BATCH = 4
SEQ = 2048
DEPTH = 4
DEC_BATCH = 1
DEC_SEQ = 8192
PAST_LEN = 128

N_EVEN = (DEPTH + 1) // 2
N_ODD = DEPTH // 2
FNET_WIDTH = D_MODEL // 2
FNET_GROUPS = 4
FNET_GROUP_DIM = FNET_WIDTH // FNET_GROUPS
DIFF_HEADS = 8
DIFF_HEAD_DIM = D_MODEL // (4 * DIFF_HEADS)
DIFF_V_DIM = 2 * DIFF_HEAD_DIM
DIFF_QK = DIFF_HEADS * 2 * DIFF_HEAD_DIM
DIFF_WIDTH = DIFF_HEADS * DIFF_V_DIM
EVEN_IN = FNET_WIDTH + 2 * DIFF_QK + DIFF_WIDTH
EVEN_OUT = FNET_WIDTH + DIFF_WIDTH
Q_BLOCK = 128
N_BUCKETS = 32
MAX_DISTANCE = 128
GLA_HEADS = 4
GLA_DK = D_MODEL // 2 // GLA_HEADS
GLA_DV = D_MODEL // GLA_HEADS
GLA_RANK = 16
GLA_TAU = 16.0
GLA_CHUNK = 64
GLA_HK = GLA_HEADS * GLA_DK
GLA_HV = GLA_HEADS * GLA_DV
ODD_IN = 2 * GLA_HK + 2 * GLA_HV
X_HEADS = 4
X_HEAD_DIM = D_MODEL // X_HEADS
N_MEM = 256
D_FF = 5632
EPS = 1e-6

kernel_name = "hybrid_fnet_diffattn_gla_encoder"


def rms_norm(x, g):
    xf = x.astype(jnp.float32)
    y = xf * lax.rsqrt(jnp.mean(xf * xf, axis=-1, keepdims=True) + EPS)
    return (y * g.astype(jnp.float32)).astype(x.dtype)


def t5_bucket(rel):
    n = N_BUCKETS // 2
    max_exact = n // 2
    base = jnp.where(rel > 0, n, 0)
    a = jnp.abs(rel)
    af = jnp.maximum(a, 1).astype(jnp.float32)
    large = max_exact + (jnp.log(af / max_exact) / math.log(MAX_DISTANCE / max_exact)
                         * (n - max_exact)).astype(jnp.int32)
    large = jnp.minimum(large, n - 1)
    return base + jnp.where(a < max_exact, a, large)


def fnet_mix(u):
    B, S, _ = u.shape
    ug = u.reshape(B, S, FNET_GROUPS, FNET_GROUP_DIM).astype(jnp.float32)
    f = jnp.fft.fft2(ug, axes=(1, 3), norm="ortho").real
    return f.reshape(B, S, FNET_WIDTH).astype(u.dtype)


def diff_attention(q, k, v, lam, lam_init, head_g, rel_bias):
    B, S = q.shape[:2]
    nb = S // Q_BLOCK
    qb = (q * (DIFF_HEAD_DIM ** -0.5)).reshape(B, nb, Q_BLOCK, DIFF_HEADS, 2, DIFF_HEAD_DIM)
    qb = qb.transpose(1, 0, 2, 3, 4, 5)
    kpos = jnp.arange(S, dtype=jnp.int32)

    def block(args):
        qi, i = args
        s = jnp.einsum('bqhcd,bkhcd->bhcqk', qi, k).astype(jnp.float32)
        qpos = i * Q_BLOCK + jnp.arange(Q_BLOCK, dtype=jnp.int32)
        bias = rel_bias[t5_bucket(kpos[None, :] - qpos[:, None])]
        s = s + bias.transpose(2, 0, 1)[None, :, None].astype(jnp.float32)
        p = jax.nn.softmax(s, axis=-1)
        a = p[:, :, 0] - lam * p[:, :, 1]
        return jnp.einsum('bhqk,bkhe->bqhe', a.astype(v.dtype), v)

    o = lax.map(block, (qb, jnp.arange(nb, dtype=jnp.int32)))
    o = o.transpose(1, 0, 2, 3, 4).reshape(B, S, DIFF_HEADS, DIFF_V_DIM)
    o = rms_norm(o, head_g) * (1.0 - lam_init)
    return o.reshape(B, S, DIFF_WIDTH)


def even_mixer(h, w_in, w_out, lam_p, head_g, rel_bias, lam_init):
    B, S, _ = h.shape
    z = h @ w_in
    u, q, k, v = jnp.split(z, [FNET_WIDTH, FNET_WIDTH + DIFF_QK, FNET_WIDTH + 2 * DIFF_QK], axis=-1)
    fa = fnet_mix(u)
    q = q.reshape(B, S, DIFF_HEADS, 2, DIFF_HEAD_DIM)
    k = k.reshape(B, S, DIFF_HEADS, 2, DIFF_HEAD_DIM)
    v = v.reshape(B, S, DIFF_HEADS, DIFF_V_DIM)
    lp = lam_p.astype(jnp.float32)
    lam = jnp.exp(jnp.sum(lp[0] * lp[1])) - jnp.exp(jnp.sum(lp[2] * lp[3])) + lam_init
    da = diff_attention(q, k, v, lam, lam_init, head_g, rel_bias)
    return jnp.concatenate([fa, da], axis=-1) @ w_out


def gla_chunked(q, k, v, g):
    B, S, H, DK = q.shape
    DV = v.shape[-1]
    nc = S // GLA_CHUNK

    def to_chunks(t):
        return t.reshape(B, nc, GLA_CHUNK, H, t.shape[-1]).transpose(1, 0, 3, 2, 4)

    lower = jnp.tril(jnp.ones((GLA_CHUNK, GLA_CHUNK), dtype=bool))

    def step(state, inp):
        qc, kc, vc, gc = inp
        b = jnp.cumsum(gc, axis=2)
        inter = jnp.einsum('bhtd,bhde->bhte', qc * jnp.exp(b), state)
        diff = b[:, :, :, None, :] - b[:, :, None, :, :]
        decay = jnp.exp(jnp.where(lower[:, :, None], diff, -jnp.inf))
        attn = jnp.einsum('bhtd,bhsd,bhtsd->bhts', qc, kc, decay)
        intra = jnp.einsum('bhts,bhse->bhte', attn, vc)
        b_last = b[:, :, -1:, :]
        state = (jnp.exp(b_last[:, :, 0, :])[..., None] * state
                 + jnp.einsum('bhsd,bhse->bhde', kc * jnp.exp(b_last - b), vc))
        return state, inter + intra

    s0 = jnp.zeros((B, H, DK, DV), jnp.float32)
    _, o = lax.scan(step, s0, (to_chunks(q), to_chunks(k), to_chunks(v), to_chunks(g)))
    return o.transpose(1, 0, 3, 2, 4).reshape(B, S, H, DV)


def odd_mixer(h, w_in, w_gd, w_gu, b_g, head_g, w_out):
    B, S, _ = h.shape
    z = h @ w_in
    q, k, v, r = jnp.split(z, [GLA_HK, 2 * GLA_HK, 2 * GLA_HK + GLA_HV], axis=-1)
    q = q.reshape(B, S, GLA_HEADS, GLA_DK).astype(jnp.float32) * (GLA_DK ** -0.5)
    k = k.reshape(B, S, GLA_HEADS, GLA_DK).astype(jnp.float32)
    v = v.reshape(B, S, GLA_HEADS, GLA_DV).astype(jnp.float32)

    def gate(d):
        pre = ((h @ w_gd[d]) @ w_gu[d] + b_g[d]).astype(jnp.float32)
        return (jax.nn.log_sigmoid(pre) / GLA_TAU).reshape(B, S, GLA_HEADS, GLA_DK)

    g_f, g_b = gate(0), gate(1)
    o_f = gla_chunked(q, k, v, g_f)
    flip = lambda t: jnp.flip(t, axis=1)
    o_b = flip(gla_chunked(flip(q), flip(k), flip(v), flip(g_b)))
    o = rms_norm(o_f + o_b, head_g).astype(h.dtype)
    o = o * jax.nn.silu(r).reshape(B, S, GLA_HEADS, GLA_DV)
    return o.reshape(B, S, GLA_HV) @ w_out


def cross_attention(h, mem_n, w_q, w_kv, w_o):
    B, S, _ = h.shape
    M = mem_n.shape[1]
    q = (h @ w_q).reshape(B, S, X_HEADS, X_HEAD_DIM)
    kv = (mem_n @ w_kv).reshape(B, M, 2, X_HEADS, X_HEAD_DIM)
    s = jnp.einsum('bqhd,bkhd->bhqk', q, kv[:, :, 0]).astype(jnp.float32) * (X_HEAD_DIM ** -0.5)
    p = jax.nn.softmax(s, axis=-1).astype(h.dtype)
    o = jnp.einsum('bhqk,bkhd->bqhd', p, kv[:, :, 1])
    return o.reshape(B, S, D_MODEL) @ w_o


def conv_ffn(h, w_up, conv_w, conv_b, w_down):
    u = h @ w_up
    up = jnp.pad(u, ((0, 0), (1, 1), (0, 0)))
    u = conv_w[0] * up[:, :-2] + conv_w[1] * up[:, 1:-1] + conv_w[2] * up[:, 2:] + conv_b
    a, g = jnp.split(u, 2, axis=-1)
    return (jax.nn.gelu(g, approximate=True) * a) @ w_down


def trunk(x, mem, norm_g, rel_bias, w_in_even, w_out_even, diff_lambda, diff_norm_g,
          w_in_odd, gla_gate_down, gla_gate_up, gla_gate_bias, gla_norm_g, w_out_odd,
          w_xq, w_xkv, w_xo, w_up, conv_w, conv_b, w_down):
    for l in range(DEPTH):
        ng = norm_g[l]
        i = l // 2
        h = rms_norm(x, ng[0])
        if l % 2 == 0:
            lam_init = 0.8 - 0.6 * math.exp(-0.3 * l)
            mix = even_mixer(h, w_in_even[i], w_out_even[i], diff_lambda[i], diff_norm_g[i],
                             rel_bias, lam_init)
        else:
            mix = odd_mixer(h, w_in_odd[i], gla_gate_down[i], gla_gate_up[i], gla_gate_bias[i],
                            gla_norm_g[i], w_out_odd[i])
        x = x + rms_norm(mix, ng[1])
        mem_n = rms_norm(mem, ng[4])
        x = x + rms_norm(cross_attention(rms_norm(x, ng[2]), mem_n, w_xq[l], w_xkv[l], w_xo[l]), ng[3])
        x = x + rms_norm(conv_ffn(rms_norm(x, ng[5]), w_up[l], conv_w[l], conv_b[l], w_down[l]), ng[6])
    return x


def setup_inputs(seed: int = 0) -> dict:
    key = jax.random.key(seed)
    ks = jax.random.split(key, 24)

    def nrm(k, shape, scale):
        return jax.random.normal(k, shape, jnp.float32) * scale

    return {
        "x_prompt": nrm(ks[0], (BATCH, SEQ, D_MODEL), 1.0),
        "x_sample": nrm(ks[1], (DEC_BATCH, DEC_SEQ, D_MODEL), 1.0),
        "mem_prompt": nrm(ks[2], (BATCH, N_MEM, D_MODEL), 1.0),
        "mem_sample": nrm(ks[3], (DEC_BATCH, N_MEM, D_MODEL), 1.0),
        "norm_g": 1.0 + nrm(ks[4], (DEPTH, 7, D_MODEL), 0.02),
        "rel_bias": nrm(ks[5], (N_BUCKETS, DIFF_HEADS), 0.5),
        "w_in_even": nrm(ks[6], (N_EVEN, D_MODEL, EVEN_IN), D_MODEL ** -0.5),
        "w_out_even": nrm(ks[7], (N_EVEN, EVEN_OUT, D_MODEL), EVEN_OUT ** -0.5),
        "diff_lambda": nrm(ks[8], (N_EVEN, 4, DIFF_HEAD_DIM), 0.1),
        "diff_norm_g": 1.0 + nrm(ks[9], (N_EVEN, DIFF_V_DIM), 0.02),
        "w_in_odd": nrm(ks[10], (N_ODD, D_MODEL, ODD_IN), D_MODEL ** -0.5),
        "gla_gate_down": nrm(ks[11], (N_ODD, 2, D_MODEL, GLA_RANK), D_MODEL ** -0.5),
        "gla_gate_up": nrm(ks[12], (N_ODD, 2, GLA_RANK, GLA_HK), GLA_RANK ** -0.5),
        "gla_gate_bias": nrm(ks[13], (N_ODD, 2, GLA_HK), 0.1),
        "gla_norm_g": 1.0 + nrm(ks[14], (N_ODD, GLA_DV), 0.02),
        "w_out_odd": nrm(ks[15], (N_ODD, GLA_HV, D_MODEL), GLA_HV ** -0.5),
        "w_xq": nrm(ks[16], (DEPTH, D_MODEL, D_MODEL), D_MODEL ** -0.5),
        "w_xkv": nrm(ks[17], (DEPTH, D_MODEL, 2 * D_MODEL), D_MODEL ** -0.5),
        "w_xo": nrm(ks[18], (DEPTH, D_MODEL, D_MODEL), D_MODEL ** -0.5),
        "w_up": nrm(ks[19], (DEPTH, D_MODEL, 2 * D_FF), D_MODEL ** -0.5),
        "conv_w": nrm(ks[20], (DEPTH, 3, 2 * D_FF), 3.0 ** -0.5),
        "conv_b": nrm(ks[21], (DEPTH, 2 * D_FF), 0.02),
        "w_down": nrm(ks[22], (DEPTH, D_FF, D_MODEL), D_FF ** -0.5),
    }


def reference(x_prompt, x_sample, mem_prompt, mem_sample, norm_g, rel_bias, w_in_even, w_out_even,
              diff_lambda, diff_norm_g, w_in_odd, gla_gate_down, gla_gate_up, gla_gate_bias,
              gla_norm_g, w_out_odd, w_xq, w_xkv, w_xo, w_up, conv_w, conv_b, w_down):
    y_prompt = trunk(x_prompt, mem_prompt, norm_g, rel_bias, w_in_even, w_out_even, diff_lambda,
                     diff_norm_g, w_in_odd, gla_gate_down, gla_gate_up, gla_gate_bias, gla_norm_g,
                     w_out_odd, w_xq, w_xkv, w_xo, w_up, conv_w, conv_b, w_down)
    y_sample = trunk(x_sample, mem_sample, norm_g, rel_bias, w_in_even, w_out_even, diff_lambda,
                     diff_norm_g, w_in_odd, gla_gate_down, gla_gate_up, gla_gate_bias, gla_norm_g,
                     w_out_odd, w_xq, w_xkv, w_xo, w_up, conv_w, conv_b, w_down)
    return (y_prompt, y_sample)
```

```python
import contextlib
import math
import numpy as np
import ml_dtypes
import concourse.bass as bass
import concourse.mybir as mybir
from concourse.bass_utils import run_bass_kernel_spmd

F32 = mybir.dt.float32
BF16 = mybir.dt.bfloat16
AF = mybir.ActivationFunctionType
ALU = mybir.AluOpType
NPBF = ml_dtypes.bfloat16

D = 2048
NMEM = 256
DFF = 5632
EPS = 1e-6


class Buf:
    __slots__ = ("t", "w", "r", "dkey", "name")

    def __init__(self, t, name):
        self.t = t
        self.w = None
        self.r = {}
        self.dkey = None
        self.name = name

    def __getitem__(self, idx):
        return self.t[idx]


class StopBuild(Exception):
    pass


class K:
    stop_at = None
    nbar = 0
    stopped = False

    def __init__(self, nc, stack, n_dsem=40):
        self.nc = nc
        self.h = dict(pe=nc.tensor, act=nc.scalar, dve=nc.vector, pool=nc.gpsimd, sp=nc.sync)
        self.sem = {}
        self.latest = {}
        self.seen = {e: {} for e in self.h}
        self.cnt = {e: 0 for e in self.h}
        self.bufs = []
        self.uid = 0
        for e in ("pe", "act", "dve", "pool"):
            self.sem[e] = stack.enter_context(nc.semaphore("s_" + e))
            self.latest[e] = 0
        self.dkeys = []
        for i in range(n_dsem):
            k = "d%d" % i
            self.sem[k] = stack.enter_context(nc.semaphore("s_" + k))
            self.latest[k] = 0
            self.dkeys.append(k)
        self.dset = set(self.dkeys)
        self.dnext = 0
        self.pe_pending = False

    def sb(self, stack, name, shape, dt):
        self.uid += 1
        t = stack.enter_context(self.nc.sbuf_tensor("%s_%d" % (name, self.uid), list(shape), dt))
        b = Buf(t, name)
        self.bufs.append(b)
        return b

    def wrap(self, t, name):
        b = Buf(t, name)
        self.bufs.append(b)
        return b

    def _dkey(self, b):
        if b.dkey is None:
            assert self.dnext < len(self.dkeys), "out of dma semaphores"
            b.dkey = self.dkeys[self.dnext]
            self.dnext += 1
        return b.dkey

    def _add(self, need, tok, eng, raw):
        k, v = tok
        if k == eng and not raw:
            return
        if k in self.dset:
            v = self.latest[k]
        if need.get(k, 0) < v:
            need[k] = v

    def _emit_waits(self, eng, reads, writes):
        need = {}
        for b in reads:
            if b.w is not None:
                self._add(need, b.w, eng, True)
        for b in writes:
            if b.w is not None:
                self._add(need, b.w, eng, False)
            for k, v in b.r.items():
                self._add(need, (k, v), eng, False)
        h = self.h[eng]
        seen = self.seen[eng]
        for k, v in need.items():
            if seen.get(k, 0) < v:
                h.wait_ge(self.sem[k], v)
                seen[k] = v

    def _record(self, tok, reads, writes):
        k, v = tok
        for b in reads:
            if b.r.get(k, 0) < v:
                b.r[k] = v
        for b in writes:
            b.w = tok
            b.r = {}

    def op(self, eng, fn, reads=(), writes=(), inc=True):
        if self.stopped:
            return
        self._emit_waits(eng, reads, writes)
        ins = fn(self.h[eng])
        if inc:
            self.cnt[eng] += 1
            ins.then_inc(self.sem[eng], 1)
            self.latest[eng] = self.cnt[eng]
            tok = (eng, self.cnt[eng])
            if eng == "pe":
                self.pe_pending = False
        else:
            assert eng == "pe"
            tok = (eng, self.cnt[eng] + 1)
            self.pe_pending = True
        self._record(tok, reads, writes)

    def dma(self, q, out_ap, in_ap, sb, reads=(), writes=()):
        if self.stopped:
            return
        key = self._dkey(sb)
        self._emit_waits(q, reads, writes)
        ins = self.h[q].dma_start(out=out_ap, in_=in_ap)
        self.latest[key] += 16
        ins.then_inc(self.sem[key], 16)
        self._record((key, self.latest[key]), reads, writes)

    def barrier(self, final=False):
        if self.stopped and not final:
            return
        assert not self.pe_pending
        self.nbar += 1
        for e, h in self.h.items():
            seen = self.seen[e]
            for k, v in self.latest.items():
                if v > 0 and seen.get(k, 0) < v:
                    h.wait_ge(self.sem[k], v)
                    seen[k] = v
        for b in self.bufs:
            b.w = None
            b.r = {}
            b.dkey = None
        self.bufs = [b for b in self.bufs if b.name.startswith("ps") or b.name.startswith("c_")]
        self.dnext = 0
        if (not final) and self.stop_at is not None and self.nbar >= self.stop_at:
            self.stopped = True


def build(S, DEPTH, taps=()):
    nc = bass.Bass("TRN2", target_bir_lowering=False)
    NT = S // 512
    TB = min(2048, S)
    NKB = S // 128
    NEVEN = (DEPTH + 1) // 2
    NODD = DEPTH // 2

    def din(name, shape, dt=F32):
        return nc.dram_tensor(name, list(shape), dt, kind="ExternalInput").ap()

    def dscr(name, shape, dt):
        kind = "ExternalOutput" if name in taps else "Internal"
        return nc.dram_tensor(name, list(shape), dt, kind=kind).ap()

    xT = din("xT", [D, S])
    memT = din("memT", [D, NMEM])
    maskb = din("maskb", [128, S])
    kmask = din("kmask", [128, NKB])
    cosS = din("cosS", [S, S], BF16)
    sinS = din("sinS", [S, S], BF16)
    cosC = din("cosC", [256, 256], BF16)
    nsinC = din("nsinC", [256, 256], BF16)
    bkt = din("bkt", [128, 6 * 512])
    trimats = din("trimats", [64, 4 * 64])
    ng = din("ng", [128, DEPTH * 7 * 16])
    relb = din("relb", [1, 256])
    lamp = din("lamp", [1, max(NEVEN, 1) * 256])
    dng = din("dng", [128, max(NEVEN, 1)])
    gng = din("gng", [128, max(NODD, 1) * 4])
    convw = din("convw", [128, DEPTH * 3 * 88])
    convb = din("convb", [128, DEPTH * 88])
    w_in_even = din("w_in_even", [max(NEVEN, 1), D, 4096])
    w_out_even = din("w_out_even", [max(NEVEN, 1), D, D])
    w_in_odd = din("w_in_odd", [max(NODD, 1), D, 6144])
    w_gd = din("w_gd", [max(NODD, 1), D, 32])
    w_gu = din("w_gu", [max(NODD, 1) * 2 * 32, 1024])
    w_out_odd = din("w_out_odd", [max(NODD, 1), D, D])
    w_xq = din("w_xq", [DEPTH, D, D])
    w_xkv = din("w_xkv", [DEPTH, D, 2 * D])
    w_xo = din("w_xo", [DEPTH, D, D])
    w_up = din("w_up", [DEPTH, D, 2 * DFF])
    w_down = din("w_down", [DEPTH, DFF, D])

    yT = nc.dram_tensor("yT", [D, S], F32, kind="ExternalOutput").ap()

    b_in_even = dscr("b_in_even", [max(NEVEN, 1), D, 4096], BF16)
    b_out_even = dscr("b_out_even", [max(NEVEN, 1), D, D], BF16)
    b_in_odd = dscr("b_in_odd", [max(NODD, 1), D, 6144], BF16)
    b_gd = dscr("b_gd", [max(NODD, 1), D, 32], BF16)
    b_gu = dscr("b_gu", [max(NODD, 1) * 2 * 32, 1024], BF16)
    b_out_odd = dscr("b_out_odd", [max(NODD, 1), D, D], BF16)
    b_xq = dscr("b_xq", [DEPTH, D, D], BF16)
    b_xkv = dscr("b_xkv", [DEPTH, D, 2 * D], BF16)
    b_xo = dscr("b_xo", [DEPTH, D, D], BF16)
    b_up = dscr("b_up", [DEPTH, D, 2 * DFF], BF16)
    b_down = dscr("b_down", [DEPTH, DFF, D], BF16)

    H = dscr("H", [D, S], BF16)
    Y = dscr("Y", [D, S], F32)
    QT = dscr("QT", [1024, S], BF16)
    KT = dscr("KT", [1024, S], BF16)
    Vt = dscr("Vt", [S, 1024], BF16)
    Ut = dscr("Ut", [S, 1024], BF16)
    CAT = dscr("CAT", [D, S], BF16)
    BT = dscr("BT", [8, 128, 6 * 512], F32)
    QgT = dscr("QgT", [1024, S], BF16)
    KgT = dscr("KgT", [1024, S], BF16)
    Kg = dscr("Kg", [S, 1024], BF16)
    Vg = dscr("Vg", [S, 2048], BF16)
    GH = dscr("GH", [2, S, 1024], BF16)
    GL = dscr("GL", [2, S, 1024], BF16)
    HDT = dscr("HDT", [32, S], BF16)
    RS = dscr("RS", [D, S], BF16)
    OG = dscr("OG", [2, D, S], F32)
    KxT = dscr("KxT", [D, NMEM], BF16)
    Vx = dscr("Vx", [NMEM, D], BF16)
    MN = dscr("MN", [D, NMEM], BF16)
    QxT = dscr("QxT", [D, S], BF16)
    OxT = dscr("OxT", [D, S], BF16)
    GG = dscr("GG", [DFF, S], BF16)

    with contextlib.ExitStack() as top:
        k = K(nc, top)
        PS = []
        for i in range(8):
            t = top.enter_context(nc.psum_tensor("psb%d" % i, [128, 512], F32))
            PS.append(k.wrap(t, "ps%d" % i))
        psrr = [0]

        def ps_next():
            p = PS[psrr[0] % 8]
            psrr[0] += 1
            return p

        prr = [0]

        def ps_pick(lo, hi):
            p = PS[lo + prr[0] % (hi - lo)]
            prr[0] += 1
            return p

        c_ones = k.sb(top, "c_ones", [128, 128], BF16)
        c_ng = k.sb(top, "c_ng", [128, DEPTH * 7 * 16], F32)
        c_tri = k.sb(top, "c_tri", [64, 256], F32)
        c_trib = k.sb(top, "c_trib", [64, 256], BF16)
        c_relb = k.sb(top, "c_relb", [128, 256], F32)
        c_kmask = k.sb(top, "c_kmask", [128, NKB], F32)
        c_far = k.sb(top, "c_far", [128, 16 * NKB], F32)
        c_lam = k.sb(top, "c_lam", [128, max(NEVEN, 1)], F32)
        c_dng = k.sb(top, "c_dng", [128, max(NEVEN, 1)], F32)
        c_gng = k.sb(top, "c_gng", [128, max(NODD, 1) * 4], F32)
        c_cw = k.sb(top, "c_cw", [128, DEPTH * 3 * 88], F32)
        c_cb = k.sb(top, "c_cb", [128, DEPTH * 88], F32)
        c_tmp = k.sb(top, "c_tmp", [128, 256], F32)
        c_tmp2 = k.sb(top, "c_tmp2", [128, 4], F32)

        k.op("dve", lambda e: e.memset(c_ones[:], 1.0), writes=[c_ones])
        k.dma("sp", c_ng[:], ng, c_ng, writes=[c_ng])
        k.dma("sp", c_tri[:], trimats, c_tri, writes=[c_tri])
        k.dma("sp", c_relb[:], relb.partition_broadcast(128), c_relb, writes=[c_relb])
        k.dma("sp", c_kmask[:], kmask, c_kmask, writes=[c_kmask])
        k.dma("sp", c_dng[:], dng, c_dng, writes=[c_dng])
        k.dma("sp", c_gng[:], gng, c_gng, writes=[c_gng])
        k.dma("sp", c_cw[:], convw, c_cw, writes=[c_cw])
        k.dma("sp", c_cb[:], convb, c_cb, writes=[c_cb])
        k.op("dve", lambda e: e.tensor_copy(out=c_trib[:], in_=c_tri[:]), reads=[c_tri], writes=[c_trib])
        for hh in range(8):
            for side in range(2):
                col = (15 + 16 * side) * 8 + hh
                o0 = (hh * 2 + side) * NKB
                k.op("dve", lambda e, o0=o0, col=col: e.tensor_scalar(
                    out=c_far[:, o0:o0 + NKB], in0=c_kmask[:], scalar1=c_relb[:, col:col + 1], scalar2=None,
                    op0=ALU.add), reads=[c_kmask, c_relb], writes=[c_far])
        for i in range(NEVEN):
            lam_init = 0.8 - 0.6 * math.exp(-0.3 * (2 * i))
            k.dma("sp", c_tmp[:], lamp[:, i * 256:(i + 1) * 256].partition_broadcast(128), c_tmp, writes=[c_tmp])
            k.op("dve", lambda e: e.tensor_tensor(out=c_tmp[:, 0:64], in0=c_tmp[:, 0:64], in1=c_tmp[:, 64:128],
                                                  op=ALU.mult), reads=[c_tmp], writes=[c_tmp])
            k.op("dve", lambda e: e.tensor_tensor(out=c_tmp[:, 128:192], in0=c_tmp[:, 128:192], in1=c_tmp[:, 192:256],
                                                  op=ALU.mult), reads=[c_tmp], writes=[c_tmp])
            k.op("dve", lambda e: e.reduce_sum(out=c_tmp2[:, 0:1], in_=c_tmp[:, 0:64], axis=mybir.AxisListType.X),
                 reads=[c_tmp], writes=[c_tmp2])
            k.op("dve", lambda e: e.reduce_sum(out=c_tmp2[:, 1:2], in_=c_tmp[:, 128:192], axis=mybir.AxisListType.X),
                 reads=[c_tmp], writes=[c_tmp2])
            k.op("act", lambda e: e.activation(out=c_tmp2[:, 2:4], in_=c_tmp2[:, 0:2], func=AF.Exp),
                 reads=[c_tmp2], writes=[c_tmp2])
            k.op("dve", lambda e, i=i, lam_init=lam_init: e.scalar_tensor_tensor(
                out=c_lam[:, i:i + 1], in0=c_tmp2[:, 3:4], scalar=-lam_init, in1=c_tmp2[:, 2:3],
                op0=ALU.add, op1=ALU.subtract), reads=[c_tmp2], writes=[c_lam])

        def ngcol(l, j, c):
            o = (l * 7 + j) * 16 + c
            return c_ng[:, o:o + 1]

        def cast_weight(src, dst, Kr, Nc, rr):
            with contextlib.ExitStack() as st:
                CW = min(2048, Nc)
                fin = [k.sb(st, "cwf", [128, CW], F32) for _ in range(3)]
                fo = [k.sb(st, "cwb", [128, CW], BF16) for _ in range(3)]
                i = 0
                for r0 in range(0, Kr, 128):
                    rows = min(128, Kr - r0)
                    for c0 in range(0, Nc, CW):
                        cw = min(CW, Nc - c0)
                        a, b = fin[i % 3], fo[i % 3]
                        k.dma("sp", a[0:rows, 0:cw], src[r0:r0 + rows, c0:c0 + cw], a, writes=[a])
                        eng = ("pool", "dve", "act")[rr[0] % 3]
                        rr[0] += 1
                        if eng == "act":
                            k.op("act", lambda e, a=a, b=b, rows=rows, cw=cw: e.copy(out=b[0:rows, 0:cw], in_=a[0:rows, 0:cw]),
                                 reads=[a], writes=[b])
                        else:
                            k.op(eng, lambda e, a=a, b=b, rows=rows, cw=cw: e.tensor_copy(out=b[0:rows, 0:cw], in_=a[0:rows, 0:cw]),
                                 reads=[a], writes=[b])
                        k.dma("pool", dst[r0:r0 + rows, c0:c0 + cw], b[0:rows, 0:cw], b, reads=[b])
                        i += 1
            k.barrier()

        rr = [0]
        for i in range(NEVEN):
            cast_weight(w_in_even[i], b_in_even[i], D, 4096, rr)
            cast_weight(w_out_even[i], b_out_even[i], D, D, rr)
        for i in range(NODD):
            cast_weight(w_in_odd[i], b_in_odd[i], D, 6144, rr)
            cast_weight(w_gd[i], b_gd[i], D, 32, rr)
            cast_weight(w_out_odd[i], b_out_odd[i], D, D, rr)
        if NODD:
            cast_weight(w_gu, b_gu, NODD * 64, 1024, rr)
        for l in range(DEPTH):
            cast_weight(w_xq[l], b_xq[l], D, D, rr)
            cast_weight(w_xkv[l], b_xkv[l], D, 2 * D, rr)
            cast_weight(w_xo[l], b_xo[l], D, D, rr)
            cast_weight(w_up[l], b_up[l], D, 2 * DFF, rr)
            cast_weight(w_down[l], b_down[l], DFF, D, rr)

        if NEVEN:
            with contextlib.ExitStack() as st:
                bk = k.sb(st, "bk", [128, 3072], F32)
                acc = k.sb(st, "bacc", [128, 3072], F32)
                tmpb = [k.sb(st, "btmp", [128, 3072], F32) for _ in range(2)]
                k.dma("sp", bk[:], bkt, bk, writes=[bk])
                for hh in range(8):
                    for b in range(32):
                        col = b * 8 + hh
                        t = tmpb[b % 2]
                        if b == 0:
                            k.op("dve", lambda e, col=col, b=b: e.tensor_scalar(
                                out=acc[:], in0=bk[:], scalar1=float(b), scalar2=c_relb[:, col:col + 1],
                                op0=ALU.is_equal, op1=ALU.mult), reads=[bk, c_relb], writes=[acc])
                        else:
                            k.op("dve", lambda e, col=col, b=b, t=t: e.tensor_scalar(
                                out=t[:], in0=bk[:], scalar1=float(b), scalar2=c_relb[:, col:col + 1],
                                op0=ALU.is_equal, op1=ALU.mult), reads=[bk, c_relb], writes=[t])
                            k.op("pool", lambda e, t=t: e.tensor_tensor(out=acc[:], in0=acc[:], in1=t[:], op=ALU.add),
                                 reads=[acc, t], writes=[acc])
                    k.dma("pool", BT[hh], acc[:], acc, reads=[acc])
            k.barrier()

        def linear_fm(inT, Kdim, W, col_list, epi, tb=None, gw=512, extra_tok=None):
            KC = Kdim // 128
            TBs = tb or TB
            nper = gw // 128
            with contextlib.ExitStack() as st:
                act = k.sb(st, "lin_act", [128, KC, TBs], BF16)
                wts = [k.sb(st, "lin_w", [128, KC, gw], BF16) for _ in range(2)]
                Wv = W.rearrange("(kc p) n -> p kc n", p=128)
                inv = inT.rearrange("(kc p) s -> p kc s", p=128)
                groups = []
                i = 0
                while i < len(col_list):
                    j = i
                    while j + 1 < len(col_list) and j + 1 - i < nper and col_list[j + 1][0] == col_list[j][0] + 128:
                        j += 1
                    groups.append(col_list[i:j + 1])
                    i = j + 1
                gi = 0
                for tb0 in range(0, S, TBs):
                    hk = KC // 2
                    k.dma("sp", act[:, 0:hk, :], inv[:, 0:hk, tb0:tb0 + TBs], act, writes=[act])
                    k.dma("sp", act[:, hk:KC, :], inv[:, hk:KC, tb0:tb0 + TBs], act, writes=[act])
                    for grp in groups:
                        wt = wts[gi % 2]
                        gi += 1
                        c0 = grp[0][0]
                        gwid = 128 * len(grp)
                        k.dma("sp", wt[:, :, 0:gwid], Wv[:, :, c0:c0 + gwid], wt, writes=[wt])
                        for bi, (cc, tag) in enumerate(grp):
                            for tt in range(TBs // 512):
                                ps = ps_next()
                                for kc in range(KC):
                                    k.op("pe", lambda e, ps=ps, wt=wt, kc=kc, bi=bi, tt=tt: e.matmul(
                                        ps[:], lhsT=wt[:, kc, bi * 128:(bi + 1) * 128], rhs=act[:, kc, tt * 512:(tt + 1) * 512],
                                        start=(kc == 0), stop=(kc == KC - 1)),
                                        reads=[wt, act], writes=[ps], inc=(kc == KC - 1))
                                epi(ps, tag, tb0 + tt * 512, 512, tt)
                            epi(None, tag, tb0, TBs, -1)

        def linear_tm(inT, Kdim, W, c0, ncols, epi, gw=512):
            KC = Kdim // 128
            with contextlib.ExitStack() as st:
                act = k.sb(st, "ltm_act", [128, KC, TB], BF16)
                wts = [k.sb(st, "ltm_w", [128, KC, gw], BF16) for _ in range(2)]
                Wv = W.rearrange("(kc p) n -> p kc n", p=128)
                inv = inT.rearrange("(kc p) s -> p kc s", p=128)
                gi = 0
                for tb0 in range(0, S, TB):
                    k.dma("sp", act[:], inv[:, :, tb0:tb0 + TB], act, writes=[act])
                    for g0 in range(0, ncols, gw):
                        gwid = min(gw, ncols - g0)
                        wt = wts[gi % 2]
                        gi += 1
                        k.dma("sp", wt[:, :, 0:gwid], Wv[:, :, c0 + g0:c0 + g0 + gwid], wt, writes=[wt])
                        for tk in range(TB // 128):
                            ps = ps_next()
                            for kc in range(KC):
                                k.op("pe", lambda e, ps=ps, wt=wt, kc=kc, tk=tk, gwid=gwid: e.matmul(
                                    ps[:, 0:gwid], lhsT=act[:, kc, tk * 128:(tk + 1) * 128], rhs=wt[:, kc, 0:gwid],
                                    start=(kc == 0), stop=(kc == KC - 1)),
                                    reads=[wt, act], writes=[ps], inc=(kc == KC - 1))
                            epi(ps, g0, gwid, tb0 + tk * 128)
            k.barrier()

        def rstd_from(ps, n, dim, out_buf, tmp_buf):
            k.op("dve", lambda e: e.tensor_scalar(out=tmp_buf[:, 0:n], in0=ps[:, 0:n], scalar1=1.0 / dim, scalar2=EPS,
                                                  op0=ALU.mult, op1=ALU.add), reads=[ps], writes=[tmp_buf])
            k.op("act", lambda e: e.activation(out=tmp_buf[:, 0:n], in_=tmp_buf[:, 0:n], func=AF.Ln),
                 reads=[tmp_buf], writes=[tmp_buf])
            k.op("act", lambda e: e.activation(out=out_buf[:, 0:n], in_=tmp_buf[:, 0:n], func=AF.Exp, scale=-0.5),
                 reads=[tmp_buf], writes=[out_buf])

        def norm_pass(l_post, j_post, Ysrc, l_pre, j_pre, Xsrc=None):
            Xs = Xsrc if Xsrc is not None else yT
            with contextlib.ExitStack() as st:
                xt = [k.sb(st, "np_x", [128, 16, 512], F32) for _ in range(2)]
                yt = [k.sb(st, "np_y", [128, 16, 512], F32) for _ in range(1)] if Ysrc is not None else None
                sq = k.sb(st, "np_sq", [128, 16, 512], BF16)
                ht = [k.sb(st, "np_h", [128, 16, 512], BF16) for _ in range(1)] if j_pre is not None else None
                mk = [k.sb(st, "np_m", [128, 512], F32) for _ in range(2)]
                r1 = k.sb(st, "np_r1", [128, 512], F32)
                r2 = k.sb(st, "np_r2", [128, 512], F32)
                tmp = k.sb(st, "np_t", [128, 512], F32)
                xv = Xs.rearrange("(c p) s -> p c s", p=128)
                xo = yT.rearrange("(c p) s -> p c s", p=128)
                hv = H.rearrange("(c p) s -> p c s", p=128)
                for ti in range(NT):
                    t0 = ti * 512
                    x = xt[ti % 2]
                    k.dma("sp", x[:], xv[:, :, t0:t0 + 512], x, writes=[x])
                    if Ysrc is not None:
                        y = yt[0]
                        m = mk[ti % 2]
                        yv = Ysrc.rearrange("(c p) s -> p c s", p=128)
                        k.dma("sp", y[:], yv[:, :, t0:t0 + 512], y, writes=[y])
                        k.dma("sp", m[:], maskb[:, t0:t0 + 512], m, writes=[m])
                        k.op("act", lambda e, y=y: e.activation(out=sq[:], in_=y[:], func=AF.Square), reads=[y], writes=[sq])
                        ps = ps_next()
                        for c in range(16):
                            k.op("pe", lambda e, ps=ps, c=c: e.matmul(ps[:], lhsT=c_ones[:], rhs=sq[:, c, :], start=(c == 0), stop=(c == 15)),
                                 reads=[c_ones, sq], writes=[ps], inc=(c == 15))
                        rstd_from(ps, 512, D, r1, tmp)
                        k.op("dve", lambda e, m=m: e.tensor_tensor(out=r1[:], in0=r1[:], in1=m[:], op=ALU.mult), reads=[r1, m], writes=[r1])
                        for c in range(16):
                            k.op("dve", lambda e, y=y, c=c: e.scalar_tensor_tensor(
                                out=y[:, c, :], in0=y[:, c, :], scalar=ngcol(l_post, j_post, c), in1=r1[:],
                                op0=ALU.mult, op1=ALU.mult), reads=[y, r1, c_ng], writes=[y])
                        k.op("pool", lambda e, x=x, y=y: e.tensor_tensor(out=x[:], in0=x[:], in1=y[:], op=ALU.add), reads=[x, y], writes=[x])
                    if Ysrc is not None or Xsrc is not None:
                        k.dma("pool", xo[:, :, t0:t0 + 512], x[:], x, reads=[x])
                    if j_pre is not None:
                        h = ht[0]
                        k.op("act", lambda e, x=x: e.activation(out=sq[:], in_=x[:], func=AF.Square), reads=[x], writes=[sq])
                        ps = ps_next()
                        for c in range(16):
                            k.op("pe", lambda e, ps=ps, c=c: e.matmul(ps[:], lhsT=c_ones[:], rhs=sq[:, c, :], start=(c == 0), stop=(c == 15)),
                                 reads=[c_ones, sq], writes=[ps], inc=(c == 15))
                        rstd_from(ps, 512, D, r2, tmp)
                        for c in range(16):
                            eng = "dve" if c % 2 == 0 else "dve"
                            k.op(eng, lambda e, x=x, h=h, c=c: e.scalar_tensor_tensor(
                                out=h[:, c, :], in0=x[:, c, :], scalar=ngcol(l_pre, j_pre, c), in1=r2[:],
                                op0=ALU.mult, op1=ALU.mult), reads=[x, r2, c_ng], writes=[h])
                        k.dma("pool", hv[:, :, t0:t0 + 512], h[:], h, reads=[h])
            k.barrier()

        def make_store_epi(st, dst_fn, dt, TBs, scale=None, func=None):
            stg = [k.sb(st, "epi_stg", [128, TBs], dt) for _ in range(2)]
            state = {"i": 0, "rr": 0}

            def epi(ps, tag, tok0, ntok, tt):
                sg = stg[state["i"] % 2]
                if ps is not None:
                    o0 = tt * 512
                    use_act = (func is not None) or (state["rr"] % 2 == 0)
                    state["rr"] += 1
                    if use_act:
                        f = func if func is not None else AF.Copy
                        if scale is not None:
                            k.op("act", lambda e: e.activation(out=sg[:, o0:o0 + 512], in_=ps[:], func=f, scale=scale), reads=[ps], writes=[sg])
                        else:
                            k.op("act", lambda e: e.activation(out=sg[:, o0:o0 + 512], in_=ps[:], func=f), reads=[ps], writes=[sg])
                    else:
                        if scale is not None:
                            k.op("dve", lambda e: e.tensor_scalar(out=sg[:, o0:o0 + 512], in0=ps[:], scalar1=scale, scalar2=None, op0=ALU.mult),
                                 reads=[ps], writes=[sg])
                        else:
                            k.op("dve", lambda e: e.tensor_copy(out=sg[:, o0:o0 + 512], in_=ps[:]), reads=[ps], writes=[sg])
                else:
                    k.dma("pool", dst_fn(tag, tok0, ntok), sg[:, 0:ntok], sg, reads=[sg])
                    state["i"] += 1
            return epi

        def cross_attn(l):
            with contextlib.ExitStack() as st:
                m32 = k.sb(st, "ca_m32", [128, 16, NMEM], F32)
                msq = k.sb(st, "ca_msq", [128, 16, NMEM], BF16)
                mn = k.sb(st, "ca_mn", [128, 16, NMEM], BF16)
                r = k.sb(st, "ca_r", [128, NMEM], F32)
                tmp = k.sb(st, "ca_t", [128, NMEM], F32)
                k.dma("sp", m32[:], memT.rearrange("(c p) m -> p c m", p=128), m32, writes=[m32])
                k.op("act", lambda e: e.activation(out=msq[:], in_=m32[:], func=AF.Square), reads=[m32], writes=[msq])
                ps = ps_next()
                for c in range(16):
                    k.op("pe", lambda e, c=c: e.matmul(ps[:, 0:NMEM], lhsT=c_ones[:], rhs=msq[:, c, :], start=(c == 0), stop=(c == 15)),
                         reads=[c_ones, msq], writes=[ps], inc=(c == 15))
                rstd_from(ps, NMEM, D, r, tmp)
                for c in range(16):
                    k.op("dve", lambda e, c=c: e.scalar_tensor_tensor(out=mn[:, c, :], in0=m32[:, c, :], scalar=ngcol(l, 4, c), in1=r[:],
                                                                      op0=ALU.mult, op1=ALU.mult), reads=[m32, r, c_ng], writes=[mn])
                k.dma("pool", MN.rearrange("(c p) m -> p c m", p=128), mn[:], mn, reads=[mn])
            k.barrier()
            with contextlib.ExitStack() as st:
                mn = k.sb(st, "ca_mn2", [128, 16, NMEM], BF16)
                wts = [k.sb(st, "ca_w", [128, 16, 512], BF16) for _ in range(2)]
                stg = [k.sb(st, "ca_s", [128, 512], BF16) for _ in range(2)]
                k.dma("sp", mn[:], MN.rearrange("(c p) m -> p c m", p=128), mn, writes=[mn])
                Wv = b_xkv[l].rearrange("(kc p) n -> p kc n", p=128)
                si = 0
                for g in range(8):
                    wt = wts[g % 2]
                    k.dma("sp", wt[:], Wv[:, :, g * 512:(g + 1) * 512], wt, writes=[wt])
                    if g < 4:
                        for bi in range(4):
                            ps = ps_next()
                            for kc in range(16):
                                k.op("pe", lambda e, ps=ps, wt=wt, kc=kc, bi=bi: e.matmul(
                                    ps[:, 0:NMEM], lhsT=wt[:, kc, bi * 128:(bi + 1) * 128], rhs=mn[:, kc, :], start=(kc == 0), stop=(kc == 15)),
                                    reads=[wt, mn], writes=[ps], inc=(kc == 15))
                            sg = stg[si % 2]
                            si += 1
                            k.op("act", lambda e, ps=ps, sg=sg: e.copy(out=sg[:, 0:NMEM], in_=ps[:, 0:NMEM]), reads=[ps], writes=[sg])
                            n0 = g * 512 + bi * 128
                            k.dma("pool", KxT[n0:n0 + 128, :], sg[:, 0:NMEM], sg, reads=[sg])
                    else:
                        for mb in range(2):
                            ps = ps_next()
                            for kc in range(16):
                                k.op("pe", lambda e, ps=ps, wt=wt, kc=kc, mb=mb: e.matmul(
                                    ps[:], lhsT=mn[:, kc, mb * 128:(mb + 1) * 128], rhs=wt[:, kc, :], start=(kc == 0), stop=(kc == 15)),
                                    reads=[wt, mn], writes=[ps], inc=(kc == 15))
                            sg = stg[si % 2]
                            si += 1
                            k.op("act", lambda e, ps=ps, sg=sg: e.copy(out=sg[:], in_=ps[:]), reads=[ps], writes=[sg])
                            e0 = (g - 4) * 512
                            k.dma("pool", Vx[mb * 128:(mb + 1) * 128, e0:e0 + 512], sg[:], sg, reads=[sg])
            k.barrier()
            with contextlib.ExitStack() as st:
                epi = make_store_epi(st, lambda tag, t0, n: QxT[tag * 128:(tag + 1) * 128, t0:t0 + n], BF16, TB, scale=512 ** -0.5)
                linear_fm(H, D, b_xq[l], [(i * 128, i) for i in range(16)], epi)
            k.barrier()
            with contextlib.ExitStack() as st:
                kx = k.sb(st, "ca_kx", [128, 16, NMEM], BF16)
                vx = k.sb(st, "ca_vx", [128, 2, D], BF16)
                qs = [k.sb(st, "ca_q", [128, 16, 512], BF16) for _ in range(2)]
                es = [k.sb(st, "ca_e", [128, 2, 512], BF16) for _ in range(2)]
                rz = [k.sb(st, "ca_rz", [128, 512], F32) for _ in range(2)]
                ot = [k.sb(st, "ca_o", [128, 16, 512], BF16) for _ in range(2)]
                k.dma("sp", kx[:], KxT.rearrange("(c p) m -> p c m", p=128), kx, writes=[kx])
                k.dma("sp", vx[:], Vx.rearrange("(mb p) e -> p mb e", p=128), vx, writes=[vx])
                qv = QxT.rearrange("(c p) s -> p c s", p=128)
                ov = OxT.rearrange("(c p) s -> p c s", p=128)
                ei = 0
                for ti in range(NT):
                    t0 = ti * 512
                    q = qs[ti % 2]
                    o = ot[ti % 2]
                    k.dma("sp", q[:], qv[:, :, t0:t0 + 512], q, writes=[q])
                    for hh in range(4):
                        ee = es[ei % 2]
                        rzz = rz[ei % 2]
                        ei += 1
                        for mb in range(2):
                            ps = ps_next()
                            for dc in range(4):
                                k.op("pe", lambda e, ps=ps, dc=dc, mb=mb, hh=hh: e.matmul(
                                    ps[:], lhsT=kx[:, hh * 4 + dc, mb * 128:(mb + 1) * 128], rhs=q[:, hh * 4 + dc, :],
                                    start=(dc == 0), stop=(dc == 3)), reads=[kx, q], writes=[ps], inc=(dc == 3))
                            k.op("act", lambda e, ps=ps, ee=ee, mb=mb: e.activation(out=ee[:, mb, :], in_=ps[:], func=AF.Exp),
                                 reads=[ps], writes=[ee])
                        pz = ps_next()
                        for mb in range(2):
                            k.op("pe", lambda e, pz=pz, ee=ee, mb=mb: e.matmul(pz[:], lhsT=c_ones[:], rhs=ee[:, mb, :], start=(mb == 0), stop=(mb == 1)),
                                 reads=[c_ones, ee], writes=[pz], inc=(mb == 1))
                        k.op("dve", lambda e, pz=pz, rzz=rzz: e.reciprocal(out=rzz[:], in_=pz[:]), reads=[pz], writes=[rzz])
                        for eb in range(4):
                            po = ps_next()
                            for mb in range(2):
                                k.op("pe", lambda e, po=po, ee=ee, mb=mb, eb=eb, hh=hh: e.matmul(
                                    po[:], lhsT=vx[:, mb, hh * 512 + eb * 128:hh * 512 + (eb + 1) * 128], rhs=ee[:, mb, :],
                                    start=(mb == 0), stop=(mb == 1)), reads=[vx, ee], writes=[po], inc=(mb == 1))
                            k.op("dve", lambda e, po=po, o=o, rzz=rzz, eb=eb, hh=hh: e.tensor_tensor(
                                out=o[:, hh * 4 + eb, :], in0=po[:], in1=rzz[:], op=ALU.mult), reads=[po, rzz], writes=[o])
                    k.dma("pool", ov[:, :, t0:t0 + 512], o[:], o, reads=[o])
            k.barrier()
            with contextlib.ExitStack() as st:
                epi = make_store_epi(st, lambda tag, t0, n: Y[tag * 128:(tag + 1) * 128, t0:t0 + n], F32, TB)
                linear_fm(OxT, D, b_xo[l], [(i * 128, i) for i in range(16)], epi)
            k.barrier()

        def ffn(l):
            TBF = min(1024, S)
            with contextlib.ExitStack() as st:
                KC = 16
                act = k.sb(st, "ff_act", [128, KC, TBF + 2], BF16)
                wa = [k.sb(st, "ff_wa", [128, KC, 128], BF16) for _ in range(2)]
                wg = [k.sb(st, "ff_wg", [128, KC, 128], BF16) for _ in range(2)]
                ua = k.sb(st, "ff_ua", [128, TBF + 2], F32)
                ug = k.sb(st, "ff_ug", [128, TBF + 2], F32)
                ca = k.sb(st, "ff_ca", [128, TBF], F32)
                cg = k.sb(st, "ff_cg", [128, TBF], F32)
                t1 = k.sb(st, "ff_t1", [128, TBF], F32)
                og = [k.sb(st, "ff_og", [128, TBF], BF16) for _ in range(2)]
                Wv = b_up[l].rearrange("(kc p) n -> p kc n", p=128)
                inv = H.rearrange("(kc p) s -> p kc s", p=128)
                it = 0
                for tb0 in range(0, S, TBF):
                    lo = 1 if tb0 == 0 else 0
                    hi = TBF + 1 if tb0 + TBF >= S else TBF + 2
                    if lo == 1:
                        k.op("pool", lambda e: e.memset(act[:, :, 0:1], 0.0), writes=[act])
                    if hi == TBF + 1:
                        k.op("pool", lambda e: e.memset(act[:, :, TBF + 1:TBF + 2], 0.0), writes=[act])
                    k.dma("sp", act[:, :, lo:hi], inv[:, :, tb0 - 1 + lo:tb0 - 1 + hi], act, writes=[act])
                    for nb in range(44):
                        wta, wtg = wa[it % 2], wg[it % 2]
                        o = og[it % 2]
                        it += 1
                        k.dma("sp", wta[:], Wv[:, :, nb * 128:(nb + 1) * 128], wta, writes=[wta])
                        k.dma("sp", wtg[:], Wv[:, :, DFF + nb * 128:DFF + (nb + 1) * 128], wtg, writes=[wtg])
                        for (wt, ub, engs) in ((wta, ua, "act"), (wtg, ug, "dve")):
                            c0 = 0
                            while c0 < TBF + 2:
                                n = min(512, TBF + 2 - c0)
                                ps = ps_next()
                                for kc in range(KC):
                                    k.op("pe", lambda e, ps=ps, wt=wt, kc=kc, c0=c0, n=n: e.matmul(
                                        ps[:, 0:n], lhsT=wt[:, kc, :], rhs=act[:, kc, c0:c0 + n], start=(kc == 0), stop=(kc == KC - 1)),
                                        reads=[wt, act], writes=[ps], inc=(kc == KC - 1))
                                if engs == "act":
                                    k.op("act", lambda e, ps=ps, ub=ub, c0=c0, n=n: e.copy(out=ub[:, c0:c0 + n], in_=ps[:, 0:n]), reads=[ps], writes=[ub])
                                else:
                                    k.op("dve", lambda e, ps=ps, ub=ub, c0=c0, n=n: e.tensor_copy(out=ub[:, c0:c0 + n], in_=ps[:, 0:n]), reads=[ps], writes=[ub])
                                c0 += n
                        for (ub, cb, f0, eng) in ((ua, ca, nb, "pool"), (ug, cg, 44 + nb, "dve")):
                            w0 = c_cw[:, (l * 3 + 0) * 88 + f0:(l * 3 + 0) * 88 + f0 + 1]
                            w1 = c_cw[:, (l * 3 + 1) * 88 + f0:(l * 3 + 1) * 88 + f0 + 1]
                            w2 = c_cw[:, (l * 3 + 2) * 88 + f0:(l * 3 + 2) * 88 + f0 + 1]
                            bb = c_cb[:, l * 88 + f0:l * 88 + f0 + 1]
                            k.op("dve", lambda e, ub=ub, cb=cb, w1=w1, bb=bb: e.tensor_scalar(
                                out=cb[:], in0=ub[:, 1:TBF + 1], scalar1=w1, scalar2=bb, op0=ALU.mult, op1=ALU.add),
                                reads=[ub, c_cw, c_cb], writes=[cb])
                            k.op("dve", lambda e, ub=ub, cb=cb, w0=w0: e.scalar_tensor_tensor(
                                out=cb[:], in0=ub[:, 0:TBF], scalar=w0, in1=cb[:], op0=ALU.mult, op1=ALU.add),
                                reads=[ub, cb, c_cw], writes=[cb])
                            k.op("dve", lambda e, ub=ub, cb=cb, w2=w2: e.scalar_tensor_tensor(
                                out=cb[:], in0=ub[:, 2:TBF + 2], scalar=w2, in1=cb[:], op0=ALU.mult, op1=ALU.add),
                                reads=[ub, cb, c_cw], writes=[cb])
                        k.op("act", lambda e: e.activation(out=t1[:], in_=cg[:], func=AF.Square), reads=[cg], writes=[t1])
                        k.op("pool", lambda e: e.tensor_scalar(out=t1[:], in0=t1[:], scalar1=0.044715, scalar2=1.0, op0=ALU.mult, op1=ALU.add),
                             reads=[t1], writes=[t1])
                        k.op("pool", lambda e: e.tensor_tensor(out=t1[:], in0=t1[:], in1=cg[:], op=ALU.mult), reads=[t1, cg], writes=[t1])
                        k.op("act", lambda e: e.activation(out=t1[:], in_=t1[:], func=AF.Sigmoid, scale=1.5957691216057308), reads=[t1], writes=[t1])
                        k.op("pool", lambda e: e.tensor_tensor(out=t1[:], in0=t1[:], in1=cg[:], op=ALU.mult), reads=[t1, cg], writes=[t1])
                        k.op("pool", lambda e, o=o: e.tensor_tensor(out=o[:], in0=t1[:], in1=ca[:], op=ALU.mult), reads=[t1, ca], writes=[o])
                        k.dma("pool", GG[nb * 128:(nb + 1) * 128, tb0:tb0 + TBF], o[:], o, reads=[o])
            k.barrier()
            with contextlib.ExitStack() as st:
                epi = make_store_epi(st, lambda tag, t0, n: Y[tag * 128:(tag + 1) * 128, t0:t0 + n], F32, TBF)
                linear_fm(GG, DFF, b_down[l], [(i * 128, i) for i in range(16)], epi, tb=TBF, gw=256)
            k.barrier()

        def even_mixer(l):
            i = l // 2
            lam_init = 0.8 - 0.6 * math.exp(-0.3 * l)
            with contextlib.ExitStack() as st:
                stg = [k.sb(st, "em_s", [128, 512], BF16) for _ in range(3)]
                si = [0]

                def epi_tm(dst):
                    def epi(ps, g0, gwid, tok0):
                        sg = stg[si[0] % 3]
                        si[0] += 1
                        if si[0] % 2:
                            k.op("act", lambda e: e.copy(out=sg[:, 0:gwid], in_=ps[:, 0:gwid]), reads=[ps], writes=[sg])
                        else:
                            k.op("dve", lambda e: e.tensor_copy(out=sg[:, 0:gwid], in_=ps[:, 0:gwid]), reads=[ps], writes=[sg])
                        k.dma("pool", dst[tok0:tok0 + 128, g0:g0 + gwid], sg[:, 0:gwid], sg, reads=[sg])
                    return epi
                linear_tm(H, D, b_in_even[i], 0, 1024, epi_tm(Ut))
                linear_tm(H, D, b_in_even[i], 3072, 1024, epi_tm(Vt))
            k.barrier()
            with contextlib.ExitStack() as st:
                def dstf(tag, t0, n):
                    return (QT if tag < 8 else KT)[(tag % 8) * 128:(tag % 8 + 1) * 128, t0:t0 + n]
                epi = make_store_epi(st, dstf, BF16, TB)
                linear_fm(H, D, b_in_even[i], [(1024 + j * 128, j) for j in range(16)], epi)
            k.barrier()
            with contextlib.ExitStack() as st:
                NSC = S // 128
                ug = k.sb(st, "fn_u", [128, NSC, 256], BF16)
                SG = 8 if NSC >= 8 else NSC
                cs = [k.sb(st, "fn_c", [128, SG, 512], BF16) for _ in range(2)]
                sn = [k.sb(st, "fn_s", [128, SG, 512], BF16) for _ in range(2)]
                cc = k.sb(st, "fn_cc", [128, 2, 256], BF16)
                sc = k.sb(st, "fn_sc", [128, 2, 256], BF16)
                ab = [k.sb(st, "fn_ab", [128, 4, 512], BF16) for _ in range(2)]
                yo = [k.sb(st, "fn_y", [128, 2, 512], BF16) for _ in range(2)]
                k.dma("sp", cc[:], cosC.rearrange("(c p) n -> p c n", p=128), cc, writes=[cc])
                k.dma("sp", sc[:], nsinC.rearrange("(c p) n -> p c n", p=128), sc, writes=[sc])
                cv = cosS.rearrange("(c p) n -> p c n", p=128)
                sv = sinS.rearrange("(c p) n -> p c n", p=128)
                uv = Ut.rearrange("(c p) n -> p c n", p=128)
                li = 0
                oi = 0
                for g in range(4):
                    for u0 in range(0, NSC, 8):
                        u1 = min(NSC, u0 + 8)
                        k.dma("sp", ug[:, u0:u1, :], uv[:, u0:u1, g * 256:(g + 1) * 256], ug, writes=[ug])
                    for kt in range(NT):
                        pacc = [PS[0], PS[1], PS[2], PS[3]]
                        for s0 in range(0, NSC, SG):
                            ct, stt = cs[li % 2], sn[li % 2]
                            li += 1
                            k.dma("sp", ct[:], cv[:, s0:s0 + SG, kt * 512:(kt + 1) * 512], ct, writes=[ct])
                            k.dma("sp", stt[:], sv[:, s0:s0 + SG, kt * 512:(kt + 1) * 512], stt, writes=[stt])
                            for sj in range(SG):
                                sci = s0 + sj
                                for cb in range(2):
                                    for (mi, mt) in ((0, ct), (1, stt)):
                                        p = pacc[mi * 2 + cb]
                                        k.op("pe", lambda e, p=p, mt=mt, sj=sj, sci=sci, cb=cb: e.matmul(
                                            p[:], lhsT=ug[:, sci, cb * 128:(cb + 1) * 128], rhs=mt[:, sj, :],
                                            start=(sci == 0), stop=(sci == NSC - 1)),
                                            reads=[ug, mt], writes=[p], inc=(sci == NSC - 1) or (sj == SG - 1 and cb == 1 and mi == 1))
                        a = ab[oi % 2]
                        y = yo[oi % 2]
                        oi += 1
                        for j in range(4):
                            if j % 2 == 0:
                                k.op("act", lambda e, j=j: e.copy(out=a[:, j, :], in_=pacc[j][:]), reads=[pacc[j]], writes=[a])
                            else:
                                k.op("dve", lambda e, j=j: e.tensor_copy(out=a[:, j, :], in_=pacc[j][:]), reads=[pacc[j]], writes=[a])
                        for cpb in range(2):
                            py = ps_pick(4, 8)
                            for j in range(4):
                                mat = cc if j < 2 else sc
                                k.op("pe", lambda e, py=py, mat=mat, j=j, cpb=cpb: e.matmul(
                                    py[:], lhsT=mat[:, j % 2, cpb * 128:(cpb + 1) * 128], rhs=a[:, j, :], start=(j == 0), stop=(j == 3)),
                                    reads=[mat, a], writes=[py], inc=(j == 3))
                            k.op("act", lambda e, py=py, cpb=cpb: e.copy(out=y[:, cpb, :], in_=py[:]), reads=[py], writes=[y])
                        k.dma("pool", CAT[g * 256:(g + 1) * 256, kt * 512:(kt + 1) * 512].rearrange("(c p) s -> p c s", p=128),
                              y[:], y, reads=[y])
            k.barrier()
            with contextlib.ExitStack() as st:
                kt_ = k.sb(st, "da_k", [128, S], BF16)
                qt_ = k.sb(st, "da_q", [128, S], BF16)
                vt_ = k.sb(st, "da_v", [128, NKB, 128], BF16)
                bt_ = k.sb(st, "da_bt", [128, 6, 512], F32)
                es = [k.sb(st, "da_e", [128, 2, 512], BF16) for _ in range(3)]
                tm = [k.sb(st, "da_tm", [128, 2, 512], F32) for _ in range(2)]
                rz = k.sb(st, "da_rz", [128, 2, 512], F32)
                o32 = k.sb(st, "da_o", [128, 512], F32)
                t2 = k.sb(st, "da_t2", [128, 512], F32)
                osq = k.sb(st, "da_sq", [128, 512], BF16)
                rs = k.sb(st, "da_rs", [128, 512], F32)
                tmp = k.sb(st, "da_tmp", [128, 512], F32)
                ob = [k.sb(st, "da_ob", [128, 512], BF16) for _ in range(2)]
                hg = k.sb(st, "da_hg", [128, 1], F32)
                k.op("dve", lambda e: e.tensor_scalar(out=hg[:], in0=c_dng[:, i:i + 1], scalar1=(1.0 - lam_init), scalar2=None, op0=ALU.mult),
                     reads=[c_dng], writes=[hg])
                ei = 0
                oi = 0
                for hh in range(8):
                    k.dma("sp", kt_[:], KT[hh * 128:(hh + 1) * 128, :], kt_, writes=[kt_])
                    k.dma("sp", qt_[:], QT[hh * 128:(hh + 1) * 128, :], qt_, writes=[qt_])
                    vsrc = Vt[:, hh * 128:(hh + 1) * 128].rearrange("(kb p) e -> p kb e", p=128)
                    for v0 in range(0, NKB, 8):
                        v1 = min(NKB, v0 + 8)
                        k.dma("sp", vt_[:, v0:v1, :], vsrc[:, v0:v1, :], vt_, writes=[vt_])
                    k.dma("sp", bt_[:], BT[hh].rearrange("p (d q) -> p d q", q=512), bt_, writes=[bt_])
                    for qi in range(NT):
                        i0 = qi * 4
                        po = [PS[0], PS[1]]
                        pz = [PS[2], PS[3]]
                        for kb in range(NKB):
                            Dd = kb - i0
                            near = (-1 <= Dd <= 4)
                            side = 1 if kb > i0 else 0
                            ee = es[ei % 3]
                            ei += 1
                            pss = [ps_pick(4, 8), ps_pick(4, 8)]
                            for c in range(2):
                                k.op("pe", lambda e, c=c, kb=kb, qi=qi, pss=pss: e.matmul(
                                    pss[c][:], lhsT=kt_[c * 64:(c + 1) * 64, kb * 128:(kb + 1) * 128],
                                    rhs=qt_[c * 64:(c + 1) * 64, qi * 512:(qi + 1) * 512], start=True, stop=True),
                                    reads=[kt_, qt_], writes=[pss[c]], inc=True)
                            if near:
                                tt = tm[ei % 2]
                                for c in range(2):
                                    k.op("dve", lambda e, c=c, tt=tt, pss=pss, Dd=Dd: e.scalar_tensor_tensor(
                                        out=tt[:, c, :], in0=pss[c][:], scalar=0.125, in1=bt_[:, Dd + 1, :], op0=ALU.mult, op1=ALU.add),
                                        reads=[pss[c], bt_], writes=[tt])
                                k.op("act", lambda e, tt=tt, ee=ee, kb=kb: e.activation(
                                    out=ee[:], in_=tt[:], func=AF.Exp, bias=c_kmask[:, kb:kb + 1]), reads=[tt, c_kmask], writes=[ee])
                            else:
                                fo = (hh * 2 + side) * NKB + kb
                                for c in range(2):
                                    k.op("act", lambda e, c=c, ee=ee, pss=pss, fo=fo: e.activation(
                                        out=ee[:, c, :], in_=pss[c][:], func=AF.Exp, scale=0.125, bias=c_far[:, fo:fo + 1]),
                                        reads=[pss[c], c_far], writes=[ee])
                            for c in range(2):
                                k.op("pe", lambda e, c=c, ee=ee, kb=kb: e.matmul(
                                    po[c][:], lhsT=vt_[:, kb, :], rhs=ee[:, c, :], start=(kb == 0), stop=(kb == NKB - 1)),
                                    reads=[vt_, ee], writes=[po[c]], inc=(kb == NKB - 1))
                                k.op("pe", lambda e, c=c, ee=ee, kb=kb: e.matmul(
                                    pz[c][:], lhsT=c_ones[:], rhs=ee[:, c, :], start=(kb == 0), stop=(kb == NKB - 1)),
                                    reads=[c_ones, ee], writes=[pz[c]], inc=(kb == NKB - 1))
                        for c in range(2):
                            k.op("dve", lambda e, c=c: e.reciprocal(out=rz[:, c, :], in_=pz[c][:]), reads=[pz[c]], writes=[rz])
                        k.op("dve", lambda e: e.tensor_tensor(out=o32[:], in0=po[0][:], in1=rz[:, 0, :], op=ALU.mult), reads=[po[0], rz], writes=[o32])
                        k.op("dve", lambda e: e.tensor_tensor(out=t2[:], in0=po[1][:], in1=rz[:, 1, :], op=ALU.mult), reads=[po[1], rz], writes=[t2])
                        k.op("dve", lambda e: e.scalar_tensor_tensor(out=o32[:], in0=t2[:], scalar=c_lam[:, i:i + 1], in1=o32[:],
                                                                     op0=ALU.mult, op1=ALU.add), reads=[t2, o32, c_lam], writes=[o32])
                        k.op("act", lambda e: e.activation(out=osq[:], in_=o32[:], func=AF.Square), reads=[o32], writes=[osq])
                        pn = ps_pick(4, 8)
                        k.op("pe", lambda e, pn=pn: e.matmul(pn[:], lhsT=c_ones[:], rhs=osq[:], start=True, stop=True),
                             reads=[c_ones, osq], writes=[pn], inc=True)
                        rstd_from(pn, 512, 128, rs, tmp)
                        obb = ob[oi % 2]
                        oi += 1
                        k.op("dve", lambda e, obb=obb: e.scalar_tensor_tensor(out=obb[:], in0=o32[:], scalar=hg[:, 0:1], in1=rs[:],
                                                                              op0=ALU.mult, op1=ALU.mult), reads=[o32, hg, rs], writes=[obb])
                        k.dma("pool", CAT[1024 + hh * 128:1024 + (hh + 1) * 128, qi * 512:(qi + 1) * 512], obb[:], obb, reads=[obb])
            k.barrier()
            with contextlib.ExitStack() as st:
                epi = make_store_epi(st, lambda tag, t0, n: Y[tag * 128:(tag + 1) * 128, t0:t0 + n], F32, TB)
                linear_fm(CAT, D, b_out_even[i], [(j * 128, j) for j in range(16)], epi)
            k.barrier()

        def odd_mixer(l):
            i = l // 2
            C = 64
            with contextlib.ExitStack() as st:
                def dstf(tag, t0, n):
                    return (QgT if tag < 8 else KgT)[(tag % 8) * 128:(tag % 8 + 1) * 128, t0:t0 + n]
                epi_q = make_store_epi(st, dstf, BF16, TB, scale=256 ** -0.5)
                linear_fm(H, D, b_in_odd[i], [(j * 128, j) for j in range(8)], epi_q)
            k.barrier()
            with contextlib.ExitStack() as st:
                def dstf2(tag, t0, n):
                    return KgT[tag * 128:(tag + 1) * 128, t0:t0 + n]
                epi_k = make_store_epi(st, dstf2, BF16, TB)
                linear_fm(H, D, b_in_odd[i], [(1024 + j * 128, j) for j in range(8)], epi_k)
            k.barrier()
            with contextlib.ExitStack() as st:
                def dstf3(tag, t0, n):
                    return RS[tag * 128:(tag + 1) * 128, t0:t0 + n]
                epi_r = make_store_epi(st, dstf3, BF16, TB, func=AF.Silu)
                linear_fm(H, D, b_in_odd[i], [(4096 + j * 128, j) for j in range(16)], epi_r)
            k.barrier()
            with contextlib.ExitStack() as st:
                stg = [k.sb(st, "om_s", [128, 512], BF16) for _ in range(3)]
                si = [0]

                def epi_tm(dst):
                    def epi(ps, g0, gwid, tok0):
                        sg = stg[si[0] % 3]
                        si[0] += 1
                        if si[0] % 2:
                            k.op("act", lambda e: e.copy(out=sg[:, 0:gwid], in_=ps[:, 0:gwid]), reads=[ps], writes=[sg])
                        else:
                            k.op("dve", lambda e: e.tensor_copy(out=sg[:, 0:gwid], in_=ps[:, 0:gwid]), reads=[ps], writes=[sg])
                        k.dma("pool", dst[tok0:tok0 + 128, g0:g0 + gwid], sg[:, 0:gwid], sg, reads=[sg])
                    return epi
                linear_tm(H, D, b_in_odd[i], 1024, 1024, epi_tm(Kg))
                linear_tm(H, D, b_in_odd[i], 2048, 2048, epi_tm(Vg))
            k.barrier()
            with contextlib.ExitStack() as st:
                act = k.sb(st, "gd_act", [128, 16, TB], BF16)
                wt = k.sb(st, "gd_w", [128, 16, 32], BF16)
                sg = [k.sb(st, "gd_s", [32, 512], BF16) for _ in range(2)]
                k.dma("sp", wt[:], b_gd[i].rearrange("(kc p) n -> p kc n", p=128), wt, writes=[wt])
                inv = H.rearrange("(kc p) s -> p kc s", p=128)
                n_ = 0
                for tb0 in range(0, S, TB):
                    k.dma("sp", act[:], inv[:, :, tb0:tb0 + TB], act, writes=[act])
                    for tt in range(TB // 512):
                        ps = ps_next()
                        for kc in range(16):
                            k.op("pe", lambda e, ps=ps, kc=kc, tt=tt: e.matmul(ps[0:32, :], lhsT=wt[:, kc, :], rhs=act[:, kc, tt * 512:(tt + 1) * 512],
                                                                               start=(kc == 0), stop=(kc == 15)),
                                 reads=[wt, act], writes=[ps], inc=(kc == 15))
                        s_ = sg[n_ % 2]
                        n_ += 1
                        k.op("act", lambda e, ps=ps, s_=s_: e.copy(out=s_[:], in_=ps[0:32, :]), reads=[ps], writes=[s_])
                        k.dma("pool", HDT[:, tb0 + tt * 512:tb0 + (tt + 1) * 512], s_[:], s_, reads=[s_])
            k.barrier()
            with contextlib.ExitStack() as st:
                hda = [k.sb(st, "gu_h", [32, S], BF16) for _ in range(2)]
                wu = [k.sb(st, "gu_w", [32, 1024], BF16) for _ in range(2)]
                gs = [k.sb(st, "gu_g", [128, 1024], F32) for _ in range(2)]
                ghs = [k.sb(st, "gu_gh", [128, 1024], BF16) for _ in range(2)]
                gls = [k.sb(st, "gu_gl", [128, 1024], BF16) for _ in range(2)]
                n_ = 0
                for d in range(2):
                    hd = hda[d]
                    k.op("pool", lambda e, hd=hd: e.memset(hd[:], 1.0), writes=[hd])
                    k.dma("sp", hd[0:16, :], HDT[d * 16:(d + 1) * 16, :], hd, writes=[hd])
                    k.dma("sp", wu[d][:], b_gu[(i * 2 + d) * 32:(i * 2 + d + 1) * 32, :], wu[d], writes=[wu[d]])
                    for tk in range(S // 128):
                        g_ = gs[n_ % 2]
                        n_ += 1
                        for half in range(2):
                            ps = ps_next()
                            k.op("pe", lambda e, ps=ps, hd=hd, d=d, tk=tk, half=half: e.matmul(
                                ps[:], lhsT=hd[:, tk * 128:(tk + 1) * 128], rhs=wu[d][:, half * 512:(half + 1) * 512], start=True, stop=True),
                                reads=[hd, wu[d]], writes=[ps], inc=True)
                            k.op("act", lambda e, ps=ps, g_=g_, half=half: e.activation(out=g_[:, half * 512:(half + 1) * 512], in_=ps[:], func=AF.Exp, scale=-1.0),
                                 reads=[ps], writes=[g_])
                        k.op("dve", lambda e, g_=g_: e.tensor_scalar(out=g_[:], in0=g_[:], scalar1=1.0, scalar2=None, op0=ALU.add), reads=[g_], writes=[g_])
                        k.op("act", lambda e, g_=g_: e.activation(out=g_[:], in_=g_[:], func=AF.Ln), reads=[g_], writes=[g_])
                        k.op("pool", lambda e, g_=g_: e.tensor_scalar(out=g_[:], in0=g_[:], scalar1=-1.0 / 16.0, scalar2=None, op0=ALU.mult), reads=[g_], writes=[g_])
                        gh_ = ghs[n_ % 2]
                        gl_ = gls[n_ % 2]
                        k.op("act", lambda e, g_=g_, gh_=gh_: e.copy(out=gh_[:], in_=g_[:]), reads=[g_], writes=[gh_])
                        k.op("dve", lambda e, g_=g_, gh_=gh_: e.tensor_tensor(out=g_[:], in0=g_[:], in1=gh_[:], op=ALU.subtract), reads=[g_, gh_], writes=[g_])
                        k.op("pool", lambda e, g_=g_, gl_=gl_: e.tensor_copy(out=gl_[:], in_=g_[:]), reads=[g_], writes=[gl_])
                        k.dma("pool", GH[d, tk * 128:(tk + 1) * 128, :], gh_[:], gh_, reads=[gh_])
                        k.dma("pool", GL[d, tk * 128:(tk + 1) * 128, :], gl_[:], gl_, reads=[gl_])
            k.barrier()
            with contextlib.ExitStack() as st:
                SC = 512
                NCH = SC // C
                qs = [k.sb(st, "gl_q", [128, 2, SC], BF16) for _ in range(2)]
                ks = [k.sb(st, "gl_k", [128, 2, SC], BF16) for _ in range(2)]
                kt = [k.sb(st, "gl_kt", [C, NCH, 256], BF16) for _ in range(2)]
                vt = [k.sb(st, "gl_vt", [C, NCH, 512], BF16) for _ in range(2)]
                gt = [k.sb(st, "gl_gt", [C, NCH, 256], BF16) for _ in range(2)]
                gtl = [k.sb(st, "gl_gtl", [C, NCH, 256], BF16) for _ in range(2)]
                ep = [k.sb(st, "gl_ep", [128, 2, C], F32) for _ in range(2)]
                en = [k.sb(st, "gl_en", [128, 2, C], F32) for _ in range(2)]
                ek = [k.sb(st, "gl_ek", [C, 256], F32) for _ in range(2)]
                ql = [k.sb(st, "gl_ql", [128, 2, C], BF16) for _ in range(2)]
                kl = [k.sb(st, "gl_kl", [128, 2, C], BF16) for _ in range(2)]
                kh = [k.sb(st, "gl_kh", [C, 256], BF16) for _ in range(2)]
                am = [k.sb(st, "gl_am", [C, C], BF16) for _ in range(2)]
                s32 = k.sb(st, "gl_s32", [128, 2, 512], F32)
                sbf = k.sb(st, "gl_sbf", [128, 2, 512], BF16)
                oo = [k.sb(st, "gl_o", [128, 4, SC], F32) for _ in range(2)]
                ci = 0
                sci = 0
                for d in range(2):
                    Ltri = c_trib[:, (0 if d == 0 else 64):(64 if d == 0 else 128)]
                    Mtri = c_trib[:, (128 if d == 0 else 192):(192 if d == 0 else 256)]
                    Lmask = c_trib[:, (0 if d == 0 else 64):(64 if d == 0 else 128)]
                    for hh in range(4):
                        k.op("pool", lambda e: e.memset(s32[:], 0.0), writes=[s32])
                        k.op("pool", lambda e: e.memset(sbf[:], 0.0), writes=[sbf])
                        sc_order = list(range(S // SC))
                        if d == 1:
                            sc_order.reverse()
                        for scx in sc_order:
                            t0 = scx * SC
                            b = sci % 2
                            sci += 1
                            q_, k_, kt_, vt_, gt_, gl2_, o_ = qs[b], ks[b], kt[b], vt[b], gt[b], gtl[b], oo[b]
                            k.dma("sp", q_[:], QgT[hh * 256:(hh + 1) * 256, t0:t0 + SC].rearrange("(c p) s -> p c s", p=128), q_, writes=[q_])
                            k.dma("sp", k_[:], KgT[hh * 256:(hh + 1) * 256, t0:t0 + SC].rearrange("(c p) s -> p c s", p=128), k_, writes=[k_])
                            k.dma("sp", kt_[:], Kg[t0:t0 + SC, hh * 256:(hh + 1) * 256].rearrange("(c p) n -> p c n", p=C), kt_, writes=[kt_])
                            k.dma("sp", vt_[:], Vg[t0:t0 + SC, hh * 512:(hh + 1) * 512].rearrange("(c p) n -> p c n", p=C), vt_, writes=[vt_])
                            k.dma("sp", gt_[:], GH[d, t0:t0 + SC, hh * 256:(hh + 1) * 256].rearrange("(c p) n -> p c n", p=C), gt_, writes=[gt_])
                            k.dma("sp", gl2_[:], GL[d, t0:t0 + SC, hh * 256:(hh + 1) * 256].rearrange("(c p) n -> p c n", p=C), gl2_, writes=[gl2_])
                            ch_order = list(range(NCH))
                            if d == 1:
                                ch_order.reverse()
                            for ch in ch_order:
                                x = ci % 2
                                ci += 1
                                c0 = ch * C
                                pb = ps_next()
                                for db in range(2):
                                    k.op("pe", lambda e, pb=pb, db=db, ch=ch, gt_=gt_: e.matmul(
                                        pb[:, db * C:(db + 1) * C], lhsT=gt_[:, ch, db * 128:(db + 1) * 128], rhs=Ltri, start=True, stop=False),
                                        reads=[gt_, c_trib], writes=[pb], inc=False)
                                    k.op("pe", lambda e, pb=pb, db=db, ch=ch, gl2_=gl2_: e.matmul(
                                        pb[:, db * C:(db + 1) * C], lhsT=gl2_[:, ch, db * 128:(db + 1) * 128], rhs=Ltri, start=False, stop=True),
                                        reads=[gl2_, c_trib], writes=[pb], inc=True)
                                pk = ps_next()
                                k.op("pe", lambda e, pk=pk, ch=ch, gt_=gt_: e.matmul(pk[0:C, 0:256], lhsT=Mtri, rhs=gt_[:, ch, :], start=True, stop=False),
                                     reads=[gt_, c_trib], writes=[pk], inc=False)
                                k.op("pe", lambda e, pk=pk, ch=ch, gl2_=gl2_: e.matmul(pk[0:C, 0:256], lhsT=Mtri, rhs=gl2_[:, ch, :], start=False, stop=True),
                                     reads=[gl2_, c_trib], writes=[pk], inc=True)
                                ep_, en_, ek_ = ep[x], en[x], ek[x]
                                k.op("act", lambda e, pb=pb, ep_=ep_: e.activation(out=ep_[:], in_=pb[:, 0:2 * C].rearrange("p (a c) -> p a c", c=C), func=AF.Exp), reads=[pb], writes=[ep_])
                                k.op("act", lambda e, pb=pb, en_=en_: e.activation(out=en_[:], in_=pb[:, 0:2 * C].rearrange("p (a c) -> p a c", c=C), func=AF.Exp, scale=-1.0), reads=[pb], writes=[en_])
                                k.op("act", lambda e, pk=pk, ek_=ek_: e.activation(out=ek_[:], in_=pk[0:C, 0:256], func=AF.Exp), reads=[pk], writes=[ek_])
                                ql_, kl_, kh_, am_ = ql[x], kl[x], kh[x], am[x]
                                k.op("dve", lambda e, q_=q_, ql_=ql_, ep_=ep_, c0=c0: e.tensor_tensor(out=ql_[:], in0=q_[:, :, c0:c0 + C], in1=ep_[:], op=ALU.mult),
                                     reads=[q_, ep_], writes=[ql_])
                                k.op("pool", lambda e, k_=k_, kl_=kl_, en_=en_, c0=c0: e.tensor_tensor(out=kl_[:], in0=k_[:, :, c0:c0 + C], in1=en_[:], op=ALU.mult),
                                     reads=[k_, en_], writes=[kl_])
                                k.op("pool", lambda e, kt_=kt_, kh_=kh_, ek_=ek_, ch=ch: e.tensor_tensor(out=kh_[:], in0=kt_[:, ch, :], in1=ek_[:], op=ALU.mult),
                                     reads=[kt_, ek_], writes=[kh_])
                                pa = ps_next()
                                for db in range(2):
                                    k.op("pe", lambda e, pa=pa, db=db, kl_=kl_, ql_=ql_: e.matmul(pa[0:C, 0:C], lhsT=kl_[:, db, :], rhs=ql_[:, db, :],
                                                                                               start=(db == 0), stop=(db == 1)),
                                         reads=[kl_, ql_], writes=[pa], inc=(db == 1))
                                k.op("dve", lambda e, pa=pa, am_=am_: e.tensor_tensor(out=am_[:], in0=pa[0:C, 0:C], in1=Lmask, op=ALU.mult),
                                     reads=[pa, c_trib], writes=[am_])
                                po = ps_next()
                                for eb in range(4):
                                    for db in range(2):
                                        k.op("pe", lambda e, po=po, eb=eb, db=db, ql_=ql_: e.matmul(
                                            po[:, eb * C:(eb + 1) * C], lhsT=sbf[:, db, eb * 128:(eb + 1) * 128], rhs=ql_[:, db, :],
                                            start=(db == 0), stop=False), reads=[sbf, ql_], writes=[po], inc=False)
                                    k.op("pe", lambda e, po=po, eb=eb, vt_=vt_, am_=am_, ch=ch: e.matmul(
                                        po[:, eb * C:(eb + 1) * C], lhsT=vt_[:, ch, eb * 128:(eb + 1) * 128], rhs=am_[:],
                                        start=False, stop=True), reads=[vt_, am_], writes=[po], inc=True)
                                k.op("act", lambda e, po=po, o_=o_, c0=c0: e.copy(out=o_[:, :, c0:c0 + C], in_=po[:, 0:4 * C].rearrange("p (a c) -> p a c", c=C)), reads=[po], writes=[o_])
                                bl = (C - 1) if d == 0 else 0
                                for db in range(2):
                                    pd = ps_next()
                                    k.op("pe", lambda e, pd=pd, db=db, kh_=kh_, vt_=vt_, ch=ch: e.matmul(
                                        pd[:], lhsT=kh_[:, db * 128:(db + 1) * 128], rhs=vt_[:, ch, :], start=True, stop=True),
                                        reads=[kh_, vt_], writes=[pd], inc=True)
                                    k.op("dve", lambda e, pd=pd, db=db, ep_=ep_, bl=bl: e.scalar_tensor_tensor(
                                        out=s32[:, db, :], in0=s32[:, db, :], scalar=ep_[:, db, bl:bl + 1], in1=pd[:], op0=ALU.mult, op1=ALU.add),
                                        reads=[s32, ep_, pd], writes=[s32])
                                k.op("act", lambda e: e.copy(out=sbf[:], in_=s32[:]), reads=[s32], writes=[sbf])
                            k.dma("pool", OG[d, hh * 512:(hh + 1) * 512, t0:t0 + SC].rearrange("(c p) s -> p c s", p=128), o_[:], o_, reads=[o_])
            k.barrier()
            with contextlib.ExitStack() as st:
                a = [k.sb(st, "op_a", [128, 16, 512], F32) for _ in range(1)]
                b = [k.sb(st, "op_b", [128, 16, 512], F32) for _ in range(1)]
                r = [k.sb(st, "op_r", [128, 16, 512], BF16) for _ in range(1)]
                sq = k.sb(st, "op_sq", [128, 16, 512], BF16)
                rs = k.sb(st, "op_rs", [128, 512], F32)
                tmp = k.sb(st, "op_t", [128, 512], F32)
                o = [k.sb(st, "op_o", [128, 16, 512], BF16) for _ in range(2)]
                av = OG[0].rearrange("(c p) s -> p c s", p=128)
                bv = OG[1].rearrange("(c p) s -> p c s", p=128)
                rv = RS.rearrange("(c p) s -> p c s", p=128)
                ov = CAT.rearrange("(c p) s -> p c s", p=128)
                for ti in range(NT):
                    t0 = ti * 512
                    a_, b_, r_, o_ = a[0], b[0], r[0], o[ti % 2]
                    k.dma("sp", a_[:], av[:, :, t0:t0 + 512], a_, writes=[a_])
                    k.dma("sp", b_[:], bv[:, :, t0:t0 + 512], b_, writes=[b_])
                    k.dma("sp", r_[:], rv[:, :, t0:t0 + 512], r_, writes=[r_])
                    k.op("pool", lambda e, a_=a_, b_=b_: e.tensor_tensor(out=a_[:], in0=a_[:], in1=b_[:], op=ALU.add), reads=[a_, b_], writes=[a_])
                    k.op("act", lambda e, a_=a_: e.activation(out=sq[:], in_=a_[:], func=AF.Square), reads=[a_], writes=[sq])
                    for hh in range(4):
                        ps = ps_next()
                        for c in range(4):
                            k.op("pe", lambda e, ps=ps, c=c, hh=hh: e.matmul(ps[:], lhsT=c_ones[:], rhs=sq[:, hh * 4 + c, :], start=(c == 0), stop=(c == 3)),
                                 reads=[c_ones, sq], writes=[ps], inc=(c == 3))
                        rstd_from(ps, 512, 512, rs, tmp)
                        for c in range(4):
                            cc_ = hh * 4 + c
                            k.op("dve", lambda e, a_=a_, cc_=cc_, c=c: e.scalar_tensor_tensor(
                                out=a_[:, cc_, :], in0=a_[:, cc_, :], scalar=c_gng[:, i * 4 + c:i * 4 + c + 1], in1=rs[:], op0=ALU.mult, op1=ALU.mult),
                                reads=[a_, rs, c_gng], writes=[a_])
                    k.op("pool", lambda e, a_=a_, r_=r_, o_=o_: e.tensor_tensor(out=o_[:], in0=a_[:], in1=r_[:], op=ALU.mult), reads=[a_, r_], writes=[o_])
                    k.dma("pool", ov[:, :, t0:t0 + 512], o_[:], o_, reads=[o_])
            k.barrier()
            with contextlib.ExitStack() as st:
                epi = make_store_epi(st, lambda tag, t0, n: Y[tag * 128:(tag + 1) * 128, t0:t0 + n], F32, TB)
                linear_fm(CAT, D, b_out_odd[i], [(j * 128, j) for j in range(16)], epi)
            k.barrier()

        norm_pass(None, None, None, 0, 0, Xsrc=xT)
        for l in range(DEPTH):
            if l % 2 == 0:
                even_mixer(l)
            else:
                odd_mixer(l)
            norm_pass(l, 1, Y, l, 2)
            cross_attn(l)
            norm_pass(l, 3, Y, l, 5)
            ffn(l)
            if l + 1 < DEPTH:
                norm_pass(l, 6, Y, l + 1, 0)
            else:
                norm_pass(l, 6, Y, None, None)
        k.stopped = False
        k.barrier(final=True)
    return nc


def _t5_bucket(rel):
    try:
        import jax
        import jax.numpy as jnp
        with jax.default_device(jax.devices("cpu")[0]):
            r = jnp.asarray(np.asarray(rel, np.int32))
            n, max_exact = 16, 8
            base = jnp.where(r > 0, n, 0)
            a = jnp.abs(r)
            af = jnp.maximum(a, 1).astype(jnp.float32)
            large = max_exact + (jnp.log(af / max_exact) / math.log(128 / max_exact) * (n - max_exact)).astype(jnp.int32)
            large = jnp.minimum(large, n - 1)
            return np.asarray(base + jnp.where(a < max_exact, a, large))
    except Exception:
        pass
    n = 16
    max_exact = 8
    base = np.where(rel > 0, n, 0)
    a = np.abs(rel)
    af = np.maximum(a, 1).astype(np.float32)
    large = max_exact + (np.log(af / np.float32(max_exact)) / np.float32(math.log(128 / max_exact))
                         * np.float32(n - max_exact)).astype(np.int32)
    large = np.minimum(large, n - 1)
    return base + np.where(a < max_exact, a, large)


def host_constants(S, S_real):
    NKB = S // 128
    maskb = np.zeros((128, S), np.float32)
    maskb[:, :S_real] = 1.0
    kmask = np.zeros((128, NKB), np.float32)
    pos = np.arange(S).reshape(NKB, 128).T
    kmask[pos >= S_real] = -30000.0
    idx = np.arange(S_real, dtype=np.int64)
    ang = 2.0 * np.pi * ((idx[:, None] * idx[None, :]) % S_real).astype(np.float64) / S_real
    cs = np.zeros((S, S), np.float32)
    sn = np.zeros((S, S), np.float32)
    cs[:S_real, :S_real] = np.cos(ang) / math.sqrt(S_real)
    sn[:S_real, :S_real] = np.sin(ang) / math.sqrt(S_real)
    ic = np.arange(256)
    angc = 2.0 * np.pi * ((ic[:, None] * ic[None, :]) % 256) / 256.0
    cc = (np.cos(angc) / 16.0).astype(np.float32)
    nsc = (-np.sin(angc) / 16.0).astype(np.float32)
    kk = np.arange(128)[:, None]
    qq = np.arange(512)[None, :]
    bk = np.concatenate([_t5_bucket((128 * Dd + kk - qq).astype(np.int32)) for Dd in range(-1, 5)], axis=1).astype(np.float32)
    s_ = np.arange(64)[:, None]
    t_ = np.arange(64)[None, :]
    tri = np.concatenate([(s_ <= t_), (s_ >= t_), (s_ > t_), (s_ < t_)], axis=1).astype(np.float32)
    return dict(maskb=maskb, kmask=kmask, cosS=cs.astype(NPBF), sinS=sn.astype(NPBF), cosC=cc.astype(NPBF),
                nsinC=nsc.astype(NPBF), bkt=bk, trimats=tri)


def host_weights(inp, DEPTH):
    NEVEN = (DEPTH + 1) // 2
    NODD = DEPTH // 2
    f = lambda a: np.ascontiguousarray(np.asarray(a, dtype=np.float32))
    out = {}
    out["ng"] = f(np.asarray(inp["norm_g"])[:DEPTH].reshape(DEPTH, 7, 16, 128).transpose(3, 0, 1, 2).reshape(128, DEPTH * 7 * 16))
    out["relb"] = f(np.asarray(inp["rel_bias"]).reshape(1, 256))
    out["lamp"] = f(np.asarray(inp["diff_lambda"])[:max(NEVEN, 1)].reshape(1, -1))
    out["dng"] = f(np.asarray(inp["diff_norm_g"])[:max(NEVEN, 1)].T)
    out["gng"] = f(np.asarray(inp["gla_norm_g"])[:max(NODD, 1)].reshape(max(NODD, 1), 4, 128).transpose(2, 0, 1).reshape(128, -1))
    out["convw"] = f(np.asarray(inp["conv_w"])[:DEPTH].reshape(DEPTH, 3, 88, 128).transpose(3, 0, 1, 2).reshape(128, -1))
    out["convb"] = f(np.asarray(inp["conv_b"])[:DEPTH].reshape(DEPTH, 88, 128).transpose(2, 0, 1).reshape(128, -1))
    out["w_in_even"] = f(np.asarray(inp["w_in_even"])[:max(NEVEN, 1)])
    out["w_out_even"] = f(np.asarray(inp["w_out_even"])[:max(NEVEN, 1)])
    out["w_in_odd"] = f(np.asarray(inp["w_in_odd"])[:max(NODD, 1)])
    gd = np.asarray(inp["gla_gate_down"])[:max(NODD, 1)]
    out["w_gd"] = f(gd.transpose(0, 2, 1, 3).reshape(gd.shape[0], D, 32))
    gu = np.asarray(inp["gla_gate_up"])[:max(NODD, 1)]
    gb = np.asarray(inp["gla_gate_bias"])[:max(NODD, 1)]
    aug = np.zeros((gu.shape[0], 2, 32, 1024), np.float32)
    aug[:, :, 0:16, :] = gu
    aug[:, :, 16, :] = gb
    out["w_gu"] = f(aug.reshape(-1, 1024))
    out["w_out_odd"] = f(np.asarray(inp["w_out_odd"])[:max(NODD, 1)])
    for nm in ("w_xq", "w_xkv", "w_xo", "w_up", "w_down"):
        out[nm] = f(np.asarray(inp[nm])[:DEPTH])
    return out


_CACHE = {}


def run_trunk(seqs, mems, inp, S, DEPTH, taps=()):
    key = (S, DEPTH, tuple(taps))
    if key not in _CACHE:
        _CACHE[key] = build(S, DEPTH, taps)
    nc = _CACHE[key]
    wts = host_weights(inp, DEPTH)
    consts = {}
    in_maps = []
    for x, m in zip(seqs, mems):
        sr = x.shape[0]
        if sr not in consts:
            consts[sr] = host_constants(S, sr)
        xT = np.zeros((D, S), np.float32)
        xT[:, :sr] = np.asarray(x, np.float32).T
        d = dict(wts)
        d.update(consts[sr])
        d["xT"] = xT
        d["memT"] = np.ascontiguousarray(np.asarray(m, np.float32).T)
        in_maps.append(d)
    res = run_bass_kernel_spmd(nc, in_maps, core_ids=list(range(len(in_maps))))
    return res


def kernel(**inp):
    xp = np.asarray(inp["x_prompt"])
    xs = np.asarray(inp["x_sample"])
    mp = np.asarray(inp["mem_prompt"])
    ms = np.asarray(inp["mem_sample"])
    S = 8192
    seqs = [xp[0], xp[1], xp[2], xp[3], xs[0], xs[0], xs[0], xs[0]]
    mems = [mp[0], mp[1], mp[2], mp[3], ms[0], ms[0], ms[0], ms[0]]
    res = run_trunk(seqs, mems, inp, S, 4)
    yp = np.stack([np.ascontiguousarray(res.results[c]["yT"][:, :2048].T) for c in range(4)], axis=0).astype(np.float32)
    ys = np.ascontiguousarray(res.results[4]["yT"].T)[None].astype(np.float32)
    return (yp, ys)
```

```python
import contextlib
import math
import numpy as np
import ml_dtypes
import concourse.bass as bass
import concourse.mybir as mybir
from concourse.bass_utils import run_bass_kernel_spmd

F32 = mybir.dt.float32
BF16 = mybir.dt.bfloat16
AF = mybir.ActivationFunctionType
ALU = mybir.AluOpType
NPBF = ml_dtypes.bfloat16

D = 2048
NMEM = 256
DFF = 5632
EPS = 1e-6


class Buf:
    __slots__ = ("t", "w", "r", "dkey", "name")

    def __init__(self, t, name):
        self.t = t
        self.w = None
        self.r = {}
        self.dkey = None
        self.name = name

    def __getitem__(self, idx):
        return self.t[idx]


class StopBuild(Exception):
    pass


class K:
    stop_at = None
    nbar = 0
    stopped = False

    def __init__(self, nc, stack, n_dsem=40):
        self.nc = nc
        self.h = dict(pe=nc.tensor, act=nc.scalar, dve=nc.vector, pool=nc.gpsimd, sp=nc.sync)
        self.sem = {}
        self.latest = {}
        self.seen = {e: {} for e in self.h}
        self.cnt = {e: 0 for e in self.h}
        self.bufs = []
        self.uid = 0
        for e in ("pe", "act", "dve", "pool"):
            self.sem[e] = stack.enter_context(nc.semaphore("s_" + e))
            self.latest[e] = 0
        self.dkeys = []
        for i in range(n_dsem):
            k = "d%d" % i
            self.sem[k] = stack.enter_context(nc.semaphore("s_" + k))
            self.latest[k] = 0
            self.dkeys.append(k)
        self.dset = set(self.dkeys)
        self.dnext = 0
        self.pe_pending = False

    def sb(self, stack, name, shape, dt):
        self.uid += 1
        t = stack.enter_context(self.nc.sbuf_tensor("%s_%d" % (name, self.uid), list(shape), dt))
        b = Buf(t, name)
        self.bufs.append(b)
        return b

    def wrap(self, t, name):
        b = Buf(t, name)
        self.bufs.append(b)
        return b

    def _dkey(self, b):
        if b.dkey is None:
            assert self.dnext < len(self.dkeys), "out of dma semaphores"
            b.dkey = self.dkeys[self.dnext]
            self.dnext += 1
        return b.dkey

    def _add(self, need, tok, eng, raw):
        k, v = tok
        if k == eng and not raw:
            return
        if k in self.dset:
            v = self.latest[k]
        if need.get(k, 0) < v:
            need[k] = v

    def _emit_waits(self, eng, reads, writes):
        need = {}
        for b in reads:
            if b.w is not None:
                self._add(need, b.w, eng, True)
        for b in writes:
            if b.w is not None:
                self._add(need, b.w, eng, False)
            for k, v in b.r.items():
                self._add(need, (k, v), eng, False)
        h = self.h[eng]
        seen = self.seen[eng]
        for k, v in need.items():
            if seen.get(k, 0) < v:
                h.wait_ge(self.sem[k], v)
                seen[k] = v

    def _record(self, tok, reads, writes):
        k, v = tok
        for b in reads:
            if b.r.get(k, 0) < v:
                b.r[k] = v
        for b in writes:
            b.w = tok
            b.r = {}

    def op(self, eng, fn, reads=(), writes=(), inc=True):
        if self.stopped:
            return
        self._emit_waits(eng, reads, writes)
        ins = fn(self.h[eng])
        if inc:
            self.cnt[eng] += 1
            ins.then_inc(self.sem[eng], 1)
            self.latest[eng] = self.cnt[eng]
            tok = (eng, self.cnt[eng])
            if eng == "pe":
                self.pe_pending = False
        else:
            assert eng == "pe"
            tok = (eng, self.cnt[eng] + 1)
            self.pe_pending = True
        self._record(tok, reads, writes)

    def dma(self, q, out_ap, in_ap, sb, reads=(), writes=()):
        if self.stopped:
            return
        key = self._dkey(sb)
        self._emit_waits(q, reads, writes)
        ins = self.h[q].dma_start(out=out_ap, in_=in_ap)
        self.latest[key] += 16
        ins.then_inc(self.sem[key], 16)
        self._record((key, self.latest[key]), reads, writes)

    def barrier(self, final=False):
        if self.stopped and not final:
            return
        assert not self.pe_pending
        self.nbar += 1
        for e, h in self.h.items():
            seen = self.seen[e]
            for k, v in self.latest.items():
                if v > 0 and seen.get(k, 0) < v:
                    h.wait_ge(self.sem[k], v)
                    seen[k] = v
        for b in self.bufs:
            b.w = None
            b.r = {}
            b.dkey = None
        self.bufs = [b for b in self.bufs if b.name.startswith("ps") or b.name.startswith("c_")]
        self.dnext = 0
        if (not final) and self.stop_at is not None and self.nbar >= self.stop_at:
            self.stopped = True


def build(S, DEPTH, taps=()):
    nc = bass.Bass("TRN2", target_bir_lowering=False)
    NT = S // 512
    TB = min(2048, S)
    NKB = S // 128
    NEVEN = (DEPTH + 1) // 2
    NODD = DEPTH // 2

    def din(name, shape, dt=F32):
        return nc.dram_tensor(name, list(shape), dt, kind="ExternalInput").ap()

    def dscr(name, shape, dt):
        kind = "ExternalOutput" if name in taps else "Internal"
        return nc.dram_tensor(name, list(shape), dt, kind=kind).ap()

    xT = din("xT", [D, S])
    memT = din("memT", [D, NMEM])
    maskb = din("maskb", [128, S])
    kmask = din("kmask", [128, NKB])
    cosS = din("cosS", [S, S], BF16)
    sinS = din("sinS", [S, S], BF16)
    cosC = din("cosC", [256, 256], BF16)
    nsinC = din("nsinC", [256, 256], BF16)
    bkt = din("bkt", [128, 6 * 512])
    trimats = din("trimats", [64, 4 * 64])
    ng = din("ng", [128, DEPTH * 7 * 16])
    relb = din("relb", [1, 256])
    lamp = din("lamp", [1, max(NEVEN, 1) * 256])
    dng = din("dng", [128, max(NEVEN, 1)])
    gng = din("gng", [128, max(NODD, 1) * 4])
    convw = din("convw", [128, DEPTH * 3 * 88])
    convb = din("convb", [128, DEPTH * 88])
    w_in_even = din("w_in_even", [max(NEVEN, 1), D, 4096])
    w_out_even = din("w_out_even", [max(NEVEN, 1), D, D])
    w_in_odd = din("w_in_odd", [max(NODD, 1), D, 6144])
    w_gd = din("w_gd", [max(NODD, 1), D, 32])
    w_gu = din("w_gu", [max(NODD, 1) * 2 * 32, 1024])
    w_out_odd = din("w_out_odd", [max(NODD, 1), D, D])
    w_xq = din("w_xq", [DEPTH, D, D])
    w_xkv = din("w_xkv", [DEPTH, D, 2 * D])
    w_xo = din("w_xo", [DEPTH, D, D])
    w_up = din("w_up", [DEPTH, D, 2 * DFF])
    w_down = din("w_down", [DEPTH, DFF, D])

    yT = nc.dram_tensor("yT", [D, S], F32, kind="ExternalOutput").ap()

    b_in_even = dscr("b_in_even", [max(NEVEN, 1), D, 4096], BF16)
    b_out_even = dscr("b_out_even", [max(NEVEN, 1), D, D], BF16)
    b_in_odd = dscr("b_in_odd", [max(NODD, 1), D, 6144], BF16)
    b_gd = dscr("b_gd", [max(NODD, 1), D, 32], BF16)
    b_gu = dscr("b_gu", [max(NODD, 1) * 2 * 32, 1024], BF16)
    b_out_odd = dscr("b_out_odd", [max(NODD, 1), D, D], BF16)
    b_xq = dscr("b_xq", [DEPTH, D, D], BF16)
    b_xkv = dscr("b_xkv", [DEPTH, D, 2 * D], BF16)
    b_xo = dscr("b_xo", [DEPTH, D, D], BF16)
    b_up = dscr("b_up", [DEPTH, D, 2 * DFF], BF16)
    b_down = dscr("b_down", [DEPTH, DFF, D], BF16)

    H = dscr("H", [D, S], BF16)
    Y = dscr("Y", [D, S], F32)
    QT = dscr("QT", [1024, S], BF16)
    KT = dscr("KT", [1024, S], BF16)
    Vt = dscr("Vt", [S, 1024], BF16)
    Ut = dscr("Ut", [S, 1024], BF16)
    CAT = dscr("CAT", [D, S], BF16)
    BT = dscr("BT", [8, 128, 6 * 512], F32)
    QgT = dscr("QgT", [1024, S], BF16)
    KgT = dscr("KgT", [1024, S], BF16)
    Kg = dscr("Kg", [S, 1024], BF16)
    Vg = dscr("Vg", [S, 2048], BF16)
    GH = dscr("GH", [2, S, 1024], BF16)
    GL = dscr("GL", [2, S, 1024], BF16)
    HDT = dscr("HDT", [32, S], BF16)
    RS = dscr("RS", [D, S], BF16)
    OG = dscr("OG", [2, D, S], F32)
    KxT = dscr("KxT", [D, NMEM], BF16)
    Vx = dscr("Vx", [NMEM, D], BF16)
    MN = dscr("MN", [D, NMEM], BF16)
    QxT = dscr("QxT", [D, S], BF16)
    OxT = dscr("OxT", [D, S], BF16)
    GG = dscr("GG", [DFF, S], BF16)

    with contextlib.ExitStack() as top:
        k = K(nc, top)
        PS = []
        for i in range(8):
            t = top.enter_context(nc.psum_tensor("psb%d" % i, [128, 512], F32))
            PS.append(k.wrap(t, "ps%d" % i))
        psrr = [0]

        def ps_next():
            p = PS[psrr[0] % 8]
            psrr[0] += 1
            return p

        prr = [0]

        def ps_pick(lo, hi):
            p = PS[lo + prr[0] % (hi - lo)]
            prr[0] += 1
            return p

        c_ones = k.sb(top, "c_ones", [128, 128], BF16)
        c_ng = k.sb(top, "c_ng", [128, DEPTH * 7 * 16], F32)
        c_tri = k.sb(top, "c_tri", [64, 256], F32)
        c_trib = k.sb(top, "c_trib", [64, 256], BF16)
        c_relb = k.sb(top, "c_relb", [128, 256], F32)
        c_kmask = k.sb(top, "c_kmask", [128, NKB], F32)
        c_far = k.sb(top, "c_far", [128, 16 * NKB], F32)
        c_lam = k.sb(top, "c_lam", [128, max(NEVEN, 1)], F32)
        c_dng = k.sb(top, "c_dng", [128, max(NEVEN, 1)], F32)
        c_gng = k.sb(top, "c_gng", [128, max(NODD, 1) * 4], F32)
        c_cw = k.sb(top, "c_cw", [128, DEPTH * 3 * 88], F32)
        c_cb = k.sb(top, "c_cb", [128, DEPTH * 88], F32)
        c_tmp = k.sb(top, "c_tmp", [128, 256], F32)
        c_tmp2 = k.sb(top, "c_tmp2", [128, 4], F32)

        k.op("dve", lambda e: e.memset(c_ones[:], 1.0), writes=[c_ones])
        k.dma("sp", c_ng[:], ng, c_ng, writes=[c_ng])
        k.dma("sp", c_tri[:], trimats, c_tri, writes=[c_tri])
        k.dma("sp", c_relb[:], relb.partition_broadcast(128), c_relb, writes=[c_relb])
        k.dma("sp", c_kmask[:], kmask, c_kmask, writes=[c_kmask])
        k.dma("sp", c_dng[:], dng, c_dng, writes=[c_dng])
        k.dma("sp", c_gng[:], gng, c_gng, writes=[c_gng])
        k.dma("sp", c_cw[:], convw, c_cw, writes=[c_cw])
        k.dma("sp", c_cb[:], convb, c_cb, writes=[c_cb])
        k.op("dve", lambda e: e.tensor_copy(out=c_trib[:], in_=c_tri[:]), reads=[c_tri], writes=[c_trib])
        for hh in range(8):
            for side in range(2):
                col = (15 + 16 * side) * 8 + hh
                o0 = (hh * 2 + side) * NKB
                k.op("dve", lambda e, o0=o0, col=col: e.tensor_scalar(
                    out=c_far[:, o0:o0 + NKB], in0=c_kmask[:], scalar1=c_relb[:, col:col + 1], scalar2=None,
                    op0=ALU.add), reads=[c_kmask, c_relb], writes=[c_far])
        for i in range(NEVEN):
            lam_init = 0.8 - 0.6 * math.exp(-0.3 * (2 * i))
            k.dma("sp", c_tmp[:], lamp[:, i * 256:(i + 1) * 256].partition_broadcast(128), c_tmp, writes=[c_tmp])
            k.op("dve", lambda e: e.tensor_tensor(out=c_tmp[:, 0:64], in0=c_tmp[:, 0:64], in1=c_tmp[:, 64:128],
                                                  op=ALU.mult), reads=[c_tmp], writes=[c_tmp])
            k.op("dve", lambda e: e.tensor_tensor(out=c_tmp[:, 128:192], in0=c_tmp[:, 128:192], in1=c_tmp[:, 192:256],
                                                  op=ALU.mult), reads=[c_tmp], writes=[c_tmp])
            k.op("dve", lambda e: e.reduce_sum(out=c_tmp2[:, 0:1], in_=c_tmp[:, 0:64], axis=mybir.AxisListType.X),
                 reads=[c_tmp], writes=[c_tmp2])
            k.op("dve", lambda e: e.reduce_sum(out=c_tmp2[:, 1:2], in_=c_tmp[:, 128:192], axis=mybir.AxisListType.X),
                 reads=[c_tmp], writes=[c_tmp2])
            k.op("act", lambda e: e.activation(out=c_tmp2[:, 2:4], in_=c_tmp2[:, 0:2], func=AF.Exp),
                 reads=[c_tmp2], writes=[c_tmp2])
            k.op("dve", lambda e, i=i, lam_init=lam_init: e.scalar_tensor_tensor(
                out=c_lam[:, i:i + 1], in0=c_tmp2[:, 3:4], scalar=-lam_init, in1=c_tmp2[:, 2:3],
                op0=ALU.add, op1=ALU.subtract), reads=[c_tmp2], writes=[c_lam])

        def ngcol(l, j, c):
            o = (l * 7 + j) * 16 + c
            return c_ng[:, o:o + 1]

        def cast_weight(src, dst, Kr, Nc, rr):
            with contextlib.ExitStack() as st:
                CW = min(2048, Nc)
                fin = [k.sb(st, "cwf", [128, CW], F32) for _ in range(3)]
                fo = [k.sb(st, "cwb", [128, CW], BF16) for _ in range(3)]
                i = 0
                for r0 in range(0, Kr, 128):
                    rows = min(128, Kr - r0)
                    for c0 in range(0, Nc, CW):
                        cw = min(CW, Nc - c0)
                        a, b = fin[i % 3], fo[i % 3]
                        k.dma("sp", a[0:rows, 0:cw], src[r0:r0 + rows, c0:c0 + cw], a, writes=[a])
                        eng = ("pool", "dve", "act")[rr[0] % 3]
                        rr[0] += 1
                        if eng == "act":
                            k.op("act", lambda e, a=a, b=b, rows=rows, cw=cw: e.copy(out=b[0:rows, 0:cw], in_=a[0:rows, 0:cw]),
                                 reads=[a], writes=[b])
                        else:
                            k.op(eng, lambda e, a=a, b=b, rows=rows, cw=cw: e.tensor_copy(out=b[0:rows, 0:cw], in_=a[0:rows, 0:cw]),
                                 reads=[a], writes=[b])
                        k.dma("pool", dst[r0:r0 + rows, c0:c0 + cw], b[0:rows, 0:cw], b, reads=[b])
                        i += 1
            k.barrier()

        rr = [0]
        for i in range(NEVEN):
            cast_weight(w_in_even[i], b_in_even[i], D, 4096, rr)
            cast_weight(w_out_even[i], b_out_even[i], D, D, rr)
        for i in range(NODD):
            cast_weight(w_in_odd[i], b_in_odd[i], D, 6144, rr)
            cast_weight(w_gd[i], b_gd[i], D, 32, rr)
            cast_weight(w_out_odd[i], b_out_odd[i], D, D, rr)
        if NODD:
            cast_weight(w_gu, b_gu, NODD * 64, 1024, rr)
        for l in range(DEPTH):
            cast_weight(w_xq[l], b_xq[l], D, D, rr)
            cast_weight(w_xkv[l], b_xkv[l], D, 2 * D, rr)
            cast_weight(w_xo[l], b_xo[l], D, D, rr)
            cast_weight(w_up[l], b_up[l], D, 2 * DFF, rr)
            cast_weight(w_down[l], b_down[l], DFF, D, rr)

        if NEVEN:
            with contextlib.ExitStack() as st:
                bk = k.sb(st, "bk", [128, 3072], F32)
                acc = k.sb(st, "bacc", [128, 3072], F32)
                tmpb = [k.sb(st, "btmp", [128, 3072], F32) for _ in range(2)]
                k.dma("sp", bk[:], bkt, bk, writes=[bk])
                for hh in range(8):
                    for b in range(32):
                        col = b * 8 + hh
                        t = tmpb[b % 2]
                        if b == 0:
                            k.op("dve", lambda e, col=col, b=b: e.tensor_scalar(
                                out=acc[:], in0=bk[:], scalar1=float(b), scalar2=c_relb[:, col:col + 1],
                                op0=ALU.is_equal, op1=ALU.mult), reads=[bk, c_relb], writes=[acc])
                        else:
                            k.op("dve", lambda e, col=col, b=b, t=t: e.tensor_scalar(
                                out=t[:], in0=bk[:], scalar1=float(b), scalar2=c_relb[:, col:col + 1],
                                op0=ALU.is_equal, op1=ALU.mult), reads=[bk, c_relb], writes=[t])
                            k.op("pool", lambda e, t=t: e.tensor_tensor(out=acc[:], in0=acc[:], in1=t[:], op=ALU.add),
                                 reads=[acc, t], writes=[acc])
                    k.dma("pool", BT[hh], acc[:], acc, reads=[acc])
            k.barrier()

        def linear_fm(inT, Kdim, W, col_list, epi, tb=None, gw=512, extra_tok=None):
            KC = Kdim // 128
            TBs = tb or TB
            nper = gw // 128
            with contextlib.ExitStack() as st:
                act = k.sb(st, "lin_act", [128, KC, TBs], BF16)
                wts = [k.sb(st, "lin_w", [128, KC, gw], BF16) for _ in range(2)]
                Wv = W.rearrange("(kc p) n -> p kc n", p=128)
                inv = inT.rearrange("(kc p) s -> p kc s", p=128)
                groups = []
                i = 0
                while i < len(col_list):
                    j = i
                    while j + 1 < len(col_list) and j + 1 - i < nper and col_list[j + 1][0] == col_list[j][0] + 128:
                        j += 1
                    groups.append(col_list[i:j + 1])
                    i = j + 1
                gi = 0
                for tb0 in range(0, S, TBs):
                    hk = KC // 2
                    k.dma("sp", act[:, 0:hk, :], inv[:, 0:hk, tb0:tb0 + TBs], act, writes=[act])
                    k.dma("sp", act[:, hk:KC, :], inv[:, hk:KC, tb0:tb0 + TBs], act, writes=[act])
                    for grp in groups:
                        wt = wts[gi % 2]
                        gi += 1
                        c0 = grp[0][0]
                        gwid = 128 * len(grp)
                        k.dma("sp", wt[:, :, 0:gwid], Wv[:, :, c0:c0 + gwid], wt, writes=[wt])
                        for bi, (cc, tag) in enumerate(grp):
                            ntile = TBs // 512
                            pss_ = [ps_next() for _ in range(ntile)]
                            for kc in range(KC):
                                for tt in range(ntile):
                                    ps = pss_[tt]
                                    k.op("pe", lambda e, ps=ps, wt=wt, kc=kc, bi=bi, tt=tt: e.matmul(
                                        ps[:], lhsT=wt[:, kc, bi * 128:(bi + 1) * 128], rhs=act[:, kc, tt * 512:(tt + 1) * 512],
                                        start=(kc == 0), stop=(kc == KC - 1)),
                                        reads=[wt, act], writes=[ps], inc=(kc == KC - 1))
                            for tt in range(ntile):
                                epi(pss_[tt], tag, tb0 + tt * 512, 512, tt)
                            epi(None, tag, tb0, TBs, -1)

        def linear_tm(inT, Kdim, W, c0, ncols, epi, gw=512):
            KC = Kdim // 128
            with contextlib.ExitStack() as st:
                act = k.sb(st, "ltm_act", [128, KC, TB], BF16)
                wts = [k.sb(st, "ltm_w", [128, KC, gw], BF16) for _ in range(2)]
                Wv = W.rearrange("(kc p) n -> p kc n", p=128)
                inv = inT.rearrange("(kc p) s -> p kc s", p=128)
                gi = 0
                for tb0 in range(0, S, TB):
                    k.dma("sp", act[:], inv[:, :, tb0:tb0 + TB], act, writes=[act])
                    for g0 in range(0, ncols, gw):
                        gwid = min(gw, ncols - g0)
                        wt = wts[gi % 2]
                        gi += 1
                        k.dma("sp", wt[:, :, 0:gwid], Wv[:, :, c0 + g0:c0 + g0 + gwid], wt, writes=[wt])
                        for tk in range(TB // 128):
                            ps = ps_next()
                            for kc in range(KC):
                                k.op("pe", lambda e, ps=ps, wt=wt, kc=kc, tk=tk, gwid=gwid: e.matmul(
                                    ps[:, 0:gwid], lhsT=act[:, kc, tk * 128:(tk + 1) * 128], rhs=wt[:, kc, 0:gwid],
                                    start=(kc == 0), stop=(kc == KC - 1)),
                                    reads=[wt, act], writes=[ps], inc=(kc == KC - 1))
                            epi(ps, g0, gwid, tb0 + tk * 128)
            k.barrier()

        def rstd_from(ps, n, dim, out_buf, tmp_buf):
            k.op("dve", lambda e: e.tensor_scalar(out=tmp_buf[:, 0:n], in0=ps[:, 0:n], scalar1=1.0 / dim, scalar2=EPS,
                                                  op0=ALU.mult, op1=ALU.add), reads=[ps], writes=[tmp_buf])
            k.op("act", lambda e: e.activation(out=tmp_buf[:, 0:n], in_=tmp_buf[:, 0:n], func=AF.Ln),
                 reads=[tmp_buf], writes=[tmp_buf])
            k.op("act", lambda e: e.activation(out=out_buf[:, 0:n], in_=tmp_buf[:, 0:n], func=AF.Exp, scale=-0.5),
                 reads=[tmp_buf], writes=[out_buf])

        def norm_pass(l_post, j_post, Ysrc, l_pre, j_pre, Xsrc=None):
            Xs = Xsrc if Xsrc is not None else yT
            with contextlib.ExitStack() as st:
                xt = [k.sb(st, "np_x", [128, 16, 512], F32) for _ in range(2)]
                yt = [k.sb(st, "np_y", [128, 16, 512], F32) for _ in range(1)] if Ysrc is not None else None
                sq = k.sb(st, "np_sq", [128, 16, 512], BF16)
                ht = [k.sb(st, "np_h", [128, 16, 512], BF16) for _ in range(1)] if j_pre is not None else None
                mk = [k.sb(st, "np_m", [128, 512], F32) for _ in range(2)]
                r1 = k.sb(st, "np_r1", [128, 512], F32)
                r2 = k.sb(st, "np_r2", [128, 512], F32)
                tmp = k.sb(st, "np_t", [128, 512], F32)
                xv = Xs.rearrange("(c p) s -> p c s", p=128)
                xo = yT.rearrange("(c p) s -> p c s", p=128)
                hv = H.rearrange("(c p) s -> p c s", p=128)
                for ti in range(NT):
                    t0 = ti * 512
                    x = xt[ti % 2]
                    k.dma("sp", x[:], xv[:, :, t0:t0 + 512], x, writes=[x])
                    if Ysrc is not None:
                        y = yt[0]
                        m = mk[ti % 2]
                        yv = Ysrc.rearrange("(c p) s -> p c s", p=128)
                        k.dma("sp", y[:], yv[:, :, t0:t0 + 512], y, writes=[y])
                        k.dma("sp", m[:], maskb[:, t0:t0 + 512], m, writes=[m])
                        k.op("act", lambda e, y=y: e.activation(out=sq[:], in_=y[:], func=AF.Square), reads=[y], writes=[sq])
                        ps = ps_next()
                        for c in range(16):
                            k.op("pe", lambda e, ps=ps, c=c: e.matmul(ps[:], lhsT=c_ones[:], rhs=sq[:, c, :], start=(c == 0), stop=(c == 15)),
                                 reads=[c_ones, sq], writes=[ps], inc=(c == 15))
                        rstd_from(ps, 512, D, r1, tmp)
                        k.op("dve", lambda e, m=m: e.tensor_tensor(out=r1[:], in0=r1[:], in1=m[:], op=ALU.mult), reads=[r1, m], writes=[r1])
                        for c in range(16):
                            k.op("dve", lambda e, y=y, c=c: e.scalar_tensor_tensor(
                                out=y[:, c, :], in0=y[:, c, :], scalar=ngcol(l_post, j_post, c), in1=r1[:],
                                op0=ALU.mult, op1=ALU.mult), reads=[y, r1, c_ng], writes=[y])
                        k.op("pool", lambda e, x=x, y=y: e.tensor_tensor(out=x[:], in0=x[:], in1=y[:], op=ALU.add), reads=[x, y], writes=[x])
                    if Ysrc is not None or Xsrc is not None:
                        k.dma("pool", xo[:, :, t0:t0 + 512], x[:], x, reads=[x])
                    if j_pre is not None:
                        h = ht[0]
                        k.op("act", lambda e, x=x: e.activation(out=sq[:], in_=x[:], func=AF.Square), reads=[x], writes=[sq])
                        ps = ps_next()
                        for c in range(16):
                            k.op("pe", lambda e, ps=ps, c=c: e.matmul(ps[:], lhsT=c_ones[:], rhs=sq[:, c, :], start=(c == 0), stop=(c == 15)),
                                 reads=[c_ones, sq], writes=[ps], inc=(c == 15))
                        rstd_from(ps, 512, D, r2, tmp)
                        for c in range(16):
                            eng = "dve" if c % 2 == 0 else "dve"
                            k.op(eng, lambda e, x=x, h=h, c=c: e.scalar_tensor_tensor(
                                out=h[:, c, :], in0=x[:, c, :], scalar=ngcol(l_pre, j_pre, c), in1=r2[:],
                                op0=ALU.mult, op1=ALU.mult), reads=[x, r2, c_ng], writes=[h])
                        k.dma("pool", hv[:, :, t0:t0 + 512], h[:], h, reads=[h])
            k.barrier()

        def make_store_epi(st, dst_fn, dt, TBs, scale=None, func=None):
            stg = [k.sb(st, "epi_stg", [128, TBs], dt) for _ in range(2)]
            state = {"i": 0, "rr": 0}

            def epi(ps, tag, tok0, ntok, tt):
                sg = stg[state["i"] % 2]
                if ps is not None:
                    o0 = tt * 512
                    use_act = (func is not None) or (state["rr"] % 2 == 0)
                    state["rr"] += 1
                    if use_act:
                        f = func if func is not None else AF.Copy
                        if scale is not None:
                            k.op("act", lambda e: e.activation(out=sg[:, o0:o0 + 512], in_=ps[:], func=f, scale=scale), reads=[ps], writes=[sg])
                        else:
                            k.op("act", lambda e: e.activation(out=sg[:, o0:o0 + 512], in_=ps[:], func=f), reads=[ps], writes=[sg])
                    else:
                        if scale is not None:
                            k.op("dve", lambda e: e.tensor_scalar(out=sg[:, o0:o0 + 512], in0=ps[:], scalar1=scale, scalar2=None, op0=ALU.mult),
                                 reads=[ps], writes=[sg])
                        else:
                            k.op("dve", lambda e: e.tensor_copy(out=sg[:, o0:o0 + 512], in_=ps[:]), reads=[ps], writes=[sg])
                else:
                    k.dma("pool", dst_fn(tag, tok0, ntok), sg[:, 0:ntok], sg, reads=[sg])
                    state["i"] += 1
            return epi

        def cross_attn(l):
            with contextlib.ExitStack() as st:
                m32 = k.sb(st, "ca_m32", [128, 16, NMEM], F32)
                msq = k.sb(st, "ca_msq", [128, 16, NMEM], BF16)
                mn = k.sb(st, "ca_mn", [128, 16, NMEM], BF16)
                r = k.sb(st, "ca_r", [128, NMEM], F32)
                tmp = k.sb(st, "ca_t", [128, NMEM], F32)
                k.dma("sp", m32[:], memT.rearrange("(c p) m -> p c m", p=128), m32, writes=[m32])
                k.op("act", lambda e: e.activation(out=msq[:], in_=m32[:], func=AF.Square), reads=[m32], writes=[msq])
                ps = ps_next()
                for c in range(16):
                    k.op("pe", lambda e, c=c: e.matmul(ps[:, 0:NMEM], lhsT=c_ones[:], rhs=msq[:, c, :], start=(c == 0), stop=(c == 15)),
                         reads=[c_ones, msq], writes=[ps], inc=(c == 15))
                rstd_from(ps, NMEM, D, r, tmp)
                for c in range(16):
                    k.op("dve", lambda e, c=c: e.scalar_tensor_tensor(out=mn[:, c, :], in0=m32[:, c, :], scalar=ngcol(l, 4, c), in1=r[:],
                                                                      op0=ALU.mult, op1=ALU.mult), reads=[m32, r, c_ng], writes=[mn])
                k.dma("pool", MN.rearrange("(c p) m -> p c m", p=128), mn[:], mn, reads=[mn])
            k.barrier()
            with contextlib.ExitStack() as st:
                mn = k.sb(st, "ca_mn2", [128, 16, NMEM], BF16)
                wts = [k.sb(st, "ca_w", [128, 16, 512], BF16) for _ in range(2)]
                stg = [k.sb(st, "ca_s", [128, 512], BF16) for _ in range(2)]
                k.dma("sp", mn[:], MN.rearrange("(c p) m -> p c m", p=128), mn, writes=[mn])
                Wv = b_xkv[l].rearrange("(kc p) n -> p kc n", p=128)
                si = 0
                for g in range(8):
                    wt = wts[g % 2]
                    k.dma("sp", wt[:], Wv[:, :, g * 512:(g + 1) * 512], wt, writes=[wt])
                    if g < 4:
                        for bi in range(4):
                            ps = ps_next()
                            for kc in range(16):
                                k.op("pe", lambda e, ps=ps, wt=wt, kc=kc, bi=bi: e.matmul(
                                    ps[:, 0:NMEM], lhsT=wt[:, kc, bi * 128:(bi + 1) * 128], rhs=mn[:, kc, :], start=(kc == 0), stop=(kc == 15)),
                                    reads=[wt, mn], writes=[ps], inc=(kc == 15))
                            sg = stg[si % 2]
                            si += 1
                            k.op("act", lambda e, ps=ps, sg=sg: e.copy(out=sg[:, 0:NMEM], in_=ps[:, 0:NMEM]), reads=[ps], writes=[sg])
                            n0 = g * 512 + bi * 128
                            k.dma("pool", KxT[n0:n0 + 128, :], sg[:, 0:NMEM], sg, reads=[sg])
                    else:
                        for mb in range(2):
                            ps = ps_next()
                            for kc in range(16):
                                k.op("pe", lambda e, ps=ps, wt=wt, kc=kc, mb=mb: e.matmul(
                                    ps[:], lhsT=mn[:, kc, mb * 128:(mb + 1) * 128], rhs=wt[:, kc, :], start=(kc == 0), stop=(kc == 15)),
                                    reads=[wt, mn], writes=[ps], inc=(kc == 15))
                            sg = stg[si % 2]
                            si += 1
                            k.op("act", lambda e, ps=ps, sg=sg: e.copy(out=sg[:], in_=ps[:]), reads=[ps], writes=[sg])
                            e0 = (g - 4) * 512
                            k.dma("pool", Vx[mb * 128:(mb + 1) * 128, e0:e0 + 512], sg[:], sg, reads=[sg])
            k.barrier()
            with contextlib.ExitStack() as st:
                epi = make_store_epi(st, lambda tag, t0, n: QxT[tag * 128:(tag + 1) * 128, t0:t0 + n], BF16, TB, scale=512 ** -0.5)
                linear_fm(H, D, b_xq[l], [(i * 128, i) for i in range(16)], epi)
            k.barrier()
            with contextlib.ExitStack() as st:
                kx = k.sb(st, "ca_kx", [128, 16, NMEM], BF16)
                vx = k.sb(st, "ca_vx", [128, 2, D], BF16)
                qs = [k.sb(st, "ca_q", [128, 16, 512], BF16) for _ in range(2)]
                es = [k.sb(st, "ca_e", [128, 2, 512], BF16) for _ in range(2)]
                rz = [k.sb(st, "ca_rz", [128, 512], F32) for _ in range(2)]
                ot = [k.sb(st, "ca_o", [128, 16, 512], BF16) for _ in range(2)]
                k.dma("sp", kx[:], KxT.rearrange("(c p) m -> p c m", p=128), kx, writes=[kx])
                k.dma("sp", vx[:], Vx.rearrange("(mb p) e -> p mb e", p=128), vx, writes=[vx])
                qv = QxT.rearrange("(c p) s -> p c s", p=128)
                ov = OxT.rearrange("(c p) s -> p c s", p=128)
                ei = 0
                for ti in range(NT):
                    t0 = ti * 512
                    q = qs[ti % 2]
                    o = ot[ti % 2]
                    k.dma("sp", q[:], qv[:, :, t0:t0 + 512], q, writes=[q])
                    for hh in range(4):
                        ee = es[ei % 2]
                        rzz = rz[ei % 2]
                        ei += 1
                        for mb in range(2):
                            ps = ps_next()
                            for dc in range(4):
                                k.op("pe", lambda e, ps=ps, dc=dc, mb=mb, hh=hh: e.matmul(
                                    ps[:], lhsT=kx[:, hh * 4 + dc, mb * 128:(mb + 1) * 128], rhs=q[:, hh * 4 + dc, :],
                                    start=(dc == 0), stop=(dc == 3)), reads=[kx, q], writes=[ps], inc=(dc == 3))
                            k.op("act", lambda e, ps=ps, ee=ee, mb=mb: e.activation(out=ee[:, mb, :], in_=ps[:], func=AF.Exp),
                                 reads=[ps], writes=[ee])
                        pz = ps_next()
                        for mb in range(2):
                            k.op("pe", lambda e, pz=pz, ee=ee, mb=mb: e.matmul(pz[:], lhsT=c_ones[:], rhs=ee[:, mb, :], start=(mb == 0), stop=(mb == 1)),
                                 reads=[c_ones, ee], writes=[pz], inc=(mb == 1))
                        k.op("dve", lambda e, pz=pz, rzz=rzz: e.reciprocal(out=rzz[:], in_=pz[:]), reads=[pz], writes=[rzz])
                        for eb in range(4):
                            po = ps_next()
                            for mb in range(2):
                                k.op("pe", lambda e, po=po, ee=ee, mb=mb, eb=eb, hh=hh: e.matmul(
                                    po[:], lhsT=vx[:, mb, hh * 512 + eb * 128:hh * 512 + (eb + 1) * 128], rhs=ee[:, mb, :],
                                    start=(mb == 0), stop=(mb == 1)), reads=[vx, ee], writes=[po], inc=(mb == 1))
                            k.op("dve", lambda e, po=po, o=o, rzz=rzz, eb=eb, hh=hh: e.tensor_tensor(
                                out=o[:, hh * 4 + eb, :], in0=po[:], in1=rzz[:], op=ALU.mult), reads=[po, rzz], writes=[o])
                    k.dma("pool", ov[:, :, t0:t0 + 512], o[:], o, reads=[o])
            k.barrier()
            with contextlib.ExitStack() as st:
                epi = make_store_epi(st, lambda tag, t0, n: Y[tag * 128:(tag + 1) * 128, t0:t0 + n], F32, TB)
                linear_fm(OxT, D, b_xo[l], [(i * 128, i) for i in range(16)], epi)
            k.barrier()

        def ffn(l):
            TBF = min(1024, S)
            with contextlib.ExitStack() as st:
                KC = 16
                act = k.sb(st, "ff_act", [128, KC, TBF + 2], BF16)
                wa = [k.sb(st, "ff_wa", [128, KC, 128], BF16) for _ in range(2)]
                wg = [k.sb(st, "ff_wg", [128, KC, 128], BF16) for _ in range(2)]
                uas = [k.sb(st, "ff_ua", [128, TBF + 2], F32) for _ in range(2)]
                ugs = [k.sb(st, "ff_ug", [128, TBF + 2], F32) for _ in range(2)]
                cas = [k.sb(st, "ff_ca", [128, TBF], F32) for _ in range(2)]
                cgs = [k.sb(st, "ff_cg", [128, TBF], F32) for _ in range(2)]
                t1s = [k.sb(st, "ff_t1", [128, TBF], F32) for _ in range(2)]
                og = [k.sb(st, "ff_og", [128, TBF], BF16) for _ in range(2)]
                Wv = b_up[l].rearrange("(kc p) n -> p kc n", p=128)
                inv = H.rearrange("(kc p) s -> p kc s", p=128)
                it = 0
                for tb0 in range(0, S, TBF):
                    lo = 1 if tb0 == 0 else 0
                    hi = TBF + 1 if tb0 + TBF >= S else TBF + 2
                    if lo == 1:
                        k.op("pool", lambda e: e.memset(act[:, :, 0:1], 0.0), writes=[act])
                    if hi == TBF + 1:
                        k.op("pool", lambda e: e.memset(act[:, :, TBF + 1:TBF + 2], 0.0), writes=[act])
                    k.dma("sp", act[:, :, lo:hi], inv[:, :, tb0 - 1 + lo:tb0 - 1 + hi], act, writes=[act])
                    for nb in range(44):
                        wta, wtg = wa[it % 2], wg[it % 2]
                        o = og[it % 2]
                        ua, ug, ca, cg, t1 = uas[it % 2], ugs[it % 2], cas[it % 2], cgs[it % 2], t1s[it % 2]
                        it += 1
                        k.dma("sp", wta[:], Wv[:, :, nb * 128:(nb + 1) * 128], wta, writes=[wta])
                        k.dma("sp", wtg[:], Wv[:, :, DFF + nb * 128:DFF + (nb + 1) * 128], wtg, writes=[wtg])
                        for (wt, ub, engs) in ((wta, ua, "act"), (wtg, ug, "dve")):
                            tiles = []
                            c0 = 0
                            while c0 < TBF + 2:
                                n = min(512, TBF + 2 - c0)
                                tiles.append((c0, n, ps_next()))
                                c0 += n
                            for kc in range(KC):
                                for (c0, n, ps) in tiles:
                                    k.op("pe", lambda e, ps=ps, wt=wt, kc=kc, c0=c0, n=n: e.matmul(
                                        ps[:, 0:n], lhsT=wt[:, kc, :], rhs=act[:, kc, c0:c0 + n], start=(kc == 0), stop=(kc == KC - 1)),
                                        reads=[wt, act], writes=[ps], inc=(kc == KC - 1))
                            for (c0, n, ps) in tiles:
                                if engs == "act":
                                    k.op("act", lambda e, ps=ps, ub=ub, c0=c0, n=n: e.copy(out=ub[:, c0:c0 + n], in_=ps[:, 0:n]), reads=[ps], writes=[ub])
                                else:
                                    k.op("dve", lambda e, ps=ps, ub=ub, c0=c0, n=n: e.tensor_copy(out=ub[:, c0:c0 + n], in_=ps[:, 0:n]), reads=[ps], writes=[ub])
                        for (ub, cb, f0, eng) in ((ua, ca, nb, "pool"), (ug, cg, 44 + nb, "dve")):
                            w0 = c_cw[:, (l * 3 + 0) * 88 + f0:(l * 3 + 0) * 88 + f0 + 1]
                            w1 = c_cw[:, (l * 3 + 1) * 88 + f0:(l * 3 + 1) * 88 + f0 + 1]
                            w2 = c_cw[:, (l * 3 + 2) * 88 + f0:(l * 3 + 2) * 88 + f0 + 1]
                            bb = c_cb[:, l * 88 + f0:l * 88 + f0 + 1]
                            k.op("dve", lambda e, ub=ub, cb=cb, w1=w1, bb=bb: e.tensor_scalar(
                                out=cb[:], in0=ub[:, 1:TBF + 1], scalar1=w1, scalar2=bb, op0=ALU.mult, op1=ALU.add),
                                reads=[ub, c_cw, c_cb], writes=[cb])
                            k.op("dve", lambda e, ub=ub, cb=cb, w0=w0: e.scalar_tensor_tensor(
                                out=cb[:], in0=ub[:, 0:TBF], scalar=w0, in1=cb[:], op0=ALU.mult, op1=ALU.add),
                                reads=[ub, cb, c_cw], writes=[cb])
                            k.op("dve", lambda e, ub=ub, cb=cb, w2=w2: e.scalar_tensor_tensor(
                                out=cb[:], in0=ub[:, 2:TBF + 2], scalar=w2, in1=cb[:], op0=ALU.mult, op1=ALU.add),
                                reads=[ub, cb, c_cw], writes=[cb])
                        k.op("act", lambda e: e.activation(out=t1[:], in_=cg[:], func=AF.Square), reads=[cg], writes=[t1])
                        k.op("pool", lambda e: e.tensor_scalar(out=t1[:], in0=t1[:], scalar1=0.044715, scalar2=1.0, op0=ALU.mult, op1=ALU.add),
                             reads=[t1], writes=[t1])
                        k.op("pool", lambda e: e.tensor_tensor(out=t1[:], in0=t1[:], in1=cg[:], op=ALU.mult), reads=[t1, cg], writes=[t1])
                        k.op("act", lambda e: e.activation(out=t1[:], in_=t1[:], func=AF.Sigmoid, scale=1.5957691216057308), reads=[t1], writes=[t1])
                        k.op("pool", lambda e: e.tensor_tensor(out=t1[:], in0=t1[:], in1=cg[:], op=ALU.mult), reads=[t1, cg], writes=[t1])
                        k.op("pool", lambda e, o=o: e.tensor_tensor(out=o[:], in0=t1[:], in1=ca[:], op=ALU.mult), reads=[t1, ca], writes=[o])
                        k.dma("pool", GG[nb * 128:(nb + 1) * 128, tb0:tb0 + TBF], o[:], o, reads=[o])
            k.barrier()
            with contextlib.ExitStack() as st:
                epi = make_store_epi(st, lambda tag, t0, n: Y[tag * 128:(tag + 1) * 128, t0:t0 + n], F32, TBF)
                linear_fm(GG, DFF, b_down[l], [(i * 128, i) for i in range(16)], epi, tb=TBF, gw=256)
            k.barrier()

        def even_mixer(l):
            i = l // 2
            lam_init = 0.8 - 0.6 * math.exp(-0.3 * l)
            with contextlib.ExitStack() as st:
                stg = [k.sb(st, "em_s", [128, 512], BF16) for _ in range(3)]
                si = [0]

                def epi_tm(dst):
                    def epi(ps, g0, gwid, tok0):
                        sg = stg[si[0] % 3]
                        si[0] += 1
                        if si[0] % 2:
                            k.op("act", lambda e: e.copy(out=sg[:, 0:gwid], in_=ps[:, 0:gwid]), reads=[ps], writes=[sg])
                        else:
                            k.op("dve", lambda e: e.tensor_copy(out=sg[:, 0:gwid], in_=ps[:, 0:gwid]), reads=[ps], writes=[sg])
                        k.dma("pool", dst[tok0:tok0 + 128, g0:g0 + gwid], sg[:, 0:gwid], sg, reads=[sg])
                    return epi
                linear_tm(H, D, b_in_even[i], 0, 1024, epi_tm(Ut))
                linear_tm(H, D, b_in_even[i], 3072, 1024, epi_tm(Vt))
            k.barrier()
            with contextlib.ExitStack() as st:
                def dstf(tag, t0, n):
                    return (QT if tag < 8 else KT)[(tag % 8) * 128:(tag % 8 + 1) * 128, t0:t0 + n]
                epi = make_store_epi(st, dstf, BF16, TB)
                linear_fm(H, D, b_in_even[i], [(1024 + j * 128, j) for j in range(16)], epi)
            k.barrier()
            with contextlib.ExitStack() as st:
                NSC = S // 128
                ug = k.sb(st, "fn_u", [128, NSC, 256], BF16)
                SG = 8 if NSC >= 8 else NSC
                cs = [k.sb(st, "fn_c", [128, SG, 512], BF16) for _ in range(2)]
                sn = [k.sb(st, "fn_s", [128, SG, 512], BF16) for _ in range(2)]
                cc = k.sb(st, "fn_cc", [128, 2, 256], BF16)
                sc = k.sb(st, "fn_sc", [128, 2, 256], BF16)
                ab = [k.sb(st, "fn_ab", [128, 4, 512], BF16) for _ in range(2)]
                yo = [k.sb(st, "fn_y", [128, 2, 512], BF16) for _ in range(2)]
                k.dma("sp", cc[:], cosC.rearrange("(c p) n -> p c n", p=128), cc, writes=[cc])
                k.dma("sp", sc[:], nsinC.rearrange("(c p) n -> p c n", p=128), sc, writes=[sc])
                cv = cosS.rearrange("(c p) n -> p c n", p=128)
                sv = sinS.rearrange("(c p) n -> p c n", p=128)
                uv = Ut.rearrange("(c p) n -> p c n", p=128)
                li = 0
                oi = 0
                for g in range(4):
                    for u0 in range(0, NSC, 8):
                        u1 = min(NSC, u0 + 8)
                        k.dma("sp", ug[:, u0:u1, :], uv[:, u0:u1, g * 256:(g + 1) * 256], ug, writes=[ug])
                    for kt in range(NT):
                        pacc = [PS[0], PS[1], PS[2], PS[3]]
                        for s0 in range(0, NSC, SG):
                            ct, stt = cs[li % 2], sn[li % 2]
                            li += 1
                            k.dma("sp", ct[:], cv[:, s0:s0 + SG, kt * 512:(kt + 1) * 512], ct, writes=[ct])
                            k.dma("sp", stt[:], sv[:, s0:s0 + SG, kt * 512:(kt + 1) * 512], stt, writes=[stt])
                            for sj in range(SG):
                                sci = s0 + sj
                                for cb in range(2):
                                    for (mi, mt) in ((0, ct), (1, stt)):
                                        p = pacc[mi * 2 + cb]
                                        k.op("pe", lambda e, p=p, mt=mt, sj=sj, sci=sci, cb=cb: e.matmul(
                                            p[:], lhsT=ug[:, sci, cb * 128:(cb + 1) * 128], rhs=mt[:, sj, :],
                                            start=(sci == 0), stop=(sci == NSC - 1)),
                                            reads=[ug, mt], writes=[p], inc=(sci == NSC - 1) or (sj == SG - 1 and cb == 1 and mi == 1))
                        a = ab[oi % 2]
                        y = yo[oi % 2]
                        oi += 1
                        for j in range(4):
                            if j % 2 == 0:
                                k.op("act", lambda e, j=j: e.copy(out=a[:, j, :], in_=pacc[j][:]), reads=[pacc[j]], writes=[a])
                            else:
                                k.op("dve", lambda e, j=j: e.tensor_copy(out=a[:, j, :], in_=pacc[j][:]), reads=[pacc[j]], writes=[a])
                        for cpb in range(2):
                            py = ps_pick(4, 8)
                            for j in range(4):
                                mat = cc if j < 2 else sc
                                k.op("pe", lambda e, py=py, mat=mat, j=j, cpb=cpb: e.matmul(
                                    py[:], lhsT=mat[:, j % 2, cpb * 128:(cpb + 1) * 128], rhs=a[:, j, :], start=(j == 0), stop=(j == 3)),
                                    reads=[mat, a], writes=[py], inc=(j == 3))
                            k.op("act", lambda e, py=py, cpb=cpb: e.copy(out=y[:, cpb, :], in_=py[:]), reads=[py], writes=[y])
                        k.dma("pool", CAT[g * 256:(g + 1) * 256, kt * 512:(kt + 1) * 512].rearrange("(c p) s -> p c s", p=128),
                              y[:], y, reads=[y])
            k.barrier()
            with contextlib.ExitStack() as st:
                kt_ = k.sb(st, "da_k", [128, S], BF16)
                qt_ = k.sb(st, "da_q", [128, S], BF16)
                vt_ = k.sb(st, "da_v", [128, NKB, 128], BF16)
                bt_ = k.sb(st, "da_bt", [128, 6, 512], F32)
                es = [k.sb(st, "da_e", [128, 2, 512], BF16) for _ in range(3)]
                tm = [k.sb(st, "da_tm", [128, 2, 512], F32) for _ in range(2)]
                rz = k.sb(st, "da_rz", [128, 2, 512], F32)
                o32 = k.sb(st, "da_o", [128, 512], F32)
                t2 = k.sb(st, "da_t2", [128, 512], F32)
                osq = k.sb(st, "da_sq", [128, 512], BF16)
                rs = k.sb(st, "da_rs", [128, 512], F32)
                tmp = k.sb(st, "da_tmp", [128, 512], F32)
                ob = [k.sb(st, "da_ob", [128, 512], BF16) for _ in range(2)]
                hg = k.sb(st, "da_hg", [128, 1], F32)
                k.op("dve", lambda e: e.tensor_scalar(out=hg[:], in0=c_dng[:, i:i + 1], scalar1=(1.0 - lam_init), scalar2=None, op0=ALU.mult),
                     reads=[c_dng], writes=[hg])
                ei = 0
                oi = 0
                for hh in range(8):
                    k.dma("sp", kt_[:], KT[hh * 128:(hh + 1) * 128, :], kt_, writes=[kt_])
                    k.dma("sp", qt_[:], QT[hh * 128:(hh + 1) * 128, :], qt_, writes=[qt_])
                    vsrc = Vt[:, hh * 128:(hh + 1) * 128].rearrange("(kb p) e -> p kb e", p=128)
                    for v0 in range(0, NKB, 8):
                        v1 = min(NKB, v0 + 8)
                        k.dma("sp", vt_[:, v0:v1, :], vsrc[:, v0:v1, :], vt_, writes=[vt_])
                    k.dma("sp", bt_[:], BT[hh].rearrange("p (d q) -> p d q", q=512), bt_, writes=[bt_])
                    for qi in range(NT):
                        i0 = qi * 4
                        po = [PS[0], PS[1]]
                        pz = [PS[2], PS[3]]
                        for kb in range(NKB):
                            Dd = kb - i0
                            near = (-1 <= Dd <= 4)
                            side = 1 if kb > i0 else 0
                            ee = es[ei % 3]
                            ei += 1
                            pss = [ps_pick(4, 8), ps_pick(4, 8)]
                            for c in range(2):
                                k.op("pe", lambda e, c=c, kb=kb, qi=qi, pss=pss: e.matmul(
                                    pss[c][:], lhsT=kt_[c * 64:(c + 1) * 64, kb * 128:(kb + 1) * 128],
                                    rhs=qt_[c * 64:(c + 1) * 64, qi * 512:(qi + 1) * 512], start=True, stop=True),
                                    reads=[kt_, qt_], writes=[pss[c]], inc=True)
                            if near:
                                tt = tm[ei % 2]
                                for c in range(2):
                                    k.op("dve", lambda e, c=c, tt=tt, pss=pss, Dd=Dd: e.scalar_tensor_tensor(
                                        out=tt[:, c, :], in0=pss[c][:], scalar=0.125, in1=bt_[:, Dd + 1, :], op0=ALU.mult, op1=ALU.add),
                                        reads=[pss[c], bt_], writes=[tt])
                                k.op("act", lambda e, tt=tt, ee=ee, kb=kb: e.activation(
                                    out=ee[:], in_=tt[:], func=AF.Exp, bias=c_kmask[:, kb:kb + 1]), reads=[tt, c_kmask], writes=[ee])
                            else:
                                fo = (hh * 2 + side) * NKB + kb
                                for c in range(2):
                                    k.op("act", lambda e, c=c, ee=ee, pss=pss, fo=fo: e.activation(
                                        out=ee[:, c, :], in_=pss[c][:], func=AF.Exp, scale=0.125, bias=c_far[:, fo:fo + 1]),
                                        reads=[pss[c], c_far], writes=[ee])
                            for c in range(2):
                                k.op("pe", lambda e, c=c, ee=ee, kb=kb: e.matmul(
                                    po[c][:], lhsT=vt_[:, kb, :], rhs=ee[:, c, :], start=(kb == 0), stop=(kb == NKB - 1)),
                                    reads=[vt_, ee], writes=[po[c]], inc=(kb == NKB - 1))
                            for c in range(2):
                                k.op("pe", lambda e, c=c, ee=ee, kb=kb: e.matmul(
                                    pz[c][:], lhsT=c_ones[:], rhs=ee[:, c, :], start=(kb == 0), stop=(kb == NKB - 1)),
                                    reads=[c_ones, ee], writes=[pz[c]], inc=(kb == NKB - 1))
                        for c in range(2):
                            k.op("dve", lambda e, c=c: e.reciprocal(out=rz[:, c, :], in_=pz[c][:]), reads=[pz[c]], writes=[rz])
                        k.op("dve", lambda e: e.tensor_tensor(out=o32[:], in0=po[0][:], in1=rz[:, 0, :], op=ALU.mult), reads=[po[0], rz], writes=[o32])
                        k.op("dve", lambda e: e.tensor_tensor(out=t2[:], in0=po[1][:], in1=rz[:, 1, :], op=ALU.mult), reads=[po[1], rz], writes=[t2])
                        k.op("dve", lambda e: e.scalar_tensor_tensor(out=o32[:], in0=t2[:], scalar=c_lam[:, i:i + 1], in1=o32[:],
                                                                     op0=ALU.mult, op1=ALU.add), reads=[t2, o32, c_lam], writes=[o32])
                        k.op("act", lambda e: e.activation(out=osq[:], in_=o32[:], func=AF.Square), reads=[o32], writes=[osq])
                        pn = ps_pick(4, 8)
                        k.op("pe", lambda e, pn=pn: e.matmul(pn[:], lhsT=c_ones[:], rhs=osq[:], start=True, stop=True),
                             reads=[c_ones, osq], writes=[pn], inc=True)
                        rstd_from(pn, 512, 128, rs, tmp)
                        obb = ob[oi % 2]
                        oi += 1
                        k.op("dve", lambda e, obb=obb: e.scalar_tensor_tensor(out=obb[:], in0=o32[:], scalar=hg[:, 0:1], in1=rs[:],
                                                                              op0=ALU.mult, op1=ALU.mult), reads=[o32, hg, rs], writes=[obb])
                        k.dma("pool", CAT[1024 + hh * 128:1024 + (hh + 1) * 128, qi * 512:(qi + 1) * 512], obb[:], obb, reads=[obb])
            k.barrier()
            with contextlib.ExitStack() as st:
                epi = make_store_epi(st, lambda tag, t0, n: Y[tag * 128:(tag + 1) * 128, t0:t0 + n], F32, TB)
                linear_fm(CAT, D, b_out_even[i], [(j * 128, j) for j in range(16)], epi)
            k.barrier()

        def odd_mixer(l):
            i = l // 2
            C = 64
            with contextlib.ExitStack() as st:
                def dstf(tag, t0, n):
                    return (QgT if tag < 8 else KgT)[(tag % 8) * 128:(tag % 8 + 1) * 128, t0:t0 + n]
                epi_q = make_store_epi(st, dstf, BF16, TB, scale=256 ** -0.5)
                linear_fm(H, D, b_in_odd[i], [(j * 128, j) for j in range(8)], epi_q)
            k.barrier()
            with contextlib.ExitStack() as st:
                def dstf2(tag, t0, n):
                    return KgT[tag * 128:(tag + 1) * 128, t0:t0 + n]
                epi_k = make_store_epi(st, dstf2, BF16, TB)
                linear_fm(H, D, b_in_odd[i], [(1024 + j * 128, j) for j in range(8)], epi_k)
            k.barrier()
            with contextlib.ExitStack() as st:
                def dstf3(tag, t0, n):
                    return RS[tag * 128:(tag + 1) * 128, t0:t0 + n]
                epi_r = make_store_epi(st, dstf3, BF16, TB, func=AF.Silu)
                linear_fm(H, D, b_in_odd[i], [(4096 + j * 128, j) for j in range(16)], epi_r)
            k.barrier()
            with contextlib.ExitStack() as st:
                stg = [k.sb(st, "om_s", [128, 512], BF16) for _ in range(3)]
                si = [0]

                def epi_tm(dst):
                    def epi(ps, g0, gwid, tok0):
                        sg = stg[si[0] % 3]
                        si[0] += 1
                        if si[0] % 2:
                            k.op("act", lambda e: e.copy(out=sg[:, 0:gwid], in_=ps[:, 0:gwid]), reads=[ps], writes=[sg])
                        else:
                            k.op("dve", lambda e: e.tensor_copy(out=sg[:, 0:gwid], in_=ps[:, 0:gwid]), reads=[ps], writes=[sg])
                        k.dma("pool", dst[tok0:tok0 + 128, g0:g0 + gwid], sg[:, 0:gwid], sg, reads=[sg])
                    return epi
                linear_tm(H, D, b_in_odd[i], 1024, 1024, epi_tm(Kg))
                linear_tm(H, D, b_in_odd[i], 2048, 2048, epi_tm(Vg))
            k.barrier()
            with contextlib.ExitStack() as st:
                act = k.sb(st, "gd_act", [128, 16, TB], BF16)
                wt = k.sb(st, "gd_w", [128, 16, 32], BF16)
                sg = [k.sb(st, "gd_s", [32, 512], BF16) for _ in range(2)]
                k.dma("sp", wt[:], b_gd[i].rearrange("(kc p) n -> p kc n", p=128), wt, writes=[wt])
                inv = H.rearrange("(kc p) s -> p kc s", p=128)
                n_ = 0
                for tb0 in range(0, S, TB):
                    k.dma("sp", act[:], inv[:, :, tb0:tb0 + TB], act, writes=[act])
                    for tt in range(TB // 512):
                        ps = ps_next()
                        for kc in range(16):
                            k.op("pe", lambda e, ps=ps, kc=kc, tt=tt: e.matmul(ps[0:32, :], lhsT=wt[:, kc, :], rhs=act[:, kc, tt * 512:(tt + 1) * 512],
                                                                               start=(kc == 0), stop=(kc == 15)),
                                 reads=[wt, act], writes=[ps], inc=(kc == 15))
                        s_ = sg[n_ % 2]
                        n_ += 1
                        k.op("act", lambda e, ps=ps, s_=s_: e.copy(out=s_[:], in_=ps[0:32, :]), reads=[ps], writes=[s_])
                        k.dma("pool", HDT[:, tb0 + tt * 512:tb0 + (tt + 1) * 512], s_[:], s_, reads=[s_])
            k.barrier()
            with contextlib.ExitStack() as st:
                hda = [k.sb(st, "gu_h", [32, S], BF16) for _ in range(2)]
                wu = [k.sb(st, "gu_w", [32, 1024], BF16) for _ in range(2)]
                gs = [k.sb(st, "gu_g", [128, 1024], F32) for _ in range(2)]
                ghs = [k.sb(st, "gu_gh", [128, 1024], BF16) for _ in range(2)]
                gls = [k.sb(st, "gu_gl", [128, 1024], BF16) for _ in range(2)]
                n_ = 0
                for d in range(2):
                    hd = hda[d]
                    k.op("pool", lambda e, hd=hd: e.memset(hd[:], 1.0), writes=[hd])
                    k.dma("sp", hd[0:16, :], HDT[d * 16:(d + 1) * 16, :], hd, writes=[hd])
                    k.dma("sp", wu[d][:], b_gu[(i * 2 + d) * 32:(i * 2 + d + 1) * 32, :], wu[d], writes=[wu[d]])
                    for tk in range(S // 128):
                        g_ = gs[n_ % 2]
                        n_ += 1
                        for half in range(2):
                            ps = ps_next()
                            k.op("pe", lambda e, ps=ps, hd=hd, d=d, tk=tk, half=half: e.matmul(
                                ps[:], lhsT=hd[:, tk * 128:(tk + 1) * 128], rhs=wu[d][:, half * 512:(half + 1) * 512], start=True, stop=True),
                                reads=[hd, wu[d]], writes=[ps], inc=True)
                            k.op("act", lambda e, ps=ps, g_=g_, half=half: e.activation(out=g_[:, half * 512:(half + 1) * 512], in_=ps[:], func=AF.Exp, scale=-1.0),
                                 reads=[ps], writes=[g_])
                        k.op("dve", lambda e, g_=g_: e.tensor_scalar(out=g_[:], in0=g_[:], scalar1=1.0, scalar2=None, op0=ALU.add), reads=[g_], writes=[g_])
                        k.op("act", lambda e, g_=g_: e.activation(out=g_[:], in_=g_[:], func=AF.Ln), reads=[g_], writes=[g_])
                        k.op("pool", lambda e, g_=g_: e.tensor_scalar(out=g_[:], in0=g_[:], scalar1=-1.0 / 16.0, scalar2=None, op0=ALU.mult), reads=[g_], writes=[g_])
                        gh_ = ghs[n_ % 2]
                        gl_ = gls[n_ % 2]
                        k.op("act", lambda e, g_=g_, gh_=gh_: e.copy(out=gh_[:], in_=g_[:]), reads=[g_], writes=[gh_])
                        k.op("dve", lambda e, g_=g_, gh_=gh_: e.tensor_tensor(out=g_[:], in0=g_[:], in1=gh_[:], op=ALU.subtract), reads=[g_, gh_], writes=[g_])
                        k.op("pool", lambda e, g_=g_, gl_=gl_: e.tensor_copy(out=gl_[:], in_=g_[:]), reads=[g_], writes=[gl_])
                        k.dma("pool", GH[d, tk * 128:(tk + 1) * 128, :], gh_[:], gh_, reads=[gh_])
                        k.dma("pool", GL[d, tk * 128:(tk + 1) * 128, :], gl_[:], gl_, reads=[gl_])
            k.barrier()
            with contextlib.ExitStack() as st:
                SC = 512
                NCH = SC // C
                qs = [k.sb(st, "gl_q", [128, 2, SC], BF16) for _ in range(2)]
                ks = [k.sb(st, "gl_k", [128, 2, SC], BF16) for _ in range(2)]
                kt = [k.sb(st, "gl_kt", [C, NCH, 256], BF16) for _ in range(2)]
                vt = [k.sb(st, "gl_vt", [C, NCH, 512], BF16) for _ in range(2)]
                gt = [k.sb(st, "gl_gt", [C, NCH, 256], BF16) for _ in range(2)]
                gtl = [k.sb(st, "gl_gtl", [C, NCH, 256], BF16) for _ in range(2)]
                ep = [k.sb(st, "gl_ep", [128, 2, C], F32) for _ in range(2)]
                en = [k.sb(st, "gl_en", [128, 2, C], F32) for _ in range(2)]
                ek = [k.sb(st, "gl_ek", [C, 256], F32) for _ in range(2)]
                ql = [k.sb(st, "gl_ql", [128, 2, C], BF16) for _ in range(2)]
                kl = [k.sb(st, "gl_kl", [128, 2, C], BF16) for _ in range(2)]
                kh = [k.sb(st, "gl_kh", [C, 256], BF16) for _ in range(2)]
                am = [k.sb(st, "gl_am", [C, C], BF16) for _ in range(2)]
                s32 = k.sb(st, "gl_s32", [128, 2, 512], F32)
                sbf = k.sb(st, "gl_sbf", [128, 2, 512], BF16)
                oo = [k.sb(st, "gl_o", [128, 4, SC], F32) for _ in range(2)]
                ci = 0
                sci = 0
                for d in range(2):
                    Ltri = c_trib[:, (0 if d == 0 else 64):(64 if d == 0 else 128)]
                    Mtri = c_trib[:, (128 if d == 0 else 192):(192 if d == 0 else 256)]
                    Lmask = c_trib[:, (0 if d == 0 else 64):(64 if d == 0 else 128)]
                    for hh in range(4):
                        k.op("pool", lambda e: e.memset(s32[:], 0.0), writes=[s32])
                        k.op("pool", lambda e: e.memset(sbf[:], 0.0), writes=[sbf])
                        sc_order = list(range(S // SC))
                        if d == 1:
                            sc_order.reverse()
                        for scx in sc_order:
                            t0 = scx * SC
                            b = sci % 2
                            sci += 1
                            q_, k_, kt_, vt_, gt_, gl2_, o_ = qs[b], ks[b], kt[b], vt[b], gt[b], gtl[b], oo[b]
                            k.dma("sp", q_[:], QgT[hh * 256:(hh + 1) * 256, t0:t0 + SC].rearrange("(c p) s -> p c s", p=128), q_, writes=[q_])
                            k.dma("sp", k_[:], KgT[hh * 256:(hh + 1) * 256, t0:t0 + SC].rearrange("(c p) s -> p c s", p=128), k_, writes=[k_])
                            k.dma("sp", kt_[:], Kg[t0:t0 + SC, hh * 256:(hh + 1) * 256].rearrange("(c p) n -> p c n", p=C), kt_, writes=[kt_])
                            k.dma("sp", vt_[:], Vg[t0:t0 + SC, hh * 512:(hh + 1) * 512].rearrange("(c p) n -> p c n", p=C), vt_, writes=[vt_])
                            k.dma("sp", gt_[:], GH[d, t0:t0 + SC, hh * 256:(hh + 1) * 256].rearrange("(c p) n -> p c n", p=C), gt_, writes=[gt_])
                            k.dma("sp", gl2_[:], GL[d, t0:t0 + SC, hh * 256:(hh + 1) * 256].rearrange("(c p) n -> p c n", p=C), gl2_, writes=[gl2_])
                            ch_order = list(range(NCH))
                            if d == 1:
                                ch_order.reverse()
                            for ch in ch_order:
                                x = ci % 2
                                ci += 1
                                c0 = ch * C
                                pb = ps_next()
                                for db in range(2):
                                    k.op("pe", lambda e, pb=pb, db=db, ch=ch, gt_=gt_: e.matmul(
                                        pb[:, db * C:(db + 1) * C], lhsT=gt_[:, ch, db * 128:(db + 1) * 128], rhs=Ltri, start=True, stop=False),
                                        reads=[gt_, c_trib], writes=[pb], inc=False)
                                    k.op("pe", lambda e, pb=pb, db=db, ch=ch, gl2_=gl2_: e.matmul(
                                        pb[:, db * C:(db + 1) * C], lhsT=gl2_[:, ch, db * 128:(db + 1) * 128], rhs=Ltri, start=False, stop=True),
                                        reads=[gl2_, c_trib], writes=[pb], inc=True)
                                pk = ps_next()
                                k.op("pe", lambda e, pk=pk, ch=ch, gt_=gt_: e.matmul(pk[0:C, 0:256], lhsT=Mtri, rhs=gt_[:, ch, :], start=True, stop=False),
                                     reads=[gt_, c_trib], writes=[pk], inc=False)
                                k.op("pe", lambda e, pk=pk, ch=ch, gl2_=gl2_: e.matmul(pk[0:C, 0:256], lhsT=Mtri, rhs=gl2_[:, ch, :], start=False, stop=True),
                                     reads=[gl2_, c_trib], writes=[pk], inc=True)
                                ep_, en_, ek_ = ep[x], en[x], ek[x]
                                k.op("act", lambda e, pb=pb, ep_=ep_: e.activation(out=ep_[:], in_=pb[:, 0:2 * C].rearrange("p (a c) -> p a c", c=C), func=AF.Exp), reads=[pb], writes=[ep_])
                                k.op("act", lambda e, pb=pb, en_=en_: e.activation(out=en_[:], in_=pb[:, 0:2 * C].rearrange("p (a c) -> p a c", c=C), func=AF.Exp, scale=-1.0), reads=[pb], writes=[en_])
                                k.op("act", lambda e, pk=pk, ek_=ek_: e.activation(out=ek_[:], in_=pk[0:C, 0:256], func=AF.Exp), reads=[pk], writes=[ek_])
                                ql_, kl_, kh_, am_ = ql[x], kl[x], kh[x], am[x]
                                k.op("dve", lambda e, q_=q_, ql_=ql_, ep_=ep_, c0=c0: e.tensor_tensor(out=ql_[:], in0=q_[:, :, c0:c0 + C], in1=ep_[:], op=ALU.mult),
                                     reads=[q_, ep_], writes=[ql_])
                                k.op("pool", lambda e, k_=k_, kl_=kl_, en_=en_, c0=c0: e.tensor_tensor(out=kl_[:], in0=k_[:, :, c0:c0 + C], in1=en_[:], op=ALU.mult),
                                     reads=[k_, en_], writes=[kl_])
                                k.op("pool", lambda e, kt_=kt_, kh_=kh_, ek_=ek_, ch=ch: e.tensor_tensor(out=kh_[:], in0=kt_[:, ch, :], in1=ek_[:], op=ALU.mult),
                                     reads=[kt_, ek_], writes=[kh_])
                                pa = ps_next()
                                for db in range(2):
                                    k.op("pe", lambda e, pa=pa, db=db, kl_=kl_, ql_=ql_: e.matmul(pa[0:C, 0:C], lhsT=kl_[:, db, :], rhs=ql_[:, db, :],
                                                                                               start=(db == 0), stop=(db == 1)),
                                         reads=[kl_, ql_], writes=[pa], inc=(db == 1))
                                k.op("dve", lambda e, pa=pa, am_=am_: e.tensor_tensor(out=am_[:], in0=pa[0:C, 0:C], in1=Lmask, op=ALU.mult),
                                     reads=[pa, c_trib], writes=[am_])
                                po = ps_next()
                                for eb in range(4):
                                    for db in range(2):
                                        k.op("pe", lambda e, po=po, eb=eb, db=db, ql_=ql_: e.matmul(
                                            po[:, eb * C:(eb + 1) * C], lhsT=sbf[:, db, eb * 128:(eb + 1) * 128], rhs=ql_[:, db, :],
                                            start=(db == 0), stop=False), reads=[sbf, ql_], writes=[po], inc=False)
                                    k.op("pe", lambda e, po=po, eb=eb, vt_=vt_, am_=am_, ch=ch: e.matmul(
                                        po[:, eb * C:(eb + 1) * C], lhsT=vt_[:, ch, eb * 128:(eb + 1) * 128], rhs=am_[:],
                                        start=False, stop=True), reads=[vt_, am_], writes=[po], inc=True)
                                k.op("act", lambda e, po=po, o_=o_, c0=c0: e.copy(out=o_[:, :, c0:c0 + C], in_=po[:, 0:4 * C].rearrange("p (a c) -> p a c", c=C)), reads=[po], writes=[o_])
                                bl = (C - 1) if d == 0 else 0
                                for db in range(2):
                                    pd = ps_next()
                                    k.op("pe", lambda e, pd=pd, db=db, kh_=kh_, vt_=vt_, ch=ch: e.matmul(
                                        pd[:], lhsT=kh_[:, db * 128:(db + 1) * 128], rhs=vt_[:, ch, :], start=True, stop=True),
                                        reads=[kh_, vt_], writes=[pd], inc=True)
                                    k.op("dve", lambda e, pd=pd, db=db, ep_=ep_, bl=bl: e.scalar_tensor_tensor(
                                        out=s32[:, db, :], in0=s32[:, db, :], scalar=ep_[:, db, bl:bl + 1], in1=pd[:], op0=ALU.mult, op1=ALU.add),
                                        reads=[s32, ep_, pd], writes=[s32])
                                k.op("act", lambda e: e.copy(out=sbf[:], in_=s32[:]), reads=[s32], writes=[sbf])
                            k.dma("pool", OG[d, hh * 512:(hh + 1) * 512, t0:t0 + SC].rearrange("(c p) s -> p c s", p=128), o_[:], o_, reads=[o_])
            k.barrier()
            with contextlib.ExitStack() as st:
                a = [k.sb(st, "op_a", [128, 16, 512], F32) for _ in range(1)]
                b = [k.sb(st, "op_b", [128, 16, 512], F32) for _ in range(1)]
                r = [k.sb(st, "op_r", [128, 16, 512], BF16) for _ in range(1)]
                sq = k.sb(st, "op_sq", [128, 16, 512], BF16)
                rs = k.sb(st, "op_rs", [128, 512], F32)
                tmp = k.sb(st, "op_t", [128, 512], F32)
                o = [k.sb(st, "op_o", [128, 16, 512], BF16) for _ in range(2)]
                av = OG[0].rearrange("(c p) s -> p c s", p=128)
                bv = OG[1].rearrange("(c p) s -> p c s", p=128)
                rv = RS.rearrange("(c p) s -> p c s", p=128)
                ov = CAT.rearrange("(c p) s -> p c s", p=128)
                for ti in range(NT):
                    t0 = ti * 512
                    a_, b_, r_, o_ = a[0], b[0], r[0], o[ti % 2]
                    k.dma("sp", a_[:], av[:, :, t0:t0 + 512], a_, writes=[a_])
                    k.dma("sp", b_[:], bv[:, :, t0:t0 + 512], b_, writes=[b_])
                    k.dma("sp", r_[:], rv[:, :, t0:t0 + 512], r_, writes=[r_])
                    k.op("pool", lambda e, a_=a_, b_=b_: e.tensor_tensor(out=a_[:], in0=a_[:], in1=b_[:], op=ALU.add), reads=[a_, b_], writes=[a_])
                    k.op("act", lambda e, a_=a_: e.activation(out=sq[:], in_=a_[:], func=AF.Square), reads=[a_], writes=[sq])
                    for hh in range(4):
                        ps = ps_next()
                        for c in range(4):
                            k.op("pe", lambda e, ps=ps, c=c, hh=hh: e.matmul(ps[:], lhsT=c_ones[:], rhs=sq[:, hh * 4 + c, :], start=(c == 0), stop=(c == 3)),
                                 reads=[c_ones, sq], writes=[ps], inc=(c == 3))
                        rstd_from(ps, 512, 512, rs, tmp)
                        for c in range(4):
                            cc_ = hh * 4 + c
                            k.op("dve", lambda e, a_=a_, cc_=cc_, c=c: e.scalar_tensor_tensor(
                                out=a_[:, cc_, :], in0=a_[:, cc_, :], scalar=c_gng[:, i * 4 + c:i * 4 + c + 1], in1=rs[:], op0=ALU.mult, op1=ALU.mult),
                                reads=[a_, rs, c_gng], writes=[a_])
                    k.op("pool", lambda e, a_=a_, r_=r_, o_=o_: e.tensor_tensor(out=o_[:], in0=a_[:], in1=r_[:], op=ALU.mult), reads=[a_, r_], writes=[o_])
                    k.dma("pool", ov[:, :, t0:t0 + 512], o_[:], o_, reads=[o_])
            k.barrier()
            with contextlib.ExitStack() as st:
                epi = make_store_epi(st, lambda tag, t0, n: Y[tag * 128:(tag + 1) * 128, t0:t0 + n], F32, TB)
                linear_fm(CAT, D, b_out_odd[i], [(j * 128, j) for j in range(16)], epi)
            k.barrier()

        norm_pass(None, None, None, 0, 0, Xsrc=xT)
        for l in range(DEPTH):
            if l % 2 == 0:
                even_mixer(l)
            else:
                odd_mixer(l)
            norm_pass(l, 1, Y, l, 2)
            cross_attn(l)
            norm_pass(l, 3, Y, l, 5)
            ffn(l)
            if l + 1 < DEPTH:
                norm_pass(l, 6, Y, l + 1, 0)
            else:
                norm_pass(l, 6, Y, None, None)
        k.stopped = False
        k.barrier(final=True)
    return nc


def _t5_bucket(rel):
    try:
        import jax
        import jax.numpy as jnp
        with jax.default_device(jax.devices("cpu")[0]):
            r = jnp.asarray(np.asarray(rel, np.int32))
            n, max_exact = 16, 8
            base = jnp.where(r > 0, n, 0)
            a = jnp.abs(r)
            af = jnp.maximum(a, 1).astype(jnp.float32)
            large = max_exact + (jnp.log(af / max_exact) / math.log(128 / max_exact) * (n - max_exact)).astype(jnp.int32)
            large = jnp.minimum(large, n - 1)
            return np.asarray(base + jnp.where(a < max_exact, a, large))
    except Exception:
        pass
    n = 16
    max_exact = 8
    base = np.where(rel > 0, n, 0)
    a = np.abs(rel)
    af = np.maximum(a, 1).astype(np.float32)
    large = max_exact + (np.log(af / np.float32(max_exact)) / np.float32(math.log(128 / max_exact))
                         * np.float32(n - max_exact)).astype(np.int32)
    large = np.minimum(large, n - 1)
    return base + np.where(a < max_exact, a, large)


def host_constants(S, S_real):
    NKB = S // 128
    maskb = np.zeros((128, S), np.float32)
    maskb[:, :S_real] = 1.0
    kmask = np.zeros((128, NKB), np.float32)
    pos = np.arange(S).reshape(NKB, 128).T
    kmask[pos >= S_real] = -30000.0
    idx = np.arange(S_real, dtype=np.int64)
    ang = 2.0 * np.pi * ((idx[:, None] * idx[None, :]) % S_real).astype(np.float64) / S_real
    cs = np.zeros((S, S), np.float32)
    sn = np.zeros((S, S), np.float32)
    cs[:S_real, :S_real] = np.cos(ang) / math.sqrt(S_real)
    sn[:S_real, :S_real] = np.sin(ang) / math.sqrt(S_real)
    ic = np.arange(256)
    angc = 2.0 * np.pi * ((ic[:, None] * ic[None, :]) % 256) / 256.0
    cc = (np.cos(angc) / 16.0).astype(np.float32)
    nsc = (-np.sin(angc) / 16.0).astype(np.float32)
    kk = np.arange(128)[:, None]
    qq = np.arange(512)[None, :]
    bk = np.concatenate([_t5_bucket((128 * Dd + kk - qq).astype(np.int32)) for Dd in range(-1, 5)], axis=1).astype(np.float32)
    s_ = np.arange(64)[:, None]
    t_ = np.arange(64)[None, :]
    tri = np.concatenate([(s_ <= t_), (s_ >= t_), (s_ > t_), (s_ < t_)], axis=1).astype(np.float32)
    return dict(maskb=maskb, kmask=kmask, cosS=cs.astype(NPBF), sinS=sn.astype(NPBF), cosC=cc.astype(NPBF),
                nsinC=nsc.astype(NPBF), bkt=bk, trimats=tri)


def host_weights(inp, DEPTH):
    NEVEN = (DEPTH + 1) // 2
    NODD = DEPTH // 2
    f = lambda a: np.ascontiguousarray(np.asarray(a, dtype=np.float32))
    out = {}
    out["ng"] = f(np.asarray(inp["norm_g"])[:DEPTH].reshape(DEPTH, 7, 16, 128).transpose(3, 0, 1, 2).reshape(128, DEPTH * 7 * 16))
    out["relb"] = f(np.asarray(inp["rel_bias"]).reshape(1, 256))
    out["lamp"] = f(np.asarray(inp["diff_lambda"])[:max(NEVEN, 1)].reshape(1, -1))
    out["dng"] = f(np.asarray(inp["diff_norm_g"])[:max(NEVEN, 1)].T)
    out["gng"] = f(np.asarray(inp["gla_norm_g"])[:max(NODD, 1)].reshape(max(NODD, 1), 4, 128).transpose(2, 0, 1).reshape(128, -1))
    out["convw"] = f(np.asarray(inp["conv_w"])[:DEPTH].reshape(DEPTH, 3, 88, 128).transpose(3, 0, 1, 2).reshape(128, -1))
    out["convb"] = f(np.asarray(inp["conv_b"])[:DEPTH].reshape(DEPTH, 88, 128).transpose(2, 0, 1).reshape(128, -1))
    out["w_in_even"] = f(np.asarray(inp["w_in_even"])[:max(NEVEN, 1)])
    out["w_out_even"] = f(np.asarray(inp["w_out_even"])[:max(NEVEN, 1)])
    out["w_in_odd"] = f(np.asarray(inp["w_in_odd"])[:max(NODD, 1)])
    gd = np.asarray(inp["gla_gate_down"])[:max(NODD, 1)]
    out["w_gd"] = f(gd.transpose(0, 2, 1, 3).reshape(gd.shape[0], D, 32))
    gu = np.asarray(inp["gla_gate_up"])[:max(NODD, 1)]
    gb = np.asarray(inp["gla_gate_bias"])[:max(NODD, 1)]
    aug = np.zeros((gu.shape[0], 2, 32, 1024), np.float32)
    aug[:, :, 0:16, :] = gu
    aug[:, :, 16, :] = gb
    out["w_gu"] = f(aug.reshape(-1, 1024))
    out["w_out_odd"] = f(np.asarray(inp["w_out_odd"])[:max(NODD, 1)])
    for nm in ("w_xq", "w_xkv", "w_xo", "w_up", "w_down"):
        out[nm] = f(np.asarray(inp[nm])[:DEPTH])
    return out


_CACHE = {}


def run_trunk(seqs, mems, inp, S, DEPTH, taps=()):
    key = (S, DEPTH, tuple(taps))
    if key not in _CACHE:
        _CACHE[key] = build(S, DEPTH, taps)
    nc = _CACHE[key]
    wts = host_weights(inp, DEPTH)
    consts = {}
    in_maps = []
    for x, m in zip(seqs, mems):
        sr = x.shape[0]
        if sr not in consts:
            consts[sr] = host_constants(S, sr)
        xT = np.zeros((D, S), np.float32)
        xT[:, :sr] = np.asarray(x, np.float32).T
        d = dict(wts)
        d.update(consts[sr])
        d["xT"] = xT
        d["memT"] = np.ascontiguousarray(np.asarray(m, np.float32).T)
        in_maps.append(d)
    res = run_bass_kernel_spmd(nc, in_maps, core_ids=list(range(len(in_maps))))
    return res


def kernel(**inp):
    xp = np.asarray(inp["x_prompt"])
    xs = np.asarray(inp["x_sample"])
    mp = np.asarray(inp["mem_prompt"])
    ms = np.asarray(inp["mem_sample"])
    S = 8192
    seqs = [xp[0], xp[1], xp[2], xp[3], xs[0], xs[0], xs[0], xs[0]]
    mems = [mp[0], mp[1], mp[2], mp[3], ms[0], ms[0], ms[0], ms[0]]
    res = run_trunk(seqs, mems, inp, S, 4)
    yp = np.stack([np.ascontiguousarray(res.results[c]["yT"][:, :2048].T) for c in range(4)], axis=0).astype(np.float32)
    ys = np.ascontiguousarray(res.results[4]["yT"].T)[None].astype(np.float32)
    return (yp, ys)
```

```python
import contextlib
import math
import numpy as np
import ml_dtypes
import concourse.bass as bass
import concourse.mybir as mybir
from concourse.bass_utils import run_bass_kernel_spmd

F32 = mybir.dt.float32
BF16 = mybir.dt.bfloat16
AF = mybir.ActivationFunctionType
ALU = mybir.AluOpType
NPBF = ml_dtypes.bfloat16

D = 2048
NMEM = 256
DFF = 5632
EPS = 1e-6


class Buf:
    __slots__ = ("t", "w", "r", "dkey", "name")

    def __init__(self, t, name):
        self.t = t
        self.w = None
        self.r = {}
        self.dkey = None
        self.name = name

    def __getitem__(self, idx):
        return self.t[idx]


class StopBuild(Exception):
    pass


class K:
    stop_at = None
    nbar = 0
    stopped = False

    def __init__(self, nc, stack, n_dsem=40):
        self.nc = nc
        self.h = dict(pe=nc.tensor, act=nc.scalar, dve=nc.vector, pool=nc.gpsimd, sp=nc.sync)
        self.sem = {}
        self.latest = {}
        self.seen = {e: {} for e in self.h}
        self.cnt = {e: 0 for e in self.h}
        self.bufs = []
        self.uid = 0
        for e in ("pe", "act", "dve", "pool"):
            self.sem[e] = stack.enter_context(nc.semaphore("s_" + e))
            self.latest[e] = 0
        self.dkeys = []
        for i in range(n_dsem):
            k = "d%d" % i
            self.sem[k] = stack.enter_context(nc.semaphore("s_" + k))
            self.latest[k] = 0
            self.dkeys.append(k)
        self.dset = set(self.dkeys)
        self.dnext = 0
        self.pe_pending = False

    def sb(self, stack, name, shape, dt):
        self.uid += 1
        t = stack.enter_context(self.nc.sbuf_tensor("%s_%d" % (name, self.uid), list(shape), dt))
        b = Buf(t, name)
        self.bufs.append(b)
        return b

    def wrap(self, t, name):
        b = Buf(t, name)
        self.bufs.append(b)
        return b

    def _dkey(self, b):
        if b.dkey is None:
            assert self.dnext < len(self.dkeys), "out of dma semaphores"
            b.dkey = self.dkeys[self.dnext]
            self.dnext += 1
        return b.dkey

    def _add(self, need, tok, eng, raw):
        k, v = tok
        if k == eng and not raw:
            return
        if k in self.dset:
            v = self.latest[k]
        if need.get(k, 0) < v:
            need[k] = v

    def _emit_waits(self, eng, reads, writes):
        need = {}
        for b in reads:
            if b.w is not None:
                self._add(need, b.w, eng, True)
        for b in writes:
            if b.w is not None:
                self._add(need, b.w, eng, False)
            for k, v in b.r.items():
                self._add(need, (k, v), eng, False)
        h = self.h[eng]
        seen = self.seen[eng]
        for k, v in need.items():
            if seen.get(k, 0) < v:
                h.wait_ge(self.sem[k], v)
                seen[k] = v

    def _record(self, tok, reads, writes):
        k, v = tok
        for b in reads:
            if b.r.get(k, 0) < v:
                b.r[k] = v
        for b in writes:
            b.w = tok
            b.r = {}

    def op(self, eng, fn, reads=(), writes=(), inc=True):
        if self.stopped:
            return
        self._emit_waits(eng, reads, writes)
        ins = fn(self.h[eng])
        if inc:
            self.cnt[eng] += 1
            ins.then_inc(self.sem[eng], 1)
            self.latest[eng] = self.cnt[eng]
            tok = (eng, self.cnt[eng])
            if eng == "pe":
                self.pe_pending = False
        else:
            assert eng == "pe"
            tok = (eng, self.cnt[eng] + 1)
            self.pe_pending = True
        self._record(tok, reads, writes)

    def dma(self, q, out_ap, in_ap, sb, reads=(), writes=()):
        if self.stopped:
            return
        key = self._dkey(sb)
        self._emit_waits(q, reads, writes)
        ins = self.h[q].dma_start(out=out_ap, in_=in_ap)
        self.latest[key] += 16
        ins.then_inc(self.sem[key], 16)
        self._record((key, self.latest[key]), reads, writes)

    def barrier(self, final=False):
        if self.stopped and not final:
            return
        assert not self.pe_pending
        self.nbar += 1
        for e, h in self.h.items():
            seen = self.seen[e]
            for k, v in self.latest.items():
                if v > 0 and seen.get(k, 0) < v:
                    h.wait_ge(self.sem[k], v)
                    seen[k] = v
        for b in self.bufs:
            b.w = None
            b.r = {}
            b.dkey = None
        self.bufs = [b for b in self.bufs if b.name.startswith("ps") or b.name.startswith("c_")]
        self.dnext = 0
        if (not final) and self.stop_at is not None and self.nbar >= self.stop_at:
            self.stopped = True


def build(S, DEPTH, taps=()):
    nc = bass.Bass("TRN2", target_bir_lowering=False)
    NT = S // 512
    TB = min(2048, S)
    NKB = S // 128
    NEVEN = (DEPTH + 1) // 2
    NODD = DEPTH // 2

    def din(name, shape, dt=F32):
        return nc.dram_tensor(name, list(shape), dt, kind="ExternalInput").ap()

    def dscr(name, shape, dt):
        kind = "ExternalOutput" if name in taps else "Internal"
        return nc.dram_tensor(name, list(shape), dt, kind=kind).ap()

    xT = din("xT", [D, S])
    memT = din("memT", [D, NMEM])
    maskb = din("maskb", [128, S])
    kmask = din("kmask", [128, NKB])
    cosS = din("cosS", [S, S], BF16)
    sinS = din("sinS", [S, S], BF16)
    cosC = din("cosC", [256, 256], BF16)
    nsinC = din("nsinC", [256, 256], BF16)
    bkt = din("bkt", [128, 6 * 512])
    trimats = din("trimats", [64, 4 * 64])
    ng = din("ng", [128, DEPTH * 7 * 16])
    relb = din("relb", [1, 256])
    lamp = din("lamp", [1, max(NEVEN, 1) * 256])
    dng = din("dng", [128, max(NEVEN, 1)])
    gng = din("gng", [128, max(NODD, 1) * 4])
    convw = din("convw", [128, DEPTH * 3 * 88])
    convb = din("convb", [128, DEPTH * 88])
    w_in_even = din("w_in_even", [max(NEVEN, 1), D, 4096])
    w_out_even = din("w_out_even", [max(NEVEN, 1), D, D])
    w_in_odd = din("w_in_odd", [max(NODD, 1), D, 6144])
    w_gd = din("w_gd", [max(NODD, 1), D, 32])
    w_gu = din("w_gu", [max(NODD, 1) * 2 * 32, 1024])
    w_out_odd = din("w_out_odd", [max(NODD, 1), D, D])
    w_xq = din("w_xq", [DEPTH, D, D])
    w_xkv = din("w_xkv", [DEPTH, D, 2 * D])
    w_xo = din("w_xo", [DEPTH, D, D])
    w_up = din("w_up", [DEPTH, D, 2 * DFF])
    w_down = din("w_down", [DEPTH, DFF, D])

    yT = nc.dram_tensor("yT", [D, S], F32, kind="ExternalOutput").ap()

    b_in_even = dscr("b_in_even", [max(NEVEN, 1), D, 4096], BF16)
    b_out_even = dscr("b_out_even", [max(NEVEN, 1), D, D], BF16)
    b_in_odd = dscr("b_in_odd", [max(NODD, 1), D, 6144], BF16)
    b_gd = dscr("b_gd", [max(NODD, 1), D, 32], BF16)
    b_gu = dscr("b_gu", [max(NODD, 1) * 2 * 32, 1024], BF16)
    b_out_odd = dscr("b_out_odd", [max(NODD, 1), D, D], BF16)
    b_xq = dscr("b_xq", [DEPTH, D, D], BF16)
    b_xkv = dscr("b_xkv", [DEPTH, D, 2 * D], BF16)
    b_xo = dscr("b_xo", [DEPTH, D, D], BF16)
    b_up = dscr("b_up", [DEPTH, D, 2 * DFF], BF16)
    b_down = dscr("b_down", [DEPTH, DFF, D], BF16)

    H = dscr("H", [D, S], BF16)
    Y = dscr("Y", [D, S], F32)
    QT = dscr("QT", [1024, S], BF16)
    KT = dscr("KT", [1024, S], BF16)
    Vt = dscr("Vt", [S, 1024], BF16)
    Ut = dscr("Ut", [S, 1024], BF16)
    CAT = dscr("CAT", [D, S], BF16)
    BT = dscr("BT", [8, 128, 6 * 512], F32)
    QgT = dscr("QgT", [1024, S], BF16)
    KgT = dscr("KgT", [1024, S], BF16)
    Kg = dscr("Kg", [S, 1024], BF16)
    Vg = dscr("Vg", [S, 2048], BF16)
    GH = dscr("GH", [2, S, 1024], BF16)
    GL = dscr("GL", [2, S, 1024], BF16)
    HDT = dscr("HDT", [32, S], BF16)
    RS = dscr("RS", [D, S], BF16)
    OG = dscr("OG", [2, D, S], F32)
    KxT = dscr("KxT", [D, NMEM], BF16)
    Vx = dscr("Vx", [NMEM, D], BF16)
    MN = dscr("MN", [D, NMEM], BF16)
    QxT = dscr("QxT", [D, S], BF16)
    OxT = dscr("OxT", [D, S], BF16)
    GG = dscr("GG", [DFF, S], BF16)

    with contextlib.ExitStack() as top:
        k = K(nc, top)
        PS = []
        for i in range(8):
            t = top.enter_context(nc.psum_tensor("psb%d" % i, [128, 512], F32))
            PS.append(k.wrap(t, "ps%d" % i))
        psrr = [0]

        def ps_next():
            p = PS[psrr[0] % 8]
            psrr[0] += 1
            return p

        prr = [0]

        def ps_pick(lo, hi):
            p = PS[lo + prr[0] % (hi - lo)]
            prr[0] += 1
            return p

        c_ones = k.sb(top, "c_ones", [128, 128], BF16)
        c_ng = k.sb(top, "c_ng", [128, DEPTH * 7 * 16], F32)
        c_tri = k.sb(top, "c_tri", [64, 256], F32)
        c_trib = k.sb(top, "c_trib", [64, 256], BF16)
        c_relb = k.sb(top, "c_relb", [128, 256], F32)
        c_kmask = k.sb(top, "c_kmask", [128, NKB], F32)
        c_far = k.sb(top, "c_far", [128, 16 * NKB], F32)
        c_lam = k.sb(top, "c_lam", [128, max(NEVEN, 1)], F32)
        c_dng = k.sb(top, "c_dng", [128, max(NEVEN, 1)], F32)
        c_gng = k.sb(top, "c_gng", [128, max(NODD, 1) * 4], F32)
        c_cw = k.sb(top, "c_cw", [128, DEPTH * 3 * 88], F32)
        c_cb = k.sb(top, "c_cb", [128, DEPTH * 88], F32)
        c_tmp = k.sb(top, "c_tmp", [128, 256], F32)
        c_tmp2 = k.sb(top, "c_tmp2", [128, 4], F32)

        k.op("dve", lambda e: e.memset(c_ones[:], 1.0), writes=[c_ones])
        k.dma("sp", c_ng[:], ng, c_ng, writes=[c_ng])
        k.dma("sp", c_tri[:], trimats, c_tri, writes=[c_tri])
        k.dma("sp", c_relb[:], relb.partition_broadcast(128), c_relb, writes=[c_relb])
        k.dma("sp", c_kmask[:], kmask, c_kmask, writes=[c_kmask])
        k.dma("sp", c_dng[:], dng, c_dng, writes=[c_dng])
        k.dma("sp", c_gng[:], gng, c_gng, writes=[c_gng])
        k.dma("sp", c_cw[:], convw, c_cw, writes=[c_cw])
        k.dma("sp", c_cb[:], convb, c_cb, writes=[c_cb])
        k.op("dve", lambda e: e.tensor_copy(out=c_trib[:], in_=c_tri[:]), reads=[c_tri], writes=[c_trib])
        for hh in range(8):
            for side in range(2):
                col = (15 + 16 * side) * 8 + hh
                o0 = (hh * 2 + side) * NKB
                k.op("dve", lambda e, o0=o0, col=col: e.tensor_scalar(
                    out=c_far[:, o0:o0 + NKB], in0=c_kmask[:], scalar1=c_relb[:, col:col + 1], scalar2=None,
                    op0=ALU.add), reads=[c_kmask, c_relb], writes=[c_far])
        for i in range(NEVEN):
            lam_init = 0.8 - 0.6 * math.exp(-0.3 * (2 * i))
            k.dma("sp", c_tmp[:], lamp[:, i * 256:(i + 1) * 256].partition_broadcast(128), c_tmp, writes=[c_tmp])
            k.op("dve", lambda e: e.tensor_tensor(out=c_tmp[:, 0:64], in0=c_tmp[:, 0:64], in1=c_tmp[:, 64:128],
                                                  op=ALU.mult), reads=[c_tmp], writes=[c_tmp])
            k.op("dve", lambda e: e.tensor_tensor(out=c_tmp[:, 128:192], in0=c_tmp[:, 128:192], in1=c_tmp[:, 192:256],
                                                  op=ALU.mult), reads=[c_tmp], writes=[c_tmp])
            k.op("dve", lambda e: e.reduce_sum(out=c_tmp2[:, 0:1], in_=c_tmp[:, 0:64], axis=mybir.AxisListType.X),
                 reads=[c_tmp], writes=[c_tmp2])
            k.op("dve", lambda e: e.reduce_sum(out=c_tmp2[:, 1:2], in_=c_tmp[:, 128:192], axis=mybir.AxisListType.X),
                 reads=[c_tmp], writes=[c_tmp2])
            k.op("act", lambda e: e.activation(out=c_tmp2[:, 2:4], in_=c_tmp2[:, 0:2], func=AF.Exp),
                 reads=[c_tmp2], writes=[c_tmp2])
            k.op("dve", lambda e, i=i, lam_init=lam_init: e.scalar_tensor_tensor(
                out=c_lam[:, i:i + 1], in0=c_tmp2[:, 3:4], scalar=-lam_init, in1=c_tmp2[:, 2:3],
                op0=ALU.add, op1=ALU.subtract), reads=[c_tmp2], writes=[c_lam])

        def ngcol(l, j, c):
            o = (l * 7 + j) * 16 + c
            return c_ng[:, o:o + 1]

        def cast_weight(src, dst, Kr, Nc, rr):
            with contextlib.ExitStack() as st:
                CW = min(2048, Nc)
                fin = [k.sb(st, "cwf", [128, CW], F32) for _ in range(3)]
                fo = [k.sb(st, "cwb", [128, CW], BF16) for _ in range(3)]
                i = 0
                for r0 in range(0, Kr, 128):
                    rows = min(128, Kr - r0)
                    for c0 in range(0, Nc, CW):
                        cw = min(CW, Nc - c0)
                        a, b = fin[i % 3], fo[i % 3]
                        k.dma("sp", a[0:rows, 0:cw], src[r0:r0 + rows, c0:c0 + cw], a, writes=[a])
                        eng = ("pool", "dve", "act")[rr[0] % 3]
                        rr[0] += 1
                        if eng == "act":
                            k.op("act", lambda e, a=a, b=b, rows=rows, cw=cw: e.copy(out=b[0:rows, 0:cw], in_=a[0:rows, 0:cw]),
                                 reads=[a], writes=[b])
                        else:
                            k.op(eng, lambda e, a=a, b=b, rows=rows, cw=cw: e.tensor_copy(out=b[0:rows, 0:cw], in_=a[0:rows, 0:cw]),
                                 reads=[a], writes=[b])
                        k.dma("pool", dst[r0:r0 + rows, c0:c0 + cw], b[0:rows, 0:cw], b, reads=[b])
                        i += 1
            k.barrier()

        rr = [0]
        for i in range(NEVEN):
            cast_weight(w_in_even[i], b_in_even[i], D, 4096, rr)
            cast_weight(w_out_even[i], b_out_even[i], D, D, rr)
        for i in range(NODD):
            cast_weight(w_in_odd[i], b_in_odd[i], D, 6144, rr)
            cast_weight(w_gd[i], b_gd[i], D, 32, rr)
            cast_weight(w_out_odd[i], b_out_odd[i], D, D, rr)
        if NODD:
            cast_weight(w_gu, b_gu, NODD * 64, 1024, rr)
        for l in range(DEPTH):
            cast_weight(w_xq[l], b_xq[l], D, D, rr)
            cast_weight(w_xkv[l], b_xkv[l], D, 2 * D, rr)
            cast_weight(w_xo[l], b_xo[l], D, D, rr)
            cast_weight(w_up[l], b_up[l], D, 2 * DFF, rr)
            cast_weight(w_down[l], b_down[l], DFF, D, rr)

        if NEVEN:
            with contextlib.ExitStack() as st:
                bk = k.sb(st, "bk", [128, 3072], F32)
                acc = k.sb(st, "bacc", [128, 3072], F32)
                tmpb = [k.sb(st, "btmp", [128, 3072], F32) for _ in range(2)]
                k.dma("sp", bk[:], bkt, bk, writes=[bk])
                for hh in range(8):
                    for b in range(32):
                        col = b * 8 + hh
                        t = tmpb[b % 2]
                        if b == 0:
                            k.op("dve", lambda e, col=col, b=b: e.tensor_scalar(
                                out=acc[:], in0=bk[:], scalar1=float(b), scalar2=c_relb[:, col:col + 1],
                                op0=ALU.is_equal, op1=ALU.mult), reads=[bk, c_relb], writes=[acc])
                        else:
                            k.op("dve", lambda e, col=col, b=b, t=t: e.tensor_scalar(
                                out=t[:], in0=bk[:], scalar1=float(b), scalar2=c_relb[:, col:col + 1],
                                op0=ALU.is_equal, op1=ALU.mult), reads=[bk, c_relb], writes=[t])
                            k.op("pool", lambda e, t=t: e.tensor_tensor(out=acc[:], in0=acc[:], in1=t[:], op=ALU.add),
                                 reads=[acc, t], writes=[acc])
                    k.dma("pool", BT[hh], acc[:], acc, reads=[acc])
            k.barrier()

        def linear_fm(inT, Kdim, W, col_list, epi, tb=None, gw=512, extra_tok=None):
            KC = Kdim // 128
            TBs = tb or TB
            nper = gw // 128
            with contextlib.ExitStack() as st:
                act = k.sb(st, "lin_act", [128, KC, TBs], BF16)
                wts = [k.sb(st, "lin_w", [128, KC, gw], BF16) for _ in range(2)]
                Wv = W.rearrange("(kc p) n -> p kc n", p=128)
                inv = inT.rearrange("(kc p) s -> p kc s", p=128)
                groups = []
                i = 0
                while i < len(col_list):
                    j = i
                    while j + 1 < len(col_list) and j + 1 - i < nper and col_list[j + 1][0] == col_list[j][0] + 128:
                        j += 1
                    groups.append(col_list[i:j + 1])
                    i = j + 1
                gi = 0
                for tb0 in range(0, S, TBs):
                    hk = KC // 2
                    k.dma("sp", act[:, 0:hk, :], inv[:, 0:hk, tb0:tb0 + TBs], act, writes=[act])
                    k.dma("sp", act[:, hk:KC, :], inv[:, hk:KC, tb0:tb0 + TBs], act, writes=[act])
                    for grp in groups:
                        wt = wts[gi % 2]
                        gi += 1
                        c0 = grp[0][0]
                        gwid = 128 * len(grp)
                        k.dma("sp", wt[:, :, 0:gwid], Wv[:, :, c0:c0 + gwid], wt, writes=[wt])
                        for bi, (cc, tag) in enumerate(grp):
                            ntile = TBs // 512
                            pss_ = [ps_next() for _ in range(ntile)]
                            for kc in range(KC):
                                for tt in range(ntile):
                                    ps = pss_[tt]
                                    k.op("pe", lambda e, ps=ps, wt=wt, kc=kc, bi=bi, tt=tt: e.matmul(
                                        ps[:], lhsT=wt[:, kc, bi * 128:(bi + 1) * 128], rhs=act[:, kc, tt * 512:(tt + 1) * 512],
                                        start=(kc == 0), stop=(kc == KC - 1)),
                                        reads=[wt, act], writes=[ps], inc=(kc == KC - 1))
                            for tt in range(ntile):
                                epi(pss_[tt], tag, tb0 + tt * 512, 512, tt)
                            epi(None, tag, tb0, TBs, -1)

        def linear_tm(inT, Kdim, W, c0, ncols, epi, gw=512):
            KC = Kdim // 128
            with contextlib.ExitStack() as st:
                act = k.sb(st, "ltm_act", [128, KC, TB], BF16)
                wts = [k.sb(st, "ltm_w", [128, KC, gw], BF16) for _ in range(2)]
                Wv = W.rearrange("(kc p) n -> p kc n", p=128)
                inv = inT.rearrange("(kc p) s -> p kc s", p=128)
                gi = 0
                for tb0 in range(0, S, TB):
                    k.dma("sp", act[:], inv[:, :, tb0:tb0 + TB], act, writes=[act])
                    for g0 in range(0, ncols, gw):
                        gwid = min(gw, ncols - g0)
                        wt = wts[gi % 2]
                        gi += 1
                        k.dma("sp", wt[:, :, 0:gwid], Wv[:, :, c0 + g0:c0 + g0 + gwid], wt, writes=[wt])
                        for tk in range(TB // 128):
                            ps = ps_next()
                            for kc in range(KC):
                                k.op("pe", lambda e, ps=ps, wt=wt, kc=kc, tk=tk, gwid=gwid: e.matmul(
                                    ps[:, 0:gwid], lhsT=act[:, kc, tk * 128:(tk + 1) * 128], rhs=wt[:, kc, 0:gwid],
                                    start=(kc == 0), stop=(kc == KC - 1)),
                                    reads=[wt, act], writes=[ps], inc=(kc == KC - 1))
                            epi(ps, g0, gwid, tb0 + tk * 128)
            k.barrier()

        def rstd_from(ps, n, dim, out_buf, tmp_buf):
            k.op("dve", lambda e: e.tensor_scalar(out=tmp_buf[:, 0:n], in0=ps[:, 0:n], scalar1=1.0 / dim, scalar2=EPS,
                                                  op0=ALU.mult, op1=ALU.add), reads=[ps], writes=[tmp_buf])
            k.op("act", lambda e: e.activation(out=tmp_buf[:, 0:n], in_=tmp_buf[:, 0:n], func=AF.Ln),
                 reads=[tmp_buf], writes=[tmp_buf])
            k.op("act", lambda e: e.activation(out=out_buf[:, 0:n], in_=tmp_buf[:, 0:n], func=AF.Exp, scale=-0.5),
                 reads=[tmp_buf], writes=[out_buf])

        def norm_pass(l_post, j_post, Ysrc, l_pre, j_pre, Xsrc=None):
            Xs = Xsrc if Xsrc is not None else yT
            with contextlib.ExitStack() as st:
                xt = [k.sb(st, "np_x", [128, 16, 512], F32) for _ in range(2)]
                yt = [k.sb(st, "np_y", [128, 16, 512], F32) for _ in range(1)] if Ysrc is not None else None
                sq = k.sb(st, "np_sq", [128, 16, 512], BF16)
                ht = [k.sb(st, "np_h", [128, 16, 512], BF16) for _ in range(1)] if j_pre is not None else None
                mk = [k.sb(st, "np_m", [128, 512], F32) for _ in range(2)]
                r1 = k.sb(st, "np_r1", [128, 512], F32)
                r2 = k.sb(st, "np_r2", [128, 512], F32)
                tmp = k.sb(st, "np_t", [128, 512], F32)
                xv = Xs.rearrange("(c p) s -> p c s", p=128)
                xo = yT.rearrange("(c p) s -> p c s", p=128)
                hv = H.rearrange("(c p) s -> p c s", p=128)
                for ti in range(NT):
                    t0 = ti * 512
                    x = xt[ti % 2]
                    k.dma("sp", x[:], xv[:, :, t0:t0 + 512], x, writes=[x])
                    if Ysrc is not None:
                        y = yt[0]
                        m = mk[ti % 2]
                        yv = Ysrc.rearrange("(c p) s -> p c s", p=128)
                        k.dma("sp", y[:], yv[:, :, t0:t0 + 512], y, writes=[y])
                        k.dma("sp", m[:], maskb[:, t0:t0 + 512], m, writes=[m])
                        k.op("act", lambda e, y=y: e.activation(out=sq[:], in_=y[:], func=AF.Square), reads=[y], writes=[sq])
                        ps = ps_next()
                        for c in range(16):
                            k.op("pe", lambda e, ps=ps, c=c: e.matmul(ps[:], lhsT=c_ones[:], rhs=sq[:, c, :], start=(c == 0), stop=(c == 15)),
                                 reads=[c_ones, sq], writes=[ps], inc=(c == 15))
                        rstd_from(ps, 512, D, r1, tmp)
                        k.op("dve", lambda e, m=m: e.tensor_tensor(out=r1[:], in0=r1[:], in1=m[:], op=ALU.mult), reads=[r1, m], writes=[r1])
                        for c in range(16):
                            k.op("dve", lambda e, y=y, c=c: e.scalar_tensor_tensor(
                                out=y[:, c, :], in0=y[:, c, :], scalar=ngcol(l_post, j_post, c), in1=r1[:],
                                op0=ALU.mult, op1=ALU.mult), reads=[y, r1, c_ng], writes=[y])
                        k.op("pool", lambda e, x=x, y=y: e.tensor_tensor(out=x[:], in0=x[:], in1=y[:], op=ALU.add), reads=[x, y], writes=[x])
                    if Ysrc is not None or Xsrc is not None:
                        k.dma("pool", xo[:, :, t0:t0 + 512], x[:], x, reads=[x])
                    if j_pre is not None:
                        h = ht[0]
                        k.op("act", lambda e, x=x: e.activation(out=sq[:], in_=x[:], func=AF.Square), reads=[x], writes=[sq])
                        ps = ps_next()
                        for c in range(16):
                            k.op("pe", lambda e, ps=ps, c=c: e.matmul(ps[:], lhsT=c_ones[:], rhs=sq[:, c, :], start=(c == 0), stop=(c == 15)),
                                 reads=[c_ones, sq], writes=[ps], inc=(c == 15))
                        rstd_from(ps, 512, D, r2, tmp)
                        for c in range(16):
                            eng = "dve" if c % 2 == 0 else "dve"
                            k.op(eng, lambda e, x=x, h=h, c=c: e.scalar_tensor_tensor(
                                out=h[:, c, :], in0=x[:, c, :], scalar=ngcol(l_pre, j_pre, c), in1=r2[:],
                                op0=ALU.mult, op1=ALU.mult), reads=[x, r2, c_ng], writes=[h])
                        k.dma("pool", hv[:, :, t0:t0 + 512], h[:], h, reads=[h])
            k.barrier()

        def make_store_epi(st, dst_fn, dt, TBs, scale=None, func=None):
            stg = [k.sb(st, "epi_stg", [128, TBs], dt) for _ in range(2)]
            state = {"i": 0, "rr": 0}

            def epi(ps, tag, tok0, ntok, tt):
                sg = stg[state["i"] % 2]
                if ps is not None:
                    o0 = tt * 512
                    use_act = (func is not None) or (state["rr"] % 2 == 0)
                    state["rr"] += 1
                    if use_act:
                        f = func if func is not None else AF.Copy
                        if scale is not None:
                            k.op("act", lambda e: e.activation(out=sg[:, o0:o0 + 512], in_=ps[:], func=f, scale=scale), reads=[ps], writes=[sg])
                        else:
                            k.op("act", lambda e: e.activation(out=sg[:, o0:o0 + 512], in_=ps[:], func=f), reads=[ps], writes=[sg])
                    else:
                        if scale is not None:
                            k.op("dve", lambda e: e.tensor_scalar(out=sg[:, o0:o0 + 512], in0=ps[:], scalar1=scale, scalar2=None, op0=ALU.mult),
                                 reads=[ps], writes=[sg])
                        else:
                            k.op("dve", lambda e: e.tensor_copy(out=sg[:, o0:o0 + 512], in_=ps[:]), reads=[ps], writes=[sg])
                else:
                    k.dma("pool", dst_fn(tag, tok0, ntok), sg[:, 0:ntok], sg, reads=[sg])
                    state["i"] += 1
            return epi

        def cross_attn(l):
            with contextlib.ExitStack() as st:
                m32 = k.sb(st, "ca_m32", [128, 16, NMEM], F32)
                msq = k.sb(st, "ca_msq", [128, 16, NMEM], BF16)
                mn = k.sb(st, "ca_mn", [128, 16, NMEM], BF16)
                r = k.sb(st, "ca_r", [128, NMEM], F32)
                tmp = k.sb(st, "ca_t", [128, NMEM], F32)
                k.dma("sp", m32[:], memT.rearrange("(c p) m -> p c m", p=128), m32, writes=[m32])
                k.op("act", lambda e: e.activation(out=msq[:], in_=m32[:], func=AF.Square), reads=[m32], writes=[msq])
                ps = ps_next()
                for c in range(16):
                    k.op("pe", lambda e, c=c: e.matmul(ps[:, 0:NMEM], lhsT=c_ones[:], rhs=msq[:, c, :], start=(c == 0), stop=(c == 15)),
                         reads=[c_ones, msq], writes=[ps], inc=(c == 15))
                rstd_from(ps, NMEM, D, r, tmp)
                for c in range(16):
                    k.op("dve", lambda e, c=c: e.scalar_tensor_tensor(out=mn[:, c, :], in0=m32[:, c, :], scalar=ngcol(l, 4, c), in1=r[:],
                                                                      op0=ALU.mult, op1=ALU.mult), reads=[m32, r, c_ng], writes=[mn])
                k.dma("pool", MN.rearrange("(c p) m -> p c m", p=128), mn[:], mn, reads=[mn])
            k.barrier()
            with contextlib.ExitStack() as st:
                mn = k.sb(st, "ca_mn2", [128, 16, NMEM], BF16)
                wts = [k.sb(st, "ca_w", [128, 16, 512], BF16) for _ in range(2)]
                stg = [k.sb(st, "ca_s", [128, 512], BF16) for _ in range(2)]
                k.dma("sp", mn[:], MN.rearrange("(c p) m -> p c m", p=128), mn, writes=[mn])
                Wv = b_xkv[l].rearrange("(kc p) n -> p kc n", p=128)
                si = 0
                for g in range(8):
                    wt = wts[g % 2]
                    k.dma("sp", wt[:], Wv[:, :, g * 512:(g + 1) * 512], wt, writes=[wt])
                    if g < 4:
                        for bi in range(4):
                            ps = ps_next()
                            for kc in range(16):
                                k.op("pe", lambda e, ps=ps, wt=wt, kc=kc, bi=bi: e.matmul(
                                    ps[:, 0:NMEM], lhsT=wt[:, kc, bi * 128:(bi + 1) * 128], rhs=mn[:, kc, :], start=(kc == 0), stop=(kc == 15)),
                                    reads=[wt, mn], writes=[ps], inc=(kc == 15))
                            sg = stg[si % 2]
                            si += 1
                            k.op("act", lambda e, ps=ps, sg=sg: e.copy(out=sg[:, 0:NMEM], in_=ps[:, 0:NMEM]), reads=[ps], writes=[sg])
                            n0 = g * 512 + bi * 128
                            k.dma("pool", KxT[n0:n0 + 128, :], sg[:, 0:NMEM], sg, reads=[sg])
                    else:
                        for mb in range(2):
                            ps = ps_next()
                            for kc in range(16):
                                k.op("pe", lambda e, ps=ps, wt=wt, kc=kc, mb=mb: e.matmul(
                                    ps[:], lhsT=mn[:, kc, mb * 128:(mb + 1) * 128], rhs=wt[:, kc, :], start=(kc == 0), stop=(kc == 15)),
                                    reads=[wt, mn], writes=[ps], inc=(kc == 15))
                            sg = stg[si % 2]
                            si += 1
                            k.op("act", lambda e, ps=ps, sg=sg: e.copy(out=sg[:], in_=ps[:]), reads=[ps], writes=[sg])
                            e0 = (g - 4) * 512
                            k.dma("pool", Vx[mb * 128:(mb + 1) * 128, e0:e0 + 512], sg[:], sg, reads=[sg])
            k.barrier()
            with contextlib.ExitStack() as st:
                epi = make_store_epi(st, lambda tag, t0, n: QxT[tag * 128:(tag + 1) * 128, t0:t0 + n], BF16, TB, scale=512 ** -0.5)
                linear_fm(H, D, b_xq[l], [(i * 128, i) for i in range(16)], epi)
            k.barrier()
            with contextlib.ExitStack() as st:
                kx = k.sb(st, "ca_kx", [128, 16, NMEM], BF16)
                vx = k.sb(st, "ca_vx", [128, 2, D], BF16)
                qs = [k.sb(st, "ca_q", [128, 16, 512], BF16) for _ in range(2)]
                es = [k.sb(st, "ca_e", [128, 2, 512], BF16) for _ in range(2)]
                rz = [k.sb(st, "ca_rz", [128, 512], F32) for _ in range(2)]
                ot = [k.sb(st, "ca_o", [128, 16, 512], BF16) for _ in range(2)]
                k.dma("sp", kx[:], KxT.rearrange("(c p) m -> p c m", p=128), kx, writes=[kx])
                k.dma("sp", vx[:], Vx.rearrange("(mb p) e -> p mb e", p=128), vx, writes=[vx])
                qv = QxT.rearrange("(c p) s -> p c s", p=128)
                ov = OxT.rearrange("(c p) s -> p c s", p=128)
                ei = 0
                for ti in range(NT):
                    t0 = ti * 512
                    q = qs[ti % 2]
                    o = ot[ti % 2]
                    k.dma("sp", q[:], qv[:, :, t0:t0 + 512], q, writes=[q])
                    for hh in range(4):
                        ee = es[ei % 2]
                        rzz = rz[ei % 2]
                        ei += 1
                        for mb in range(2):
                            ps = ps_next()
                            for dc in range(4):
                                k.op("pe", lambda e, ps=ps, dc=dc, mb=mb, hh=hh: e.matmul(
                                    ps[:], lhsT=kx[:, hh * 4 + dc, mb * 128:(mb + 1) * 128], rhs=q[:, hh * 4 + dc, :],
                                    start=(dc == 0), stop=(dc == 3)), reads=[kx, q], writes=[ps], inc=(dc == 3))
                            k.op("act", lambda e, ps=ps, ee=ee, mb=mb: e.activation(out=ee[:, mb, :], in_=ps[:], func=AF.Exp),
                                 reads=[ps], writes=[ee])
                        pz = ps_next()
                        for mb in range(2):
                            k.op("pe", lambda e, pz=pz, ee=ee, mb=mb: e.matmul(pz[:], lhsT=c_ones[:], rhs=ee[:, mb, :], start=(mb == 0), stop=(mb == 1)),
                                 reads=[c_ones, ee], writes=[pz], inc=(mb == 1))
                        k.op("dve", lambda e, pz=pz, rzz=rzz: e.reciprocal(out=rzz[:], in_=pz[:]), reads=[pz], writes=[rzz])
                        for eb in range(4):
                            po = ps_next()
                            for mb in range(2):
                                k.op("pe", lambda e, po=po, ee=ee, mb=mb, eb=eb, hh=hh: e.matmul(
                                    po[:], lhsT=vx[:, mb, hh * 512 + eb * 128:hh * 512 + (eb + 1) * 128], rhs=ee[:, mb, :],
                                    start=(mb == 0), stop=(mb == 1)), reads=[vx, ee], writes=[po], inc=(mb == 1))
                            k.op("dve", lambda e, po=po, o=o, rzz=rzz, eb=eb, hh=hh: e.tensor_tensor(
                                out=o[:, hh * 4 + eb, :], in0=po[:], in1=rzz[:], op=ALU.mult), reads=[po, rzz], writes=[o])
                    k.dma("pool", ov[:, :, t0:t0 + 512], o[:], o, reads=[o])
            k.barrier()
            with contextlib.ExitStack() as st:
                epi = make_store_epi(st, lambda tag, t0, n: Y[tag * 128:(tag + 1) * 128, t0:t0 + n], F32, TB)
                linear_fm(OxT, D, b_xo[l], [(i * 128, i) for i in range(16)], epi)
            k.barrier()

        def ffn(l):
            TBF = min(1024, S)
            with contextlib.ExitStack() as st:
                KC = 16
                act = k.sb(st, "ff_act", [128, KC, TBF + 2], BF16)
                wa = [k.sb(st, "ff_wa", [128, KC, 128], BF16) for _ in range(2)]
                wg = [k.sb(st, "ff_wg", [128, KC, 128], BF16) for _ in range(2)]
                uas = [k.sb(st, "ff_ua", [128, TBF + 2], F32) for _ in range(2)]
                ugs = [k.sb(st, "ff_ug", [128, TBF + 2], F32) for _ in range(2)]
                cas = [k.sb(st, "ff_ca", [128, TBF], F32) for _ in range(2)]
                cgs = [k.sb(st, "ff_cg", [128, TBF], F32) for _ in range(2)]
                t1s = [k.sb(st, "ff_t1", [128, TBF], F32) for _ in range(2)]
                og = [k.sb(st, "ff_og", [128, TBF], BF16) for _ in range(2)]
                Wv = b_up[l].rearrange("(kc p) n -> p kc n", p=128)
                inv = H.rearrange("(kc p) s -> p kc s", p=128)
                it = 0
                for tb0 in range(0, S, TBF):
                    lo = 1 if tb0 == 0 else 0
                    hi = TBF + 1 if tb0 + TBF >= S else TBF + 2
                    if lo == 1:
                        k.op("pool", lambda e: e.memset(act[:, :, 0:1], 0.0), writes=[act])
                    if hi == TBF + 1:
                        k.op("pool", lambda e: e.memset(act[:, :, TBF + 1:TBF + 2], 0.0), writes=[act])
                    k.dma("sp", act[:, :, lo:hi], inv[:, :, tb0 - 1 + lo:tb0 - 1 + hi], act, writes=[act])
                    for nb in range(44):
                        wta, wtg = wa[it % 2], wg[it % 2]
                        o = og[it % 2]
                        ua, ug, ca, cg, t1 = uas[it % 2], ugs[it % 2], cas[it % 2], cgs[it % 2], t1s[it % 2]
                        it += 1
                        k.dma("sp", wta[:], Wv[:, :, nb * 128:(nb + 1) * 128], wta, writes=[wta])
                        k.dma("sp", wtg[:], Wv[:, :, DFF + nb * 128:DFF + (nb + 1) * 128], wtg, writes=[wtg])
                        for (wt, ub, engs) in ((wta, ua, "act"), (wtg, ug, "dve")):
                            tiles = []
                            c0 = 0
                            while c0 < TBF + 2:
                                n = min(512, TBF + 2 - c0)
                                tiles.append((c0, n, ps_next()))
                                c0 += n
                            for kc in range(KC):
                                for (c0, n, ps) in tiles:
                                    k.op("pe", lambda e, ps=ps, wt=wt, kc=kc, c0=c0, n=n: e.matmul(
                                        ps[:, 0:n], lhsT=wt[:, kc, :], rhs=act[:, kc, c0:c0 + n], start=(kc == 0), stop=(kc == KC - 1)),
                                        reads=[wt, act], writes=[ps], inc=(kc == KC - 1))
                            for (c0, n, ps) in tiles:
                                if engs == "act":
                                    k.op("act", lambda e, ps=ps, ub=ub, c0=c0, n=n: e.copy(out=ub[:, c0:c0 + n], in_=ps[:, 0:n]), reads=[ps], writes=[ub])
                                else:
                                    k.op("dve", lambda e, ps=ps, ub=ub, c0=c0, n=n: e.tensor_copy(out=ub[:, c0:c0 + n], in_=ps[:, 0:n]), reads=[ps], writes=[ub])
                        for (ub, cb, f0, eng) in ((ua, ca, nb, "pool"), (ug, cg, 44 + nb, "dve")):
                            w0 = c_cw[:, (l * 3 + 0) * 88 + f0:(l * 3 + 0) * 88 + f0 + 1]
                            w1 = c_cw[:, (l * 3 + 1) * 88 + f0:(l * 3 + 1) * 88 + f0 + 1]
                            w2 = c_cw[:, (l * 3 + 2) * 88 + f0:(l * 3 + 2) * 88 + f0 + 1]
                            bb = c_cb[:, l * 88 + f0:l * 88 + f0 + 1]
                            k.op("dve", lambda e, ub=ub, cb=cb, w1=w1, bb=bb: e.tensor_scalar(
                                out=cb[:], in0=ub[:, 1:TBF + 1], scalar1=w1, scalar2=bb, op0=ALU.mult, op1=ALU.add),
                                reads=[ub, c_cw, c_cb], writes=[cb])
                            k.op("dve", lambda e, ub=ub, cb=cb, w0=w0: e.scalar_tensor_tensor(
                                out=cb[:], in0=ub[:, 0:TBF], scalar=w0, in1=cb[:], op0=ALU.mult, op1=ALU.add),
                                reads=[ub, cb, c_cw], writes=[cb])
                            k.op("dve", lambda e, ub=ub, cb=cb, w2=w2: e.scalar_tensor_tensor(
                                out=cb[:], in0=ub[:, 2:TBF + 2], scalar=w2, in1=cb[:], op0=ALU.mult, op1=ALU.add),
                                reads=[ub, cb, c_cw], writes=[cb])
                        k.op("act", lambda e: e.activation(out=t1[:], in_=cg[:], func=AF.Square), reads=[cg], writes=[t1])
                        k.op("pool", lambda e: e.tensor_scalar(out=t1[:], in0=t1[:], scalar1=0.044715, scalar2=1.0, op0=ALU.mult, op1=ALU.add),
                             reads=[t1], writes=[t1])
                        k.op("pool", lambda e: e.tensor_tensor(out=t1[:], in0=t1[:], in1=cg[:], op=ALU.mult), reads=[t1, cg], writes=[t1])
                        k.op("act", lambda e: e.activation(out=t1[:], in_=t1[:], func=AF.Sigmoid, scale=1.5957691216057308), reads=[t1], writes=[t1])
                        k.op("pool", lambda e: e.tensor_tensor(out=t1[:], in0=t1[:], in1=cg[:], op=ALU.mult), reads=[t1, cg], writes=[t1])
                        k.op("pool", lambda e, o=o: e.tensor_tensor(out=o[:], in0=t1[:], in1=ca[:], op=ALU.mult), reads=[t1, ca], writes=[o])
                        k.dma("pool", GG[nb * 128:(nb + 1) * 128, tb0:tb0 + TBF], o[:], o, reads=[o])
            k.barrier()
            with contextlib.ExitStack() as st:
                epi = make_store_epi(st, lambda tag, t0, n: Y[tag * 128:(tag + 1) * 128, t0:t0 + n], F32, TBF)
                linear_fm(GG, DFF, b_down[l], [(i * 128, i) for i in range(16)], epi, tb=TBF, gw=256)
            k.barrier()

        def even_mixer(l):
            i = l // 2
            lam_init = 0.8 - 0.6 * math.exp(-0.3 * l)
            with contextlib.ExitStack() as st:
                stg = [k.sb(st, "em_s", [128, 512], BF16) for _ in range(3)]
                si = [0]

                def epi_tm(dst):
                    def epi(ps, g0, gwid, tok0):
                        sg = stg[si[0] % 3]
                        si[0] += 1
                        if si[0] % 2:
                            k.op("act", lambda e: e.copy(out=sg[:, 0:gwid], in_=ps[:, 0:gwid]), reads=[ps], writes=[sg])
                        else:
                            k.op("dve", lambda e: e.tensor_copy(out=sg[:, 0:gwid], in_=ps[:, 0:gwid]), reads=[ps], writes=[sg])
                        k.dma("pool", dst[tok0:tok0 + 128, g0:g0 + gwid], sg[:, 0:gwid], sg, reads=[sg])
                    return epi
                linear_tm(H, D, b_in_even[i], 0, 1024, epi_tm(Ut))
                linear_tm(H, D, b_in_even[i], 3072, 1024, epi_tm(Vt))
            k.barrier()
            with contextlib.ExitStack() as st:
                def dstf(tag, t0, n):
                    return (QT if tag < 8 else KT)[(tag % 8) * 128:(tag % 8 + 1) * 128, t0:t0 + n]
                epi = make_store_epi(st, dstf, BF16, TB)
                linear_fm(H, D, b_in_even[i], [(1024 + j * 128, j) for j in range(16)], epi)
            k.barrier()
            with contextlib.ExitStack() as st:
                NSC = S // 128
                ug = k.sb(st, "fn_u", [128, NSC, 256], BF16)
                SG = 8 if NSC >= 8 else NSC
                cs = [k.sb(st, "fn_c", [128, SG, 512], BF16) for _ in range(2)]
                sn = [k.sb(st, "fn_s", [128, SG, 512], BF16) for _ in range(2)]
                cc = k.sb(st, "fn_cc", [128, 2, 256], BF16)
                sc = k.sb(st, "fn_sc", [128, 2, 256], BF16)
                ab = [k.sb(st, "fn_ab", [128, 4, 512], BF16) for _ in range(2)]
                yo = [k.sb(st, "fn_y", [128, 2, 512], BF16) for _ in range(2)]
                k.dma("sp", cc[:], cosC.rearrange("(c p) n -> p c n", p=128), cc, writes=[cc])
                k.dma("sp", sc[:], nsinC.rearrange("(c p) n -> p c n", p=128), sc, writes=[sc])
                cv = cosS.rearrange("(c p) n -> p c n", p=128)
                sv = sinS.rearrange("(c p) n -> p c n", p=128)
                uv = Ut.rearrange("(c p) n -> p c n", p=128)
                li = 0
                oi = 0
                for g in range(4):
                    for u0 in range(0, NSC, 8):
                        u1 = min(NSC, u0 + 8)
                        k.dma("sp", ug[:, u0:u1, :], uv[:, u0:u1, g * 256:(g + 1) * 256], ug, writes=[ug])
                    for kt in range(NT):
                        pacc = [PS[0], PS[1], PS[2], PS[3]]
                        for s0 in range(0, NSC, SG):
                            ct, stt = cs[li % 2], sn[li % 2]
                            li += 1
                            k.dma("sp", ct[:], cv[:, s0:s0 + SG, kt * 512:(kt + 1) * 512], ct, writes=[ct])
                            k.dma("sp", stt[:], sv[:, s0:s0 + SG, kt * 512:(kt + 1) * 512], stt, writes=[stt])
                            for sj in range(SG):
                                sci = s0 + sj
                                for cb in range(2):
                                    for (mi, mt) in ((0, ct), (1, stt)):
                                        p = pacc[mi * 2 + cb]
                                        k.op("pe", lambda e, p=p, mt=mt, sj=sj, sci=sci, cb=cb: e.matmul(
                                            p[:], lhsT=ug[:, sci, cb * 128:(cb + 1) * 128], rhs=mt[:, sj, :],
                                            start=(sci == 0), stop=(sci == NSC - 1)),
                                            reads=[ug, mt], writes=[p], inc=(sci == NSC - 1) or (sj == SG - 1 and cb == 1 and mi == 1))
                        a = ab[oi % 2]
                        y = yo[oi % 2]
                        oi += 1
                        for j in range(4):
                            if j % 2 == 0:
                                k.op("act", lambda e, j=j: e.copy(out=a[:, j, :], in_=pacc[j][:]), reads=[pacc[j]], writes=[a])
                            else:
                                k.op("dve", lambda e, j=j: e.tensor_copy(out=a[:, j, :], in_=pacc[j][:]), reads=[pacc[j]], writes=[a])
                        for cpb in range(2):
                            py = ps_pick(4, 8)
                            for j in range(4):
                                mat = cc if j < 2 else sc
                                k.op("pe", lambda e, py=py, mat=mat, j=j, cpb=cpb: e.matmul(
                                    py[:], lhsT=mat[:, j % 2, cpb * 128:(cpb + 1) * 128], rhs=a[:, j, :], start=(j == 0), stop=(j == 3)),
                                    reads=[mat, a], writes=[py], inc=(j == 3))
                            k.op("act", lambda e, py=py, cpb=cpb: e.copy(out=y[:, cpb, :], in_=py[:]), reads=[py], writes=[y])
                        k.dma("pool", CAT[g * 256:(g + 1) * 256, kt * 512:(kt + 1) * 512].rearrange("(c p) s -> p c s", p=128),
                              y[:], y, reads=[y])
            k.barrier()
            with contextlib.ExitStack() as st:
                kt_ = k.sb(st, "da_k", [128, S], BF16)
                qt_ = k.sb(st, "da_q", [128, S], BF16)
                vt_ = k.sb(st, "da_v", [128, NKB, 128], BF16)
                bt_ = k.sb(st, "da_bt", [128, 6, 512], F32)
                es = [k.sb(st, "da_e", [128, 2, 512], BF16) for _ in range(3)]
                tm = [k.sb(st, "da_tm", [128, 2, 512], F32) for _ in range(2)]
                rz = k.sb(st, "da_rz", [128, 2, 512], F32)
                o32 = k.sb(st, "da_o", [128, 512], F32)
                t2 = k.sb(st, "da_t2", [128, 512], F32)
                osq = k.sb(st, "da_sq", [128, 512], BF16)
                rs = k.sb(st, "da_rs", [128, 512], F32)
                tmp = k.sb(st, "da_tmp", [128, 512], F32)
                ob = [k.sb(st, "da_ob", [128, 512], BF16) for _ in range(2)]
                hg = k.sb(st, "da_hg", [128, 1], F32)
                zacc = [k.sb(st, "da_za", [128, 512], F32) for _ in range(2)]
                zb = [k.sb(st, "da_zb", [128, 512], BF16) for _ in range(2)]
                k.op("dve", lambda e: e.tensor_scalar(out=hg[:], in0=c_dng[:, i:i + 1], scalar1=(1.0 - lam_init), scalar2=None, op0=ALU.mult),
                     reads=[c_dng], writes=[hg])
                ei = 0
                oi = 0
                for hh in range(8):
                    k.dma("sp", kt_[:], KT[hh * 128:(hh + 1) * 128, :], kt_, writes=[kt_])
                    k.dma("sp", qt_[:], QT[hh * 128:(hh + 1) * 128, :], qt_, writes=[qt_])
                    vsrc = Vt[:, hh * 128:(hh + 1) * 128].rearrange("(kb p) e -> p kb e", p=128)
                    for v0 in range(0, NKB, 8):
                        v1 = min(NKB, v0 + 8)
                        k.dma("sp", vt_[:, v0:v1, :], vsrc[:, v0:v1, :], vt_, writes=[vt_])
                    k.dma("sp", bt_[:], BT[hh].rearrange("p (d q) -> p d q", q=512), bt_, writes=[bt_])
                    for qi in range(NT):
                        i0 = qi * 4
                        po = [PS[0], PS[1]]
                        pz = [PS[2], PS[3]]
                        for kb in range(NKB):
                            Dd = kb - i0
                            near = (-1 <= Dd <= 4)
                            side = 1 if kb > i0 else 0
                            ee = es[ei % 3]
                            ei += 1
                            pss = [ps_pick(4, 8), ps_pick(4, 8)]
                            for c in range(2):
                                k.op("pe", lambda e, c=c, kb=kb, qi=qi, pss=pss: e.matmul(
                                    pss[c][:], lhsT=kt_[c * 64:(c + 1) * 64, kb * 128:(kb + 1) * 128],
                                    rhs=qt_[c * 64:(c + 1) * 64, qi * 512:(qi + 1) * 512], start=True, stop=True),
                                    reads=[kt_, qt_], writes=[pss[c]], inc=True)
                            if near:
                                tt = tm[ei % 2]
                                for c in range(2):
                                    k.op("dve", lambda e, c=c, tt=tt, pss=pss, Dd=Dd: e.scalar_tensor_tensor(
                                        out=tt[:, c, :], in0=pss[c][:], scalar=0.125, in1=bt_[:, Dd + 1, :], op0=ALU.mult, op1=ALU.add),
                                        reads=[pss[c], bt_], writes=[tt])
                                k.op("act", lambda e, tt=tt, ee=ee, kb=kb: e.activation(
                                    out=ee[:], in_=tt[:], func=AF.Exp, bias=c_kmask[:, kb:kb + 1]), reads=[tt, c_kmask], writes=[ee])
                            else:
                                fo = (hh * 2 + side) * NKB + kb
                                for c in range(2):
                                    k.op("act", lambda e, c=c, ee=ee, pss=pss, fo=fo: e.activation(
                                        out=ee[:, c, :], in_=pss[c][:], func=AF.Exp, scale=0.125, bias=c_far[:, fo:fo + 1]),
                                        reads=[pss[c], c_far], writes=[ee])
                            for c in range(2):
                                k.op("pe", lambda e, c=c, ee=ee, kb=kb: e.matmul(
                                    po[c][:], lhsT=vt_[:, kb, :], rhs=ee[:, c, :], start=(kb == 0), stop=(kb == NKB - 1)),
                                    reads=[vt_, ee], writes=[po[c]], inc=(kb == NKB - 1))
                            for c, zeng in ((0, "dve"), (1, "pool")):
                                if kb == 0:
                                    k.op(zeng, lambda e, c=c, ee=ee: e.tensor_copy(out=zacc[c][:], in_=ee[:, c, :]),
                                         reads=[ee], writes=[zacc[c]])
                                else:
                                    k.op(zeng, lambda e, c=c, ee=ee: e.tensor_tensor(out=zacc[c][:], in0=zacc[c][:], in1=ee[:, c, :], op=ALU.add),
                                         reads=[ee, zacc[c]], writes=[zacc[c]])
                        for c, zeng in ((0, "dve"), (1, "pool")):
                            k.op(zeng, lambda e, c=c: e.tensor_copy(out=zb[c][:], in_=zacc[c][:]), reads=[zacc[c]], writes=[zb[c]])
                            k.op("pe", lambda e, c=c: e.matmul(pz[c][:], lhsT=c_ones[:], rhs=zb[c][:], start=True, stop=True),
                                 reads=[c_ones, zb[c]], writes=[pz[c]], inc=True)
                        for c in range(2):
                            k.op("dve", lambda e, c=c: e.reciprocal(out=rz[:, c, :], in_=pz[c][:]), reads=[pz[c]], writes=[rz])
                        k.op("dve", lambda e: e.tensor_tensor(out=o32[:], in0=po[0][:], in1=rz[:, 0, :], op=ALU.mult), reads=[po[0], rz], writes=[o32])
                        k.op("dve", lambda e: e.tensor_tensor(out=t2[:], in0=po[1][:], in1=rz[:, 1, :], op=ALU.mult), reads=[po[1], rz], writes=[t2])
                        k.op("dve", lambda e: e.scalar_tensor_tensor(out=o32[:], in0=t2[:], scalar=c_lam[:, i:i + 1], in1=o32[:],
                                                                     op0=ALU.mult, op1=ALU.add), reads=[t2, o32, c_lam], writes=[o32])
                        k.op("act", lambda e: e.activation(out=osq[:], in_=o32[:], func=AF.Square), reads=[o32], writes=[osq])
                        pn = ps_pick(4, 8)
                        k.op("pe", lambda e, pn=pn: e.matmul(pn[:], lhsT=c_ones[:], rhs=osq[:], start=True, stop=True),
                             reads=[c_ones, osq], writes=[pn], inc=True)
                        rstd_from(pn, 512, 128, rs, tmp)
                        obb = ob[oi % 2]
                        oi += 1
                        k.op("dve", lambda e, obb=obb: e.scalar_tensor_tensor(out=obb[:], in0=o32[:], scalar=hg[:, 0:1], in1=rs[:],
                                                                              op0=ALU.mult, op1=ALU.mult), reads=[o32, hg, rs], writes=[obb])
                        k.dma("pool", CAT[1024 + hh * 128:1024 + (hh + 1) * 128, qi * 512:(qi + 1) * 512], obb[:], obb, reads=[obb])
            k.barrier()
            with contextlib.ExitStack() as st:
                epi = make_store_epi(st, lambda tag, t0, n: Y[tag * 128:(tag + 1) * 128, t0:t0 + n], F32, TB)
                linear_fm(CAT, D, b_out_even[i], [(j * 128, j) for j in range(16)], epi)
            k.barrier()

        def odd_mixer(l):
            i = l // 2
            C = 64
            with contextlib.ExitStack() as st:
                def dstf(tag, t0, n):
                    return (QgT if tag < 8 else KgT)[(tag % 8) * 128:(tag % 8 + 1) * 128, t0:t0 + n]
                epi_q = make_store_epi(st, dstf, BF16, TB, scale=256 ** -0.5)
                linear_fm(H, D, b_in_odd[i], [(j * 128, j) for j in range(8)], epi_q)
            k.barrier()
            with contextlib.ExitStack() as st:
                def dstf2(tag, t0, n):
                    return KgT[tag * 128:(tag + 1) * 128, t0:t0 + n]
                epi_k = make_store_epi(st, dstf2, BF16, TB)
                linear_fm(H, D, b_in_odd[i], [(1024 + j * 128, j) for j in range(8)], epi_k)
            k.barrier()
            with contextlib.ExitStack() as st:
                def dstf3(tag, t0, n):
                    return RS[tag * 128:(tag + 1) * 128, t0:t0 + n]
                epi_r = make_store_epi(st, dstf3, BF16, TB, func=AF.Silu)
                linear_fm(H, D, b_in_odd[i], [(4096 + j * 128, j) for j in range(16)], epi_r)
            k.barrier()
            with contextlib.ExitStack() as st:
                stg = [k.sb(st, "om_s", [128, 512], BF16) for _ in range(3)]
                si = [0]

                def epi_tm(dst):
                    def epi(ps, g0, gwid, tok0):
                        sg = stg[si[0] % 3]
                        si[0] += 1
                        if si[0] % 2:
                            k.op("act", lambda e: e.copy(out=sg[:, 0:gwid], in_=ps[:, 0:gwid]), reads=[ps], writes=[sg])
                        else:
                            k.op("dve", lambda e: e.tensor_copy(out=sg[:, 0:gwid], in_=ps[:, 0:gwid]), reads=[ps], writes=[sg])
                        k.dma("pool", dst[tok0:tok0 + 128, g0:g0 + gwid], sg[:, 0:gwid], sg, reads=[sg])
                    return epi
                linear_tm(H, D, b_in_odd[i], 1024, 1024, epi_tm(Kg))
                linear_tm(H, D, b_in_odd[i], 2048, 2048, epi_tm(Vg))
            k.barrier()
            with contextlib.ExitStack() as st:
                act = k.sb(st, "gd_act", [128, 16, TB], BF16)
                wt = k.sb(st, "gd_w", [128, 16, 32], BF16)
                sg = [k.sb(st, "gd_s", [32, 512], BF16) for _ in range(2)]
                k.dma("sp", wt[:], b_gd[i].rearrange("(kc p) n -> p kc n", p=128), wt, writes=[wt])
                inv = H.rearrange("(kc p) s -> p kc s", p=128)
                n_ = 0
                for tb0 in range(0, S, TB):
                    k.dma("sp", act[:], inv[:, :, tb0:tb0 + TB], act, writes=[act])
                    for tt in range(TB // 512):
                        ps = ps_next()
                        for kc in range(16):
                            k.op("pe", lambda e, ps=ps, kc=kc, tt=tt: e.matmul(ps[0:32, :], lhsT=wt[:, kc, :], rhs=act[:, kc, tt * 512:(tt + 1) * 512],
                                                                               start=(kc == 0), stop=(kc == 15)),
                                 reads=[wt, act], writes=[ps], inc=(kc == 15))
                        s_ = sg[n_ % 2]
                        n_ += 1
                        k.op("act", lambda e, ps=ps, s_=s_: e.copy(out=s_[:], in_=ps[0:32, :]), reads=[ps], writes=[s_])
                        k.dma("pool", HDT[:, tb0 + tt * 512:tb0 + (tt + 1) * 512], s_[:], s_, reads=[s_])
            k.barrier()
            with contextlib.ExitStack() as st:
                hda = [k.sb(st, "gu_h", [32, S], BF16) for _ in range(2)]
                wu = [k.sb(st, "gu_w", [32, 1024], BF16) for _ in range(2)]
                gs = [k.sb(st, "gu_g", [128, 1024], F32) for _ in range(2)]
                ghs = [k.sb(st, "gu_gh", [128, 1024], BF16) for _ in range(2)]
                gls = [k.sb(st, "gu_gl", [128, 1024], BF16) for _ in range(2)]
                n_ = 0
                for d in range(2):
                    hd = hda[d]
                    k.op("pool", lambda e, hd=hd: e.memset(hd[:], 1.0), writes=[hd])
                    k.dma("sp", hd[0:16, :], HDT[d * 16:(d + 1) * 16, :], hd, writes=[hd])
                    k.dma("sp", wu[d][:], b_gu[(i * 2 + d) * 32:(i * 2 + d + 1) * 32, :], wu[d], writes=[wu[d]])
                    for tk in range(S // 128):
                        g_ = gs[n_ % 2]
                        n_ += 1
                        for half in range(2):
                            ps = ps_next()
                            k.op("pe", lambda e, ps=ps, hd=hd, d=d, tk=tk, half=half: e.matmul(
                                ps[:], lhsT=hd[:, tk * 128:(tk + 1) * 128], rhs=wu[d][:, half * 512:(half + 1) * 512], start=True, stop=True),
                                reads=[hd, wu[d]], writes=[ps], inc=True)
                            k.op("act", lambda e, ps=ps, g_=g_, half=half: e.activation(out=g_[:, half * 512:(half + 1) * 512], in_=ps[:], func=AF.Exp, scale=-1.0),
                                 reads=[ps], writes=[g_])
                        k.op("dve", lambda e, g_=g_: e.tensor_scalar(out=g_[:], in0=g_[:], scalar1=1.0, scalar2=None, op0=ALU.add), reads=[g_], writes=[g_])
                        k.op("act", lambda e, g_=g_: e.activation(out=g_[:], in_=g_[:], func=AF.Ln), reads=[g_], writes=[g_])
                        k.op("pool", lambda e, g_=g_: e.tensor_scalar(out=g_[:], in0=g_[:], scalar1=-1.0 / 16.0, scalar2=None, op0=ALU.mult), reads=[g_], writes=[g_])
                        gh_ = ghs[n_ % 2]
                        gl_ = gls[n_ % 2]
                        k.op("act", lambda e, g_=g_, gh_=gh_: e.copy(out=gh_[:], in_=g_[:]), reads=[g_], writes=[gh_])
                        k.op("dve", lambda e, g_=g_, gh_=gh_: e.tensor_tensor(out=g_[:], in0=g_[:], in1=gh_[:], op=ALU.subtract), reads=[g_, gh_], writes=[g_])
                        k.op("pool", lambda e, g_=g_, gl_=gl_: e.tensor_copy(out=gl_[:], in_=g_[:]), reads=[g_], writes=[gl_])
                        k.dma("pool", GH[d, tk * 128:(tk + 1) * 128, :], gh_[:], gh_, reads=[gh_])
                        k.dma("pool", GL[d, tk * 128:(tk + 1) * 128, :], gl_[:], gl_, reads=[gl_])
            k.barrier()
            with contextlib.ExitStack() as st:
                SC = 512
                NCH = SC // C
                qs = [k.sb(st, "gl_q", [128, 2, SC], BF16) for _ in range(2)]
                ks = [k.sb(st, "gl_k", [128, 2, SC], BF16) for _ in range(2)]
                kt = [k.sb(st, "gl_kt", [C, NCH, 256], BF16) for _ in range(2)]
                vt = [k.sb(st, "gl_vt", [C, NCH, 512], BF16) for _ in range(2)]
                gt = [k.sb(st, "gl_gt", [C, NCH, 256], BF16) for _ in range(2)]
                gtl = [k.sb(st, "gl_gtl", [C, NCH, 256], BF16) for _ in range(2)]
                ep = [k.sb(st, "gl_ep", [128, 2, C], F32) for _ in range(2)]
                en = [k.sb(st, "gl_en", [128, 2, C], F32) for _ in range(2)]
                ek = [k.sb(st, "gl_ek", [C, 256], F32) for _ in range(2)]
                ql = [k.sb(st, "gl_ql", [128, 2, C], BF16) for _ in range(2)]
                kl = [k.sb(st, "gl_kl", [128, 2, C], BF16) for _ in range(2)]
                kh = [k.sb(st, "gl_kh", [C, 256], BF16) for _ in range(2)]
                am = [k.sb(st, "gl_am", [C, C], BF16) for _ in range(2)]
                s32 = k.sb(st, "gl_s32", [128, 2, 512], F32)
                sbf = k.sb(st, "gl_sbf", [128, 2, 512], BF16)
                oo = [k.sb(st, "gl_o", [128, 4, SC], F32) for _ in range(2)]
                ci = 0
                sci = 0
                for d in range(2):
                    Ltri = c_trib[:, (0 if d == 0 else 64):(64 if d == 0 else 128)]
                    Mtri = c_trib[:, (128 if d == 0 else 192):(192 if d == 0 else 256)]
                    Lmask = c_trib[:, (0 if d == 0 else 64):(64 if d == 0 else 128)]
                    for hh in range(4):
                        k.op("pool", lambda e: e.memset(s32[:], 0.0), writes=[s32])
                        k.op("pool", lambda e: e.memset(sbf[:], 0.0), writes=[sbf])
                        sc_order = list(range(S // SC))
                        if d == 1:
                            sc_order.reverse()
                        for scx in sc_order:
                            t0 = scx * SC
                            b = sci % 2
                            sci += 1
                            q_, k_, kt_, vt_, gt_, gl2_, o_ = qs[b], ks[b], kt[b], vt[b], gt[b], gtl[b], oo[b]
                            k.dma("sp", q_[:], QgT[hh * 256:(hh + 1) * 256, t0:t0 + SC].rearrange("(c p) s -> p c s", p=128), q_, writes=[q_])
                            k.dma("sp", k_[:], KgT[hh * 256:(hh + 1) * 256, t0:t0 + SC].rearrange("(c p) s -> p c s", p=128), k_, writes=[k_])
                            k.dma("sp", kt_[:], Kg[t0:t0 + SC, hh * 256:(hh + 1) * 256].rearrange("(c p) n -> p c n", p=C), kt_, writes=[kt_])
                            k.dma("sp", vt_[:], Vg[t0:t0 + SC, hh * 512:(hh + 1) * 512].rearrange("(c p) n -> p c n", p=C), vt_, writes=[vt_])
                            k.dma("sp", gt_[:], GH[d, t0:t0 + SC, hh * 256:(hh + 1) * 256].rearrange("(c p) n -> p c n", p=C), gt_, writes=[gt_])
                            k.dma("sp", gl2_[:], GL[d, t0:t0 + SC, hh * 256:(hh + 1) * 256].rearrange("(c p) n -> p c n", p=C), gl2_, writes=[gl2_])
                            ch_order = list(range(NCH))
                            if d == 1:
                                ch_order.reverse()
                            for ch in ch_order:
                                x = ci % 2
                                ci += 1
                                c0 = ch * C
                                pb = ps_next()
                                for db in range(2):
                                    k.op("pe", lambda e, pb=pb, db=db, ch=ch, gt_=gt_: e.matmul(
                                        pb[:, db * C:(db + 1) * C], lhsT=gt_[:, ch, db * 128:(db + 1) * 128], rhs=Ltri, start=True, stop=False),
                                        reads=[gt_, c_trib], writes=[pb], inc=False)
                                    k.op("pe", lambda e, pb=pb, db=db, ch=ch, gl2_=gl2_: e.matmul(
                                        pb[:, db * C:(db + 1) * C], lhsT=gl2_[:, ch, db * 128:(db + 1) * 128], rhs=Ltri, start=False, stop=True),
                                        reads=[gl2_, c_trib], writes=[pb], inc=True)
                                pk = ps_next()
                                k.op("pe", lambda e, pk=pk, ch=ch, gt_=gt_: e.matmul(pk[0:C, 0:256], lhsT=Mtri, rhs=gt_[:, ch, :], start=True, stop=False),
                                     reads=[gt_, c_trib], writes=[pk], inc=False)
                                k.op("pe", lambda e, pk=pk, ch=ch, gl2_=gl2_: e.matmul(pk[0:C, 0:256], lhsT=Mtri, rhs=gl2_[:, ch, :], start=False, stop=True),
                                     reads=[gl2_, c_trib], writes=[pk], inc=True)
                                ep_, en_, ek_ = ep[x], en[x], ek[x]
                                k.op("act", lambda e, pb=pb, ep_=ep_: e.activation(out=ep_[:], in_=pb[:, 0:2 * C].rearrange("p (a c) -> p a c", c=C), func=AF.Exp), reads=[pb], writes=[ep_])
                                k.op("act", lambda e, pb=pb, en_=en_: e.activation(out=en_[:], in_=pb[:, 0:2 * C].rearrange("p (a c) -> p a c", c=C), func=AF.Exp, scale=-1.0), reads=[pb], writes=[en_])
                                k.op("act", lambda e, pk=pk, ek_=ek_: e.activation(out=ek_[:], in_=pk[0:C, 0:256], func=AF.Exp), reads=[pk], writes=[ek_])
                                ql_, kl_, kh_, am_ = ql[x], kl[x], kh[x], am[x]
                                k.op("dve", lambda e, q_=q_, ql_=ql_, ep_=ep_, c0=c0: e.tensor_tensor(out=ql_[:], in0=q_[:, :, c0:c0 + C], in1=ep_[:], op=ALU.mult),
                                     reads=[q_, ep_], writes=[ql_])
                                k.op("pool", lambda e, k_=k_, kl_=kl_, en_=en_, c0=c0: e.tensor_tensor(out=kl_[:], in0=k_[:, :, c0:c0 + C], in1=en_[:], op=ALU.mult),
                                     reads=[k_, en_], writes=[kl_])
                                k.op("pool", lambda e, kt_=kt_, kh_=kh_, ek_=ek_, ch=ch: e.tensor_tensor(out=kh_[:], in0=kt_[:, ch, :], in1=ek_[:], op=ALU.mult),
                                     reads=[kt_, ek_], writes=[kh_])
                                pa = ps_next()
                                for db in range(2):
                                    k.op("pe", lambda e, pa=pa, db=db, kl_=kl_, ql_=ql_: e.matmul(pa[0:C, 0:C], lhsT=kl_[:, db, :], rhs=ql_[:, db, :],
                                                                                               start=(db == 0), stop=(db == 1)),
                                         reads=[kl_, ql_], writes=[pa], inc=(db == 1))
                                k.op("dve", lambda e, pa=pa, am_=am_: e.tensor_tensor(out=am_[:], in0=pa[0:C, 0:C], in1=Lmask, op=ALU.mult),
                                     reads=[pa, c_trib], writes=[am_])
                                po = ps_next()
                                for eb in range(4):
                                    for db in range(2):
                                        k.op("pe", lambda e, po=po, eb=eb, db=db, ql_=ql_: e.matmul(
                                            po[:, eb * C:(eb + 1) * C], lhsT=sbf[:, db, eb * 128:(eb + 1) * 128], rhs=ql_[:, db, :],
                                            start=(db == 0), stop=False), reads=[sbf, ql_], writes=[po], inc=False)
                                    k.op("pe", lambda e, po=po, eb=eb, vt_=vt_, am_=am_, ch=ch: e.matmul(
                                        po[:, eb * C:(eb + 1) * C], lhsT=vt_[:, ch, eb * 128:(eb + 1) * 128], rhs=am_[:],
                                        start=False, stop=True), reads=[vt_, am_], writes=[po], inc=True)
                                k.op("act", lambda e, po=po, o_=o_, c0=c0: e.copy(out=o_[:, :, c0:c0 + C], in_=po[:, 0:4 * C].rearrange("p (a c) -> p a c", c=C)), reads=[po], writes=[o_])
                                bl = (C - 1) if d == 0 else 0
                                for db in range(2):
                                    pd = ps_next()
                                    k.op("pe", lambda e, pd=pd, db=db, kh_=kh_, vt_=vt_, ch=ch: e.matmul(
                                        pd[:], lhsT=kh_[:, db * 128:(db + 1) * 128], rhs=vt_[:, ch, :], start=True, stop=True),
                                        reads=[kh_, vt_], writes=[pd], inc=True)
                                    k.op("dve", lambda e, pd=pd, db=db, ep_=ep_, bl=bl: e.scalar_tensor_tensor(
                                        out=s32[:, db, :], in0=s32[:, db, :], scalar=ep_[:, db, bl:bl + 1], in1=pd[:], op0=ALU.mult, op1=ALU.add),
                                        reads=[s32, ep_, pd], writes=[s32])
                                k.op("act", lambda e: e.copy(out=sbf[:], in_=s32[:]), reads=[s32], writes=[sbf])
                            k.dma("pool", OG[d, hh * 512:(hh + 1) * 512, t0:t0 + SC].rearrange("(c p) s -> p c s", p=128), o_[:], o_, reads=[o_])
            k.barrier()
            with contextlib.ExitStack() as st:
                a = [k.sb(st, "op_a", [128, 16, 512], F32) for _ in range(1)]
                b = [k.sb(st, "op_b", [128, 16, 512], F32) for _ in range(1)]
                r = [k.sb(st, "op_r", [128, 16, 512], BF16) for _ in range(1)]
                sq = k.sb(st, "op_sq", [128, 16, 512], BF16)
                rs = k.sb(st, "op_rs", [128, 512], F32)
                tmp = k.sb(st, "op_t", [128, 512], F32)
                o = [k.sb(st, "op_o", [128, 16, 512], BF16) for _ in range(2)]
                av = OG[0].rearrange("(c p) s -> p c s", p=128)
                bv = OG[1].rearrange("(c p) s -> p c s", p=128)
                rv = RS.rearrange("(c p) s -> p c s", p=128)
                ov = CAT.rearrange("(c p) s -> p c s", p=128)
                for ti in range(NT):
                    t0 = ti * 512
                    a_, b_, r_, o_ = a[0], b[0], r[0], o[ti % 2]
                    k.dma("sp", a_[:], av[:, :, t0:t0 + 512], a_, writes=[a_])
                    k.dma("sp", b_[:], bv[:, :, t0:t0 + 512], b_, writes=[b_])
                    k.dma("sp", r_[:], rv[:, :, t0:t0 + 512], r_, writes=[r_])
                    k.op("pool", lambda e, a_=a_, b_=b_: e.tensor_tensor(out=a_[:], in0=a_[:], in1=b_[:], op=ALU.add), reads=[a_, b_], writes=[a_])
                    k.op("act", lambda e, a_=a_: e.activation(out=sq[:], in_=a_[:], func=AF.Square), reads=[a_], writes=[sq])
                    for hh in range(4):
                        ps = ps_next()
                        for c in range(4):
                            k.op("pe", lambda e, ps=ps, c=c, hh=hh: e.matmul(ps[:], lhsT=c_ones[:], rhs=sq[:, hh * 4 + c, :], start=(c == 0), stop=(c == 3)),
                                 reads=[c_ones, sq], writes=[ps], inc=(c == 3))
                        rstd_from(ps, 512, 512, rs, tmp)
                        for c in range(4):
                            cc_ = hh * 4 + c
                            k.op("dve", lambda e, a_=a_, cc_=cc_, c=c: e.scalar_tensor_tensor(
                                out=a_[:, cc_, :], in0=a_[:, cc_, :], scalar=c_gng[:, i * 4 + c:i * 4 + c + 1], in1=rs[:], op0=ALU.mult, op1=ALU.mult),
                                reads=[a_, rs, c_gng], writes=[a_])
                    k.op("pool", lambda e, a_=a_, r_=r_, o_=o_: e.tensor_tensor(out=o_[:], in0=a_[:], in1=r_[:], op=ALU.mult), reads=[a_, r_], writes=[o_])
                    k.dma("pool", ov[:, :, t0:t0 + 512], o_[:], o_, reads=[o_])
            k.barrier()
            with contextlib.ExitStack() as st:
                epi = make_store_epi(st, lambda tag, t0, n: Y[tag * 128:(tag + 1) * 128, t0:t0 + n], F32, TB)
                linear_fm(CAT, D, b_out_odd[i], [(j * 128, j) for j in range(16)], epi)
            k.barrier()

        norm_pass(None, None, None, 0, 0, Xsrc=xT)
        for l in range(DEPTH):
            if l % 2 == 0:
                even_mixer(l)
            else:
                odd_mixer(l)
            norm_pass(l, 1, Y, l, 2)
            cross_attn(l)
            norm_pass(l, 3, Y, l, 5)
            ffn(l)
            if l + 1 < DEPTH:
                norm_pass(l, 6, Y, l + 1, 0)
            else:
                norm_pass(l, 6, Y, None, None)
        k.stopped = False
        k.barrier(final=True)
    return nc


def _t5_bucket(rel):
    try:
        import jax
        import jax.numpy as jnp
        with jax.default_device(jax.devices("cpu")[0]):
            r = jnp.asarray(np.asarray(rel, np.int32))
            n, max_exact = 16, 8
            base = jnp.where(r > 0, n, 0)
            a = jnp.abs(r)
            af = jnp.maximum(a, 1).astype(jnp.float32)
            large = max_exact + (jnp.log(af / max_exact) / math.log(128 / max_exact) * (n - max_exact)).astype(jnp.int32)
            large = jnp.minimum(large, n - 1)
            return np.asarray(base + jnp.where(a < max_exact, a, large))
    except Exception:
        pass
    n = 16
    max_exact = 8
    base = np.where(rel > 0, n, 0)
    a = np.abs(rel)
    af = np.maximum(a, 1).astype(np.float32)
    large = max_exact + (np.log(af / np.float32(max_exact)) / np.float32(math.log(128 / max_exact))
                         * np.float32(n - max_exact)).astype(np.int32)
    large = np.minimum(large, n - 1)
    return base + np.where(a < max_exact, a, large)


def host_constants(S, S_real):
    NKB = S // 128
    maskb = np.zeros((128, S), np.float32)
    maskb[:, :S_real] = 1.0
    kmask = np.zeros((128, NKB), np.float32)
    pos = np.arange(S).reshape(NKB, 128).T
    kmask[pos >= S_real] = -30000.0
    idx = np.arange(S_real, dtype=np.int64)
    ang = 2.0 * np.pi * ((idx[:, None] * idx[None, :]) % S_real).astype(np.float64) / S_real
    cs = np.zeros((S, S), np.float32)
    sn = np.zeros((S, S), np.float32)
    cs[:S_real, :S_real] = np.cos(ang) / math.sqrt(S_real)
    sn[:S_real, :S_real] = np.sin(ang) / math.sqrt(S_real)
    ic = np.arange(256)
    angc = 2.0 * np.pi * ((ic[:, None] * ic[None, :]) % 256) / 256.0
    cc = (np.cos(angc) / 16.0).astype(np.float32)
    nsc = (-np.sin(angc) / 16.0).astype(np.float32)
    kk = np.arange(128)[:, None]
    qq = np.arange(512)[None, :]
    bk = np.concatenate([_t5_bucket((128 * Dd + kk - qq).astype(np.int32)) for Dd in range(-1, 5)], axis=1).astype(np.float32)
    s_ = np.arange(64)[:, None]
    t_ = np.arange(64)[None, :]
    tri = np.concatenate([(s_ <= t_), (s_ >= t_), (s_ > t_), (s_ < t_)], axis=1).astype(np.float32)
    return dict(maskb=maskb, kmask=kmask, cosS=cs.astype(NPBF), sinS=sn.astype(NPBF), cosC=cc.astype(NPBF),
                nsinC=nsc.astype(NPBF), bkt=bk, trimats=tri)


def host_weights(inp, DEPTH):
    NEVEN = (DEPTH + 1) // 2
    NODD = DEPTH // 2
    f = lambda a: np.ascontiguousarray(np.asarray(a, dtype=np.float32))
    out = {}
    out["ng"] = f(np.asarray(inp["norm_g"])[:DEPTH].reshape(DEPTH, 7, 16, 128).transpose(3, 0, 1, 2).reshape(128, DEPTH * 7 * 16))
    out["relb"] = f(np.asarray(inp["rel_bias"]).reshape(1, 256))
    out["lamp"] = f(np.asarray(inp["diff_lambda"])[:max(NEVEN, 1)].reshape(1, -1))
    out["dng"] = f(np.asarray(inp["diff_norm_g"])[:max(NEVEN, 1)].T)
    out["gng"] = f(np.asarray(inp["gla_norm_g"])[:max(NODD, 1)].reshape(max(NODD, 1), 4, 128).transpose(2, 0, 1).reshape(128, -1))
    out["convw"] = f(np.asarray(inp["conv_w"])[:DEPTH].reshape(DEPTH, 3, 88, 128).transpose(3, 0, 1, 2).reshape(128, -1))
    out["convb"] = f(np.asarray(inp["conv_b"])[:DEPTH].reshape(DEPTH, 88, 128).transpose(2, 0, 1).reshape(128, -1))
    out["w_in_even"] = f(np.asarray(inp["w_in_even"])[:max(NEVEN, 1)])
    out["w_out_even"] = f(np.asarray(inp["w_out_even"])[:max(NEVEN, 1)])
    out["w_in_odd"] = f(np.asarray(inp["w_in_odd"])[:max(NODD, 1)])
    gd = np.asarray(inp["gla_gate_down"])[:max(NODD, 1)]
    out["w_gd"] = f(gd.transpose(0, 2, 1, 3).reshape(gd.shape[0], D, 32))
    gu = np.asarray(inp["gla_gate_up"])[:max(NODD, 1)]
    gb = np.asarray(inp["gla_gate_bias"])[:max(NODD, 1)]
    aug = np.zeros((gu.shape[0], 2, 32, 1024), np.float32)
    aug[:, :, 0:16, :] = gu
    aug[:, :, 16, :] = gb
    out["w_gu"] = f(aug.reshape(-1, 1024))
    out["w_out_odd"] = f(np.asarray(inp["w_out_odd"])[:max(NODD, 1)])
    for nm in ("w_xq", "w_xkv", "w_xo", "w_up", "w_down"):
        out[nm] = f(np.asarray(inp[nm])[:DEPTH])
    return out


_CACHE = {}


def run_trunk(seqs, mems, inp, S, DEPTH, taps=()):
    key = (S, DEPTH, tuple(taps))
    if key not in _CACHE:
        _CACHE[key] = build(S, DEPTH, taps)
    nc = _CACHE[key]
    wts = host_weights(inp, DEPTH)
    consts = {}
    in_maps = []
    for x, m in zip(seqs, mems):
        sr = x.shape[0]
        if sr not in consts:
            consts[sr] = host_constants(S, sr)
        xT = np.zeros((D, S), np.float32)
        xT[:, :sr] = np.asarray(x, np.float32).T
        d = dict(wts)
        d.update(consts[sr])
        d["xT"] = xT
        d["memT"] = np.ascontiguousarray(np.asarray(m, np.float32).T)
        in_maps.append(d)
    res = run_bass_kernel_spmd(nc, in_maps, core_ids=list(range(len(in_maps))))
    return res


def kernel(**inp):
    xp = np.asarray(inp["x_prompt"])
    xs = np.asarray(inp["x_sample"])
    mp = np.asarray(inp["mem_prompt"])
    ms = np.asarray(inp["mem_sample"])
    S = 8192
    seqs = [xp[0], xp[1], xp[2], xp[3], xs[0], xs[0], xs[0], xs[0]]
    mems = [mp[0], mp[1], mp[2], mp[3], ms[0], ms[0], ms[0], ms[0]]
    res = run_trunk(seqs, mems, inp, S, 4)
    yp = np.stack([np.ascontiguousarray(res.results[c]["yT"][:, :2048].T) for c in range(4)], axis=0).astype(np.float32)
    ys = np.ascontiguousarray(res.results[4]["yT"].T)[None].astype(np.float32)
    return (yp, ys)
```
